# Optimizing a Trainium2 kernel written in Bass

```python
import jax, jax.numpy as jnp
from jax import lax
import numpy as np

D_MODEL = 1024
BATCH = 2
SEQ = 16384
DEPTH = 1

CHUNK = 64
N_META = 16
META_PAD = CHUNK - N_META
Q_BLOCK = 128
FOX_HEADS = 8
FOX_HEAD_DIM = 64
FOX_WIDTH = FOX_HEADS * FOX_HEAD_DIM
GDN_HEADS = 8
GDN_HEAD_DIM = 64
GDN_WIDTH = GDN_HEADS * GDN_HEAD_DIM
GDN_CONV = 4
D_FF = 2816
FFN_CONV = 3
IN_SPLITS = (3 * FOX_WIDTH, FOX_HEADS, 3 * GDN_WIDTH, GDN_WIDTH, GDN_HEADS, GDN_HEADS, 2 * D_MODEL)
IN_WIDTH = 3 * FOX_WIDTH + FOX_HEADS + 4 * GDN_WIDTH + 2 * GDN_HEADS + 2 * D_MODEL
RMS_EPS = 1e-6
NEG_INF = -1e30

kernel_name = "hybrid_fox_gdn_convffn_meta"


def rms_norm(x, gain):
    xf = x.astype(jnp.float32)
    y = xf * lax.rsqrt(jnp.mean(xf * xf, axis=-1, keepdims=True) + RMS_EPS)
    return (y * gain.astype(jnp.float32)).astype(x.dtype)


def l2_norm(x):
    return x * lax.rsqrt(jnp.sum(x * x, axis=-1, keepdims=True) + RMS_EPS)


def causal_dwconv(x, w, b=None):
    k_width, ch = w.shape
    y = lax.conv_general_dilated(
        x, w[:, None, :].astype(x.dtype), window_strides=(1,), padding=[(k_width - 1, 0)],
        dimension_numbers=('NWC', 'WIO', 'NWC'), feature_group_count=ch)
    if b is not None:
        y = y + b.astype(x.dtype)
    return y


def split_cols(a, sizes):
    idx = []
    acc = 0
    for s in sizes[:-1]:
        acc += s
        idx.append(acc)
    return jnp.split(a, idx, axis=-1)


def forgetting_attention(q, k, v, log_f):
    bn, seq_len, heads, dh = q.shape
    lp = -(-seq_len // Q_BLOCK) * Q_BLOCK
    pad = lp - seq_len
    q, k, v = [jnp.pad(a, ((0, 0), (0, pad), (0, 0), (0, 0))) for a in (q, k, v)]
    c = jnp.cumsum(jnp.pad(log_f, ((0, 0), (0, pad), (0, 0))), axis=1).transpose(0, 2, 1)
    nblk = lp // Q_BLOCK
    qb = q.reshape(bn, nblk, Q_BLOCK, heads, dh).transpose(1, 0, 3, 2, 4)
    cb = c.reshape(bn, heads, nblk, Q_BLOCK).transpose(2, 0, 1, 3)
    kpos = jnp.arange(lp)
    scale = dh ** -0.5

    def block(args):
        q_i, c_i, i = args
        s = jnp.einsum('bhqd,bkhd->bhqk', q_i, k, preferred_element_type=jnp.float32) * scale
        s = s + c_i[..., None] - c[:, :, None, :]
        qpos = i * Q_BLOCK + jnp.arange(Q_BLOCK)
        s = jnp.where(kpos[None, :] <= qpos[:, None], s, NEG_INF)
        p = jax.nn.softmax(s, axis=-1).astype(v.dtype)
        return jnp.einsum('bhqk,bkhd->bqhd', p, v)

    o = lax.map(block, (qb, cb, jnp.arange(nblk)))
    return o.transpose(1, 0, 2, 3, 4).reshape(bn, lp, heads * dh)[:, :seq_len]


def gated_delta_rule(q, k, v, beta, g):
    bn, seq_len, heads, dk = q.shape
    dv = v.shape[-1]
    q = l2_norm(q.astype(jnp.float32)) * (dk ** -0.5)
    k = l2_norm(k.astype(jnp.float32))
    v = v.astype(jnp.float32)
    total = META_PAD + seq_len
    back = (-total) % CHUNK
    total = total + back
    n_chunks = total // CHUNK
    pad4 = ((0, 0), (META_PAD, back), (0, 0), (0, 0))
    pad3 = ((0, 0), (META_PAD, back), (0, 0))
    q, k, v = [jnp.pad(a, pad4).reshape(bn, n_chunks, CHUNK, heads, -1).transpose(0, 3, 1, 2, 4)
               for a in (q, k, v)]
    beta, g = [jnp.pad(a, pad3).reshape(bn, n_chunks, CHUNK, heads).transpose(0, 3, 1, 2)
               for a in (beta, g)]
    gc = jnp.cumsum(g, axis=-1)
    tril = jnp.tril(jnp.ones((CHUNK, CHUNK), dtype=bool))
    strict = jnp.tril(jnp.ones((CHUNK, CHUNK), dtype=bool), -1)
    decay = jnp.exp(jnp.where(tril, gc[..., :, None] - gc[..., None, :], NEG_INF))
    kb = k * beta[..., None]
    vb = v * beta[..., None]
    lmat = jnp.where(strict, jnp.einsum('bhncd,bhnsd->bhncs', kb, k) * decay, 0.0)
    eye = jnp.eye(CHUNK, dtype=jnp.float32)
    rhs = jnp.concatenate([vb, kb * jnp.exp(gc)[..., None]], axis=-1)
    sol = lax.linalg.triangular_solve(lmat + eye, rhs, left_side=True, lower=True, unit_diagonal=True)
    value, k_cumdecay = sol[..., :dv], sol[..., dv:]
    attn_intra = jnp.einsum('bhncd,bhnsd->bhncs', q, k) * decay
    q_dec = q * jnp.exp(gc)[..., None]
    k_tail = k * jnp.exp(gc[..., -1:] - gc)[..., None]
    chunk_decay = jnp.exp(gc[..., -1])

    def step(state, xs):
        value_n, kcd_n, attn_n, qd_n, kt_n, cd_n = xs
        v_new = value_n - jnp.einsum('bhcd,bhdv->bhcv', kcd_n, state)
        o_n = jnp.einsum('bhcd,bhdv->bhcv', qd_n, state) + jnp.einsum('bhcs,bhsv->bhcv', attn_n, v_new)
        state = state * cd_n[..., None, None] + jnp.einsum('bhcd,bhcv->bhdv', kt_n, v_new)
        return state, o_n

    xs = tuple(jnp.moveaxis(a, 2, 0) for a in (value, k_cumdecay, attn_intra, q_dec, k_tail, chunk_decay))
    state0 = jnp.zeros((bn, heads, dk, dv), jnp.float32)
    _, o = lax.scan(step, state0, xs)
    o = o.transpose(1, 0, 3, 2, 4).reshape(bn, total, heads, dv)
    return o[:, META_PAD:META_PAD + seq_len]


def hybrid_mixer(h, w_in, fgt_bias, gdn_conv_w, gdn_a_log, gdn_dt_bias, gdn_norm_w, gate_bias,
                 w_branch_fox, w_branch_gdn, w_out):
    bn, seq_len, _ = h.shape
    proj = h @ w_in.astype(h.dtype)
    fox_qkv, fox_f, gdn_qkv, gdn_z, gdn_b, gdn_a, gates = split_cols(proj, IN_SPLITS)
    fox_qkv = fox_qkv.reshape(bn, seq_len, 3, FOX_HEADS, FOX_HEAD_DIM)
    log_f = jax.nn.log_sigmoid(fox_f.astype(jnp.float32) + fgt_bias.astype(jnp.float32))
    o_fox = forgetting_attention(fox_qkv[:, :, 0], fox_qkv[:, :, 1], fox_qkv[:, :, 2], log_f)
    gdn_qkv = jax.nn.silu(causal_dwconv(gdn_qkv, gdn_conv_w))
    gdn_qkv = gdn_qkv.reshape(bn, seq_len, 3, GDN_HEADS, GDN_HEAD_DIM)
    beta = jax.nn.sigmoid(gdn_b.astype(jnp.float32))
    g = -jnp.exp(gdn_a_log.astype(jnp.float32)) * jax.nn.softplus(
        gdn_a.astype(jnp.float32) + gdn_dt_bias.astype(jnp.float32))
    o_gdn = gated_delta_rule(gdn_qkv[:, :, 0], gdn_qkv[:, :, 1], gdn_qkv[:, :, 2], beta, g)
    z = gdn_z.reshape(bn, seq_len, GDN_HEADS, GDN_HEAD_DIM).astype(jnp.float32)
    o_gdn = (rms_norm(o_gdn, gdn_norm_w) * jax.nn.silu(z)).astype(h.dtype).reshape(bn, seq_len, GDN_WIDTH)
    y_fox = o_fox @ w_branch_fox.astype(h.dtype)
    y_gdn = o_gdn @ w_branch_gdn.astype(h.dtype)
    gate = jax.nn.sigmoid(gates.astype(jnp.float32) + gate_bias.astype(jnp.float32)).astype(h.dtype)
    g_fox, g_gdn = gate[..., :D_MODEL], gate[..., D_MODEL:]
    return (g_fox * y_fox + g_gdn * y_gdn) @ w_out.astype(h.dtype)


def conv_ffn(h, w_up, conv_w, conv_b, w_down):
    u = causal_dwconv(h @ w_up.astype(h.dtype), conv_w, conv_b)
    gate, up = u[..., :D_FF], u[..., D_FF:]
    return (jax.nn.silu(gate) * up) @ w_down.astype(h.dtype)


def setup_inputs(seed: int = 0) -> dict:
    key = jax.random.key(seed)
    ks = jax.random.split(key, 20)
    f32 = jnp.float32

    def nrm(k, shape, scale):
        return jax.random.normal(k, shape, f32) * scale

    dt = jnp.exp(jax.random.uniform(ks[5], (DEPTH, GDN_HEADS), f32, np.log(1e-3), np.log(1e-1)))
    return {
        "x": nrm(ks[0], (BATCH, SEQ, D_MODEL), 1.0),
        "meta_tokens": nrm(ks[1], (N_META, D_MODEL), 1.0),
        "w_in": nrm(ks[2], (DEPTH, D_MODEL, IN_WIDTH), D_MODEL ** -0.5),
        "fgt_bias": 2.0 + nrm(ks[3], (DEPTH, FOX_HEADS), 0.5),
        "gdn_conv_w": nrm(ks[4], (DEPTH, GDN_CONV, 3 * GDN_WIDTH), GDN_CONV ** -0.5),
        "gdn_a_log": jnp.log(jax.random.uniform(ks[6], (DEPTH, GDN_HEADS), f32, 1.0, 16.0)),
        "gdn_dt_bias": dt + jnp.log(-jnp.expm1(-dt)),
        "gdn_norm_w": 1.0 + nrm(ks[7], (DEPTH, GDN_HEAD_DIM), 0.01),
        "gate_bias": nrm(ks[8], (DEPTH, 2 * D_MODEL), 0.01),
        "w_branch_fox": nrm(ks[9], (DEPTH, FOX_WIDTH, D_MODEL), FOX_WIDTH ** -0.5),
        "w_branch_gdn": nrm(ks[10], (DEPTH, GDN_WIDTH, D_MODEL), GDN_WIDTH ** -0.5),
        "w_out": nrm(ks[11], (DEPTH, D_MODEL, D_MODEL), D_MODEL ** -0.5),
        "norm_mix_w": 1.0 + nrm(ks[12], (DEPTH, D_MODEL), 0.01),
        "norm_ffn_w": 1.0 + nrm(ks[13], (DEPTH, D_MODEL), 0.01),
        "ffn_w_up": nrm(ks[14], (DEPTH, D_MODEL, 2 * D_FF), D_MODEL ** -0.5),
        "ffn_conv_w": nrm(ks[15], (DEPTH, FFN_CONV, 2 * D_FF), FFN_CONV ** -0.5),
        "ffn_conv_b": nrm(ks[16], (DEPTH, 2 * D_FF), 0.01),
        "ffn_w_down": nrm(ks[17], (DEPTH, D_FF, D_MODEL), D_FF ** -0.5),
        "norm_final_w": 1.0 + nrm(ks[18], (D_MODEL,), 0.01),
    }


def reference(x, meta_tokens, w_in, fgt_bias, gdn_conv_w, gdn_a_log, gdn_dt_bias, gdn_norm_w,
              gate_bias, w_branch_fox, w_branch_gdn, w_out, norm_mix_w, norm_ffn_w, ffn_w_up,
              ffn_conv_w, ffn_conv_b, ffn_w_down, norm_final_w):
    bn = x.shape[0]
    meta = jnp.broadcast_to(meta_tokens[None].astype(x.dtype), (bn, N_META, D_MODEL))
    h = jnp.concatenate([meta, x], axis=1)
    for layer in range(DEPTH):
        h = h + hybrid_mixer(rms_norm(h, norm_mix_w[layer]), w_in[layer], fgt_bias[layer],
                             gdn_conv_w[layer], gdn_a_log[layer], gdn_dt_bias[layer],
                             gdn_norm_w[layer], gate_bias[layer], w_branch_fox[layer],
                             w_branch_gdn[layer], w_out[layer])
        h = h + conv_ffn(rms_norm(h, norm_ffn_w[layer]), ffn_w_up[layer], ffn_conv_w[layer],
                         ffn_conv_b[layer], ffn_w_down[layer])
    return rms_norm(h, norm_final_w)[:, N_META:]
```

```python
import contextlib
import numpy as np
import concourse.bass as bass
import concourse.mybir as mybir
from concourse.bass_utils import run_bass_kernel_spmd

F32 = mybir.dt.float32
BF16 = mybir.dt.bfloat16
AF = mybir.ActivationFunctionType
ALU = mybir.AluOpType
AX = mybir.AxisListType

D = 1024
NMETA = 16
DFF = 2816
EPS = 1e-6
SELF_SYNC = True
NDMA_SEMS = 24
HOP_NS = 380.0
ENGS = ['pe', 'act', 'dve', 'pool', 'sp']


class Tracker:
    def __init__(self):
        self.ops = []
        self.lastw = {}
        self.readers = {}
        self.ndma = {'s': 0, 'p': 0}
        self.bar = set()
        self.ncc = 0
        self.tag = ''

    def op(self, eng, fn, r=(), w=(), dma=False, cc=False, cost=100.0, lat=0.0):
        i = len(self.ops)
        w = list(w) + [k for k in r if isinstance(k, str) and k.startswith('ps') and k[2:].isdigit()]
        deps = set(self.bar)
        for k in r:
            j = self.lastw.get(k)
            if j is not None:
                deps.add(j)
        for k in w:
            j = self.lastw.get(k)
            if j is not None:
                deps.add(j)
            deps.update(self.readers.get(k, ()))
        d = None
        if dma:
            ring = 'p' if eng == 'pool' else 's'
            d = (ring, -1)
        ccid = None
        if cc:
            ccid = self.ncc
            self.ncc += 1
        self.ops.append(dict(eng=eng, fn=fn, deps=deps, dma=d, users=False, cc=cc, ccid=ccid, cost=cost, lat=lat, bar=False, tag=self.tag))
        for k in r:
            self.readers.setdefault(k, []).append(i)
        for k in w:
            self.lastw[k] = i
            self.readers[k] = []
        return i

    def barrier(self, fn):
        self.bar = set()
        i = self.op('pool', fn, (), ())
        self.ops[i]['bar'] = True
        self.bar = {i}
        return i

    def finalize(self, final_wait_keys=()):
        import heapq
        ops = self.ops
        n = len(ops)
        fin = set()
        for k in final_wait_keys:
            j = self.lastw.get(k)
            if j is not None:
                fin.add(j)
        succ = [[] for _ in range(n)]
        npend = [0] * n
        for i, o in enumerate(ops):
            dd = range(i) if o['bar'] else o['deps']
            npend[i] = len(dd)
            for j in dd:
                succ[j].append(i)
        ready = [0.0] * n
        start = [0.0] * n
        t_e = {e: 0.0 for e in ENGS}
        fut = {e: [] for e in ENGS}
        avail = {e: [] for e in ENGS}
        for i, o in enumerate(ops):
            if npend[i] == 0:
                heapq.heappush(fut[o['eng']], (0.0, i))
        done = 0
        last_on = {}
        t_prev_end = {}
        rby = [None] * n
        while done < n:
            best = None
            for e in ENGS:
                f, a_ = fut[e], avail[e]
                while f and f[0][0] <= t_e[e]:
                    heapq.heappush(a_, heapq.heappop(f)[1])
                if a_:
                    cand = (t_e[e], a_[0], e, True)
                elif f:
                    cand = (f[0][0], f[0][1], e, False)
                else:
                    continue
                if best is None or cand[:2] < best[:2]:
                    best = cand
            st, i, e, from_avail = best
            if from_avail:
                heapq.heappop(avail[e])
            else:
                heapq.heappop(fut[e])
            o = ops[i]
            start[i] = st
            o['by_eng'] = (st <= t_prev_end.get(e, 0.0) + 1e-9 and e in last_on) and last_on.get(e)
            last_on[e] = i
            t_prev_end[e] = st + o['cost']
            t_e[e] = st + o['cost']
            fi = st + o['cost'] + o['lat']
            for s_ in succ[i]:
                os_ = ops[s_]
                rt = fi + (60.0 if (os_['eng'] == e and o['dma'] is None and not o['cc']) else HOP_NS)
                if rt > ready[s_]:
                    ready[s_] = rt
                    rby[s_] = i
                npend[s_] -= 1
                if npend[s_] == 0:
                    heapq.heappush(fut[os_['eng']], (ready[s_], s_))
            done += 1
        self.makespan = max(t_e.values())
        self.start = start
        self.rby = rby
        self.busy = {e: sum(o['cost'] for o in ops if o['eng'] == e) for e in ENGS}
        self.nops = {e: sum(1 for o in ops if o['eng'] == e) for e in ENGS}
        self.per_eng = {e: [] for e in ENGS}
        for i in sorted(range(n), key=lambda i: (start[i], i)):
            self.per_eng[ops[i]['eng']].append(i)
        pos = {}
        for e in ENGS:
            for p_, i in enumerate(self.per_eng[e]):
                pos[i] = p_
        for i, o in enumerate(ops):
            if not o['bar']:
                continue
            red = set()
            for e in ENGS:
                pre = [j for j in self.per_eng[e] if j < i and ops[j]['dma'] is None and not ops[j]['cc']]
                if pre:
                    red.add(pre[-1])
                for ring in ('s', 'p'):
                    pd = [j for j in self.per_eng[e] if j < i and ops[j]['dma'] is not None and ops[j]['dma'][0] == ring]
                    red.update(pd[-NDMA_SEMS:])
            red.update(j for j in range(i) if ops[j]['cc'])
            o['deps'] = red
        for ring, e in (('s', 'sp'), ('p', 'pool')):
            dma_ops = [i for i in self.per_eng[e] if ops[i]['dma'] is not None]
            assert all(ops[i]['dma'][0] == ring for i in dma_ops)
            for n_, i in enumerate(dma_ops):
                ops[i]['dma'] = (ring, n_)
                if n_ >= NDMA_SEMS:
                    ops[i]['deps'].add(dma_ops[n_ - NDMA_SEMS])
        for e in ENGS:
            assert e in ('sp', 'pool') or all(ops[i]['dma'] is None for i in self.per_eng[e])
        for i, o in enumerate(ops):
            for j in o['deps']:
                dj = ops[j]
                if dj['dma'] is not None or dj['cc']:
                    continue
                if dj['eng'] == o['eng'] and o['dma'] is None and not o['cc']:
                    if o['eng'] == 'pe' or not SELF_SYNC:
                        continue
                dj['users'] = True
        for j in fin:
            if ops[j]['dma'] is None and not ops[j]['cc']:
                ops[j]['users'] = True
        for e in ENGS:
            c_ = 0
            for i in self.per_eng[e]:
                o = ops[i]
                if o['dma'] is None and not o['cc'] and o['users']:
                    c_ += 1
                    o['val'] = c_
        self.fin = fin

    def emit(self, esem, dsems, ccsem, block):
        ops = self.ops

        def target(j):
            dj = ops[j]
            if dj['cc']:
                return ('c', dj['ccid']), ccsem[dj['ccid']], 1
            if dj['dma'] is not None:
                ring, d = dj['dma']
                return ('d', ring, d % NDMA_SEMS), dsems[ring][d % NDMA_SEMS], 16 * (d // NDMA_SEMS + 1)
            return ('e', dj['eng']), esem[dj['eng']], dj['val']

        def run(engname, eng):
            known = {}
            for i in self.per_eng[engname]:
                o = ops[i]
                waits = {}
                for j in o['deps']:
                    dj = ops[j]
                    if (dj['dma'] is None and not dj['cc'] and dj['eng'] == engname
                            and o['dma'] is None and not o['cc']):
                        if engname == 'pe' or not SELF_SYNC:
                            continue
                    key, sem, val = target(j)
                    if known.get(key, 0) >= val:
                        continue
                    if key not in waits or waits[key][1] < val:
                        waits[key] = (sem, val)
                for key, (sem, val) in waits.items():
                    eng.wait_ge(sem, val)
                    known[key] = val
                ins = o['fn'](eng)
                if o['cc']:
                    ins.then_inc(ccsem[o['ccid']])
                elif o['dma'] is not None:
                    ins.then_inc(dsems[o['dma'][0]][o['dma'][1] % NDMA_SEMS], 16)
                elif o['users']:
                    ins.then_inc(esem[engname], 1)
            if engname == 'sp':
                for j in self.fin:
                    key, sem, val = target(j)
                    if known.get(key, 0) >= val:
                        continue
                    eng.wait_ge(sem, val)
                    known[key] = val

        block.tensor(lambda e: run('pe', e))
        block.scalar(lambda e: run('act', e))
        block.vector(lambda e: run('dve', e))
        block.gpsimd(lambda e: run('pool', e))
        block.sync(lambda e: run('sp', e))


class B:
    def __init__(self, nc):
        self.nc = nc
        self.T = Tracker()
        self.off = 0
        self.n = 0
        self.scope = None

    def sb(self, shape, dt):
        t = self.scope.enter_context(self.nc.sbuf_tensor("sb%d" % self.n, list(shape), dt))
        self.n += 1
        return t

    @staticmethod
    def _n(ap):
        sh = ap.shape
        n = 1
        for x in sh[1:]:
            n *= int(x)
        return n

    def _c(self, eng, out, in_=None):
        n = self._n(out)
        if eng == 'act':
            return 155.0 + n * 0.835
        if eng == 'dve':
            f = 0.96
            return 60.0 + n / f
        if eng == 'pool':
            return 100.0 + n / 0.6
        return 100.0

    def mm(self, out, lhsT, rhs, start=True, stop=True, r=(), w=()):
        n = self._n(rhs)
        m = self._n(lhsT)
        f4 = 4.0 if rhs.dtype == F32 else 1.0
        c = 10.0 + m / 1.2 * (2.0 if f4 > 1 else 1.0) + (n / 2.4) * f4
        return self.T.op('pe', lambda e: e.matmul(out, lhsT=lhsT, rhs=rhs, start=start, stop=stop), r, w, cost=c, lat=0.0)

    def tr(self, out, in_, ident, r=(), w=()):
        return self.T.op('pe', lambda e: e.transpose(out=out, in_=in_, identity=ident), r, w, cost=10.0 + 128 / 1.2 + self._n(in_) / 2.4, lat=0.0)

    def act(self, out, in_, func, bias=0.0, scale=1.0, accum_out=None, r=(), w=()):
        c = self._c('act', out)
        if accum_out is None:
            return self.T.op('act', lambda e: e.activation(out=out, in_=in_, func=func, bias=bias, scale=scale), r, w, cost=c)
        return self.T.op('act', lambda e: e.activation(out=out, in_=in_, func=func, bias=bias, scale=scale,
                                                       accum_out=accum_out), r, w, cost=c)

    def ts(self, eng, out, in0, s1, s2, op0, op1=None, r=(), w=()):
        c = self._c(eng, out)
        if op1 is None:
            return self.T.op(eng, lambda e: e.tensor_scalar(out=out, in0=in0, scalar1=s1, scalar2=None, op0=op0), r, w, cost=c)
        return self.T.op(eng, lambda e: e.tensor_scalar(out=out, in0=in0, scalar1=s1, scalar2=s2, op0=op0, op1=op1), r, w, cost=c)

    def tt(self, eng, out, in0, in1, op, r=(), w=()):
        return self.T.op(eng, lambda e: e.tensor_tensor(out=out, in0=in0, in1=in1, op=op), r, w, cost=self._c(eng, out))

    def stt(self, eng, out, in0, scalar, in1, op0, op1, r=(), w=()):
        return self.T.op(eng, lambda e: e.scalar_tensor_tensor(out=out, in0=in0, scalar=scalar, in1=in1,
                                                               op0=op0, op1=op1), r, w, cost=self._c(eng, out))

    def cp(self, eng, out, in_, r=(), w=()):
        return self.T.op(eng, lambda e: e.tensor_copy(out=out, in_=in_), r, w, cost=self._c(eng, out))

    def recip(self, out, in_, r=(), w=()):
        return self.T.op('dve', lambda e: e.reciprocal(out=out, in_=in_), r, w, cost=self._c('dve', out))

    def memset(self, eng, ap, val, w=()):
        return self.T.op(eng, lambda e: e.memset(ap, val), (), w, cost=self._c(eng, ap))

    def asel(self, out, in_, pattern, cmp, fill, base, cm, r=(), w=()):
        return self.T.op('pool', lambda e: e.affine_select(out=out, in_=in_, pattern=pattern, compare_op=cmp,
                                                           fill=fill, base=base, channel_multiplier=cm), r, w, cost=self._c('pool', out))

    def rsum(self, eng, out, in_, r=(), w=()):
        return self.T.op(eng, lambda e: e.reduce_sum(out=out, in_=in_, axis=AX.X), r, w, cost=self._c(eng, in_))

    def dma(self, out, in_, r=(), w=(), eng='sp'):
        sh = out.shape
        nbytes = 1
        for x in sh:
            nbytes *= int(x)
        nbytes *= 4 if out.dtype == F32 else 2
        return self.T.op(eng, lambda e: e.dma_start(out=out, in_=in_), r, w, dma=True, cost=60.0, lat=2000.0 + nbytes / 60.0)


def bc3(ap2, n):
    return ap2.unsqueeze(2).to_broadcast([ap2.shape[0], ap2.shape[1], n])


def bcm(ap2, a):
    return ap2.unsqueeze(1).to_broadcast([ap2.shape[0], a, ap2.shape[1]])


def ag_chunk_blocks(TW):
    import os
    return max(1, (int(os.environ.get("AGKB", "768")) * 1024) // (128 * TW * 2))


def gath_row(src, blk, NBLK, CH):
    k, w = divmod(blk, CH)
    nbk = min(CH, NBLK - k * CH)
    return 8 * 128 * CH * k + src * (nbk * 128) + w * 128


def cfg(SEQ):
    L = SEQ + NMETA
    NB = (L + 127) // 128
    LP = NB * 128
    Q = SEQ // 4
    QH = Q + 16
    return L, NB, LP, Q, QH


NF = 128 + 128 + 64 * 3
NT = 65 + 64 + 2
UP_PIECES = [(i * 512, 512) for i in range(11)]


def build(SEQ, part=0, debug=False):
    L, NB, LP, Q, QH = cfg(SEQ)
    TW = min(512, Q)
    NT2 = Q // TW
    NTT = NT2 + 1
    assert Q % 128 == 0 and Q % TW == 0
    nc = bass.Bass("TRN2", target_bir_lowering=False)

    P1IN = {"xf", "w1f", "w1t", "pp", "convw", "gnw", "g_mix"}
    P2IN = {"x2", "g_mix", "g_ffn", "g_fin", "wg", "gbias", "wbf", "wbg", "wo", "wup", "fcw", "fcb", "wdn"}

    def din(name, shape):
        if part == 1 and name not in P1IN:
            return None
        if part == 2 and name not in P2IN:
            return None
        return nc.dram_tensor(name, list(shape), F32, kind="ExternalInput").ap()

    xf = din("xf", [2, LP, D])
    x2 = din("x2", [QH, D])
    w1f = din("w1f", [D, NF])
    w1t = din("w1t", [D, NT])
    pp = din("pp", [128, 8])
    convw = din("convw", [64, 12])
    gnw = din("gnw", [64, 64])
    g_mix = din("g_mix", [128, 8])
    g_ffn = din("g_ffn", [128, 8])
    g_fin = din("g_fin", [128, D])
    wg = din("wg", [D, 2 * D])
    gbias = din("gbias", [128, 16])
    wbf = din("wbf", [512, D])
    wbg = din("wbg", [512, D])
    wo = din("wo", [D, D])
    wup = din("wup", [D, 2 * DFF])
    fcw = din("fcw", [128, 44, 3])
    fcb = din("fcb", [128, 44])
    wdn = din("wdn", [DFF, D])
    if part != 1:
        out = nc.dram_tensor("out", [Q, D], F32, kind="ExternalOutput").ap()
        idx = nc.dram_tensor("idx", [128, 8 * NTT], mybir.dt.int32, kind="ExternalInput").ap()

    if part == 1:
        a2a_in = nc.dram_tensor("a2a_in", [8 * NTT * 128, TW], BF16, kind="ExternalOutput").ap()
    elif part == 2:
        gath = nc.dram_tensor("gath", [8 * NTT * 128, TW], BF16, kind="ExternalInput").ap()
    else:
        a2a_in = nc.dram_tensor("a2a_in", [8 * NTT * 128, TW], BF16).ap()
        gath = nc.dram_tensor("gath", [8 * 8 * NTT * 128, TW], BF16).ap()
    wg_b = nc.dram_tensor("wg_b", [D, 2 * D], BF16).ap()
    wbf_b = nc.dram_tensor("wbf_b", [512, D], BF16).ap()
    wbg_b = nc.dram_tensor("wbg_b", [512, D], BF16).ap()
    wo_b = nc.dram_tensor("wo_b", [D, D], BF16).ap()
    wup_b = nc.dram_tensor("wup_b", [D, 2 * DFF], BF16).ap()
    wdn_b = nc.dram_tensor("wdn_b", [DFF, D], BF16).ap()

    b = B(nc)
    T = b.T
    outer = contextlib.ExitStack()
    b.scope = outer
    PS = [nc.alloc_psum_tensor("ps%d" % i, [128, 512], F32) for i in range(8)]
    PSB = [nc.alloc_psum_tensor("psb%d" % i, [128, 1024], BF16) for i in range(0)]

    identb = b.sb([128, 128], BF16)
    trium = b.sb([128, 128], BF16)
    U128 = b.sb([128, 128], F32)
    ones128 = b.sb([128, 128], F32)
    m_upi = b.sb([64, 8, 64], F32)
    m_lows = b.sb([64, 8, 64], F32)
    I8 = b.sb([64, 8, 64], BF16)
    tmpf = b.sb([128, 128], F32)
    ppt = b.sb([128, 8], F32)
    negfb = b.sb([128, 1], F32)
    negA = b.sb([128, 1], F32)
    gmix = b.sb([128, 8], F32)
    gffn = b.sb([128, 8], F32)
    bar_t = b.sb([128, 1], F32)
    CONST_END = b.off

    b.memset('pool', tmpf[:], 1.0, w=['tmpf'])
    b.asel(tmpf[:], tmpf[:], [[-1, 128]], ALU.is_equal, 0.0, 0, 1, r=['tmpf'], w=['tmpf'])
    b.cp('dve', identb[:], tmpf[:], r=['tmpf'], w=['identb'])
    b.memset('pool', ones128[:], 1.0, w=['ones128'])
    b.asel(U128[:], ones128[:], [[1, 128]], ALU.is_ge, 0.0, 0, -1, r=['ones128'], w=['U128'])
    b.cp('dve', trium[:], U128[:], r=['U128'], w=['trium'])
    b.memset('pool', m_upi[:], 1.0, w=['m_upi'])
    b.asel(m_upi[:], m_upi[:], [[0, 8], [1, 64]], ALU.is_ge, 0.0, 0, -1, r=['m_upi'], w=['m_upi'])
    b.memset('pool', m_lows[:], 1.0, w=['m_lows'])
    b.asel(m_lows[:], m_lows[:], [[0, 8], [-1, 64]], ALU.is_gt, 0.0, 0, 1, r=['m_lows'], w=['m_lows'])
    i8f = tmpf[0:64, :].rearrange("p (a c) -> p a c", a=2)
    b.memset('pool', tmpf[:], 1.0, w=['tmpf'])
    b.asel(i8f, i8f, [[0, 2], [-1, 64]], ALU.is_equal, 0.0, 0, 1, r=['tmpf'], w=['tmpf'])
    for a in range(4):
        b.cp('dve', I8[:, 2 * a:2 * a + 2, :], i8f, r=['tmpf'], w=['I8'])
    if part != 2:
        b.dma(ppt[:], pp[:, :], w=['ppt'])
    else:
        b.memset('pool', ppt[:], 0.0, w=['ppt'])
    if part != 1:
        b.dma(gffn[:], g_ffn[:, :], w=['gffn'])
    b.dma(gmix[:], g_mix[:, :], w=['gmix'])
    b.ts('dve', negfb[:], ppt[:, 0:1], -1.0, None, ALU.mult, r=['ppt'], w=['negfb'])
    b.act(negA[:], ppt[:, 1:2], AF.Exp, r=['ppt'], w=['negA'])
    b.ts('dve', negA[:], negA[:], -1.0, None, ALU.mult, r=['negA'], w=['negA'])

    P1_BASE = b.off
    NSTG = 6 if part == 2 else 2
    stg = [b.sb([128, 512], F32) for _ in range(NSTG)] if part != 1 else None
    stb = [b.sb([128, 512], BF16) for _ in range(NSTG)] if part != 1 else None
    cnt = [0]

    pieces = []

    def conv_piece(src, dst, r0, c0, nc_, gain):
        pieces.append((src, dst, r0, c0, nc_, gain))

    def conv_emit(src, dst, r0, c0, nc_, gain):
        T.tag = 'conv'
        s = cnt[0] % NSTG
        cnt[0] += 1
        b.dma(stg[s][:, 0:nc_], src[r0:r0 + 128, c0:c0 + nc_], w=['stg%d' % s])
        if gain is None:
            b.cp('pool', stb[s][:, 0:nc_], stg[s][:, 0:nc_], r=['stg%d' % s], w=['stb%d' % s])
        else:
            b.ts('pool', stb[s][:, 0:nc_], stg[s][:, 0:nc_], gain, None, ALU.mult,
                 r=['stg%d' % s, 'gmix', 'gffn'], w=['stb%d' % s])
        b.dma(dst[r0:r0 + 128, c0:c0 + nc_], stb[s][:, 0:nc_], r=['stb%d' % s], w=[('W', dst.tensor.name, r0 // 128, c0 // 512)])

    for k in range(8):
        for c0 in range(0, 2048, 512):
            conv_piece(wg, wg_b, k * 128, c0, 512, gmix[:, k:k + 1])
    for k in range(4):
        for c0 in (0, 512):
            conv_piece(wbf, wbf_b, k * 128, c0, 512, None)
            conv_piece(wbg, wbg_b, k * 128, c0, 512, None)
    for k in range(8):
        for c0 in (0, 512):
            conv_piece(wo, wo_b, k * 128, c0, 512, None)
    for c0 in range(0, 5632, 512):
        for k in range(8):
            conv_piece(wup, wup_b, k * 128, c0, 512, gffn[:, k:k + 1])
    for k in range(22):
        for c0 in (0, 512):
            conv_piece(wdn, wdn_b, k * 128, c0, 512, None)

    p1scope = contextlib.ExitStack()
    b.scope = p1scope
    if part != 2:
        b.off = P1_BASE + 2 * 4096 + 2 * 2048
        w1fs = b.sb([128, 8, NF], BF16)
        w1ts = b.sb([128, 8, NT], BF16)
        cw = b.sb([64, 12], F32)
        gnws = b.sb([64, 64], F32)
        KT = b.sb([128, LP], BF16)
        Vst = b.sb([128, 2, NB, 65], BF16)
        cneg = b.sb([128, 2, NB], F32)
        cend = b.sb([128, 2, NB], F32)
        carry = b.sb([128, 2], F32)
        hbuf = [b.sb([128, 4, D], F32) for _ in range(2)]
        hnb = b.sb([128, 4, D], BF16)
        hnT1 = b.sb([128, 8, 512], BF16)
        hnT = [hnT1, hnT1]
        QT = [b.sb([128, 512], BF16) for _ in range(2)]
        ssq = b.sb([128, 8], F32)
        rstd = b.sb([128, 8], F32)
        PT = [b.sb([128, 512], BF16) for _ in range(4)]
        biasq = [b.sb([128, NB], F32) for _ in range(2)]
        spt = b.sb([128, 4], F32)
        pre = b.sb([128, 4], F32)
        osb = b.sb([65, 512], F32)
        rdn = b.sb([64, 512], F32)
        sel65 = b.sb([65, 64], F32)
        oTf = [b.sb([64, 512], BF16) for _ in range(2)]
        oTg = [b.sb([64, 512], BF16) for _ in range(2)]
        cin = [[b.sb([64, 3 + 512], F32) for g in range(3)] for bb in range(2)]
        Sst = [b.sb([64, 64], F32) for bb in range(2)]
        Sb = [b.sb([64, 64], BF16) for bb in range(2)]

        def g512(dt):
            return b.sb([64, 8, 64], dt)

        cy = [g512(F32) for _ in range(3)]
        ex = g512(F32)
        sq = g512(F32)
        qh = g512(BF16)
        kh = g512(BF16)
        vT = g512(BF16)
        vb = g512(BF16)
        kbg = g512(BF16)
        ktail = g512(BF16)
        Dm = g512(F32)
        E1 = g512(F32)
        E2 = g512(F32)
        A_ = [g512(BF16) for _ in range(2)]
        B_ = [g512(BF16) for _ in range(2)]
        IA = g512(BF16)
        R_ = [g512(BF16) for _ in range(2)]
        attnT = g512(BF16)
        val = g512(F32)
        kcdT = g512(BF16)
        og = g512(F32)
        ogb = g512(BF16)
        zsb = [g512(F32) for _ in range(2)]
        beta2 = [b.sb([64, 8], F32) for _ in range(2)]
        g2 = [b.sb([64, 8], F32) for _ in range(2)]
        qS = b.sb([64, 64], F32)
        vn = b.sb([64, 64], BF16)
        sc8 = {nm: b.sb([64, 8], F32) for nm in ['beta', 'g', 'gc', 'gtot', 'egc', 'etail', 'cd', 'bege', 'ss', 'rs', 'nb']}

        w1stage = hbuf[1][:].rearrange("p t d -> p (t d)")[:, 0:8 * NF].rearrange("p (k n) -> p k n", k=8)
        b.dma(w1stage, w1f.rearrange("(k p) n -> p k n", p=128), w=['h1'])
        for k in range(8):
            b.ts('dve', w1fs[:, k, :], w1stage[:, k, :], gmix[:, k:k + 1], None, ALU.mult, r=['h1', 'gmix'], w=['w1fs'])
        b.ts('dve', w1fs[:, :, 0:128], w1fs[:, :, 0:128], 0.125, None, ALU.mult, r=['w1fs'], w=['w1fs'])
        b.dma(w1stage[:, :, 0:NT], w1t.rearrange("(k p) n -> p k n", p=128), r=['w1fs'], w=['h1'])
        for k in range(8):
            b.ts('dve', w1ts[:, k, :], w1stage[:, k, 0:NT], gmix[:, k:k + 1], None, ALU.mult, r=['h1', 'gmix'], w=['w1ts'])
        b.dma(cw[:], convw[:, :], w=['cw'])
        b.dma(gnws[:], gnw[:, :], w=['gnws'])
        b.memset('pool', Vst[:], 1.0, w=['Vst_%d_%d' % (bb_, t_) for bb_ in range(2) for t_ in range(0, NB, 4)])
        b.memset('pool', carry[:], 0.0, w=['carry'])
        b.memset('pool', sel65[:], 0.0, w=['sel65'])
        b.memset('pool', sel65[64:65, :], 1.0, w=['sel65'])
        for bb in range(2):
            for g in range(3):
                b.memset('pool', cin[bb][g][:, 0:3], 0.0, w=['cin%d%d' % (bb, g)])
            b.memset('pool', Sst[bb][:], 0.0, w=['S%d' % bb])
            b.memset('pool', Sb[bb][:], 0.0, w=['Sb%d' % bb])

        nc._dbg = dict(KT=KT, Vst=Vst, cneg=cneg, cend=cend, hnT=hnT1, QT0=QT[0], QT1=QT[1], w1fs=w1fs, w1ts=w1ts, hnb=hnb, rstd=rstd, oTf0=oTf[0], oTg0=oTg[0], kh=kh, qh=qh, vT=vT, og=og, val=val, A0=A_[0], E1=E1, E2=E2, R1=R_[1], beta=sc8['beta'], g=sc8['g'], gc=sc8['gc'], S0=Sst[0])
        nc._p1_free = nc.sbuf_bytes_remaining
        tiles = []
        for t0 in range(0, NB, 4):
            for bb in range(2):
                tiles.append((bb, t0, min(4, NB - t0)))

        psrot = [0]

        def stage_A(ti):
            T.tag = 'A'
            bb, t0, nblk = tiles[ti]
            s = ti % 2
            ntok = nblk * 128
            p0 = t0 * 128
            hk, hTk, qk = 'h%d' % s, 'hnT', 'QT%d' % s
            b.dma(hbuf[s][:, 0:nblk, :], xf[bb, p0:p0 + ntok, :].rearrange("(t p) d -> p t d", p=128), w=[hk])
            b.memset('pool', ssq[:], 0.0, w=['ssq'])
            for t in range(nblk):
                b.act(hnb[:, t, :], hbuf[s][:, t, :], AF.Square, accum_out=ssq[:, t:t + 1], r=[hk], w=['hnb', 'ssq'])
            b.act(rstd[:, 0:nblk], ssq[:, 0:nblk], AF.Ln, bias=EPS, scale=1.0 / D, r=['ssq'], w=['rstd'])
            b.act(rstd[:, 0:nblk], rstd[:, 0:nblk], AF.Exp, scale=-0.5, r=['rstd'], w=['rstd'])
            for t in range(nblk):
                b.ts('dve', hnb[:, t, :], hbuf[s][:, t, :], rstd[:, t:t + 1], None, ALU.mult, r=[hk, 'rstd'], w=['hnb'])
            pst = PS[0][:].bitcast(BF16).rearrange("p (k c) -> p k c", k=8)
            for t in range(nblk):
                for k in range(8):
                    b.tr(pst[:, k, :], hnb[:, t, k * 128:(k + 1) * 128], identb[:], r=['hnb', 'identb'], w=['ps0'])
                b.cp('dve', hnT[s][:, :, t * 128:(t + 1) * 128], pst, r=['ps0'], w=[hTk])
            groups = [(0, 128), (128, 128), (256, 64), (320, 64), (384, 64)]
            for gi, (c0, m) in enumerate(groups):
                pb = 1
                psrot[0] += 1
                pk = 'ps%d' % pb
                for k in range(8):
                    b.mm(PS[pb][0:m, 0:ntok], w1fs[:, k, c0:c0 + m], hnT[s][:, k, 0:ntok], start=(k == 0), stop=(k == 7),
                         r=['w1fs', hTk], w=[pk])
                lo = bb * 64
                if gi == 0:
                    b.cp('dve', QT[s][lo:lo + 64, 0:ntok], PS[pb][lo:lo + 64, 0:ntok], r=[pk], w=[qk])
                elif gi == 1:
                    b.cp('dve', KT[lo:lo + 64, p0:p0 + ntok], PS[pb][lo:lo + 64, 0:ntok], r=[pk], w=['KT_%d_%d' % (bb, t0)])
                else:
                    g = gi - 2
                    b.cp('dve', cin[bb][g][:, 3:3 + ntok], PS[pb][0:64, 0:ntok], r=[pk], w=['cin%d%d' % (bb, g)])
            psv = PS[0][:, 0:260].rearrange("p (t c) -> p t c", t=4)
            for t in range(nblk):
                for k in range(8):
                    b.mm(psv[:, t, :], hnT[s][:, k, t * 128:(t + 1) * 128], w1ts[:, k, 0:65], start=(k == 0), stop=(k == 7),
                         r=['w1ts', hTk], w=['ps0'])
            b.cp('dve', Vst[:, bb, t0:t0 + nblk, 0:64], psv[:, 0:nblk, 0:64], r=['ps0'], w=['Vst_%d_%d' % (bb, t0)])
            b.act(spt[:, 0:nblk], psv[:, 0:nblk, 64], AF.Exp, bias=negfb[:, 0:1], scale=-1.0, r=['ps0', 'negfb'], w=['spt'])
            b.act(spt[:, 0:nblk], spt[:, 0:nblk], AF.Ln, bias=1.0, r=['spt'], w=['spt'])
            b.mm(PS[0][:, 264:264 + nblk], U128[:], spt[:, 0:nblk], r=['U128', 'spt'], w=['ps0'])
            b.mm(PS[0][:, 272:272 + nblk], ones128[:], spt[:, 0:nblk], r=['ones128', 'spt'], w=['ps0'])
            for t in range(nblk):
                prev = carry[:, bb:bb + 1] if t == 0 else cend[:, bb, t0 + t - 1:t0 + t]
                b.tt('dve', cend[:, bb, t0 + t:t0 + t + 1], PS[0][:, 272 + t:273 + t], prev, ALU.add,
                     r=['ps0', 'carry', 'cend_%d_%d' % (bb, t0), 'cend_%d_%d' % (bb, max(t0 - 4, 0))], w=['cend_%d_%d' % (bb, t0)])
                b.tt('dve', cneg[:, bb, t0 + t:t0 + t + 1], PS[0][:, 264 + t:265 + t], prev, ALU.add,
                     r=['ps0', 'carry', 'cend_%d_%d' % (bb, t0), 'cend_%d_%d' % (bb, max(t0 - 4, 0))], w=['cneg_%d_%d' % (bb, t0)])
            b.cp('dve', carry[:, bb:bb + 1], cend[:, bb, t0 + nblk - 1:t0 + nblk], r=['cend_%d_%d' % (bb, t0)], w=['carry'])
            nch = nblk * 2
            for c4 in range(0, nch, 4):
                n4 = min(4, nch - c4)
                pz = PS[1][0:64, 0:264].rearrange("p (n c) -> p n c", n=4)
                for n in range(n4):
                    cn = c4 + n
                    for k in range(8):
                        b.mm(pz[:, n, :], hnT[s][:, k, cn * 64:(cn + 1) * 64], w1ts[:, k, 65:131], start=(k == 0), stop=(k == 7),
                             r=['w1ts', hTk], w=['ps1'])
                b.cp('dve', zsb[s][:, c4:c4 + n4, :], pz[:, 0:n4, 0:64], r=['ps1'], w=['zs%d' % s])
                b.act(beta2[s][:, c4:c4 + n4], pz[:, 0:n4, 64], AF.Exp, scale=-1.0, r=['ps1'], w=['beta%d' % s])
                b.act(g2[s][:, c4:c4 + n4], pz[:, 0:n4, 65], AF.Exp, bias=ppt[0:64, 2:3], r=['ps1', 'ppt'], w=['g%d' % s])

        def stage_B(ti):
            T.tag = 'B'
            bb, t0, nblk = tiles[ti]
            s = ti % 2
            qk = 'QT%d' % s
            lo = bb * 64
            nkb = t0 + nblk
            NQ = nblk * 128
            halves = [(h0, min(2, nblk - h0)) for h0 in range(0, nblk, 2)]
            for hi, (h0, hn_) in enumerate(halves):
                b.ts('dve', biasq[hi][:, 0:nkb], cneg[:, bb, 0:nkb], cend[:, bb, t0 + h0:t0 + h0 + 1], None, ALU.subtract,
                     r=['cneg_%d_%d' % (bb, t_) for t_ in range(0, nkb, 4)] + ['cend_%d_%d' % (bb, t0)], w=['biasq%d' % hi])
            psoT = PS[6][0:65, 0:NQ]
            for kb in range(nkb):
                qlo = max(0, kb - t0)
                sl = (3, 4, 5)[kb % 3]
                ps_s = PS[sl][:, 0:512]
                pts = kb % 4
                c0 = qlo * 128
                b.mm(ps_s[:, c0:NQ], KT[lo:lo + 64, kb * 128:(kb + 1) * 128], QT[s][lo:lo + 64, c0:NQ],
                     r=['KT_%d_%d' % (bb, kb // 4 * 4), qk], w=['ps%d' % sl])
                for hi, (h0, hn_) in enumerate(halves):
                    a0 = max(c0, h0 * 128)
                    a1 = (h0 + hn_) * 128
                    if a0 >= a1:
                        continue
                    b.act(PT[pts][:, a0:a1], ps_s[:, a0:a1], AF.Exp, bias=biasq[hi][:, kb:kb + 1],
                          r=['ps%d' % sl, 'biasq%d' % hi], w=['PT%d' % pts])
                if qlo > 0:
                    b.memset('pool', PT[pts][:, 0:c0], 0.0, w=['PT%d' % pts])
                if kb >= t0:
                    j = kb - t0
                    b.tt('pool', PT[pts][:, j * 128:(j + 1) * 128], PT[pts][:, j * 128:(j + 1) * 128], trium[:], ALU.mult,
                         r=['PT%d' % pts, 'trium'], w=['PT%d' % pts])
                b.mm(psoT, Vst[:, bb, kb, :], PT[pts][:, 0:NQ], start=(kb == 0), stop=(kb == nkb - 1),
                     r=['PT%d' % pts, 'Vst_%d_%d' % (bb, kb // 4 * 4)], w=['ps6'])
            b.cp('dve', osb[:, 0:NQ], psoT, r=['ps6'], w=['osb'])
            b.mm(PS[6][0:64, 0:NQ], sel65[:, :], osb[:, 0:NQ], r=['osb', 'sel65'], w=['ps6'])
            b.recip(rdn[:, 0:NQ], PS[6][0:64, 0:NQ], r=['ps6'], w=['rdn'])
            b.tt('dve', oTf[s][:, 0:NQ], osb[0:64, 0:NQ], rdn[:, 0:NQ], ALU.mult, r=['osb', 'rdn'], w=['oTf%d' % s])

        def stage_C(ti):
            T.tag = 'C'
            bb, t0, nblk = tiles[ti]
            s = ti % 2
            ntok = nblk * 128
            nch = nblk * 2
            W = nch * 64

            def v3(t):
                return t[:, 0:nch, :]

            def v2(t):
                return t[:].rearrange("p n c -> p (n c)")[:, 0:W]

            for g in range(3):
                ck = 'cin%d%d' % (bb, g)
                b.ts('pool', v2(cy[g]), cin[bb][g][:, 0:W], cw[:, g * 4:g * 4 + 1], None, ALU.mult, r=[ck, 'cw'], w=['cy%d' % g])
                for j in range(1, 4):
                    b.stt('dve', v2(cy[g]), cin[bb][g][:, j:j + W], cw[:, g * 4 + j:g * 4 + j + 1], v2(cy[g]), ALU.mult, ALU.add,
                          r=[ck, 'cw', 'cy%d' % g], w=['cy%d' % g])
                b.cp('pool', cin[bb][g][:, 0:3], cin[bb][g][:, W:W + 3], r=[ck], w=[ck])
                b.act(v2(ex), v2(cy[g]), AF.Exp, scale=-1.0, r=['cy%d' % g], w=['ex'])
                b.ts('dve', v2(ex), v2(ex), 1.0, None, ALU.add, r=['ex'], w=['ex'])
                b.recip(v2(ex), v2(ex), r=['ex'], w=['ex'])
                if g == 2:
                    b.tt('dve', v2(vT), v2(cy[g]), v2(ex), ALU.mult, r=['ex', 'cy2'], w=['vT'])
                else:
                    b.tt('dve', v2(cy[g]), v2(cy[g]), v2(ex), ALU.mult, r=['ex', 'cy%d' % g], w=['cy%d' % g])
                    b.tt('pool', v2(sq), v2(cy[g]), v2(cy[g]), ALU.mult, r=['cy%d' % g], w=['sq'])
                    b.mm(PS[2][0:64, 0:W], ones128[0:64, 0:64], v2(sq), r=['sq', 'ones128'], w=['ps2'])
                    b.act(v2(ex), PS[2][0:64, 0:W], AF.Ln, bias=EPS, r=['ps2'], w=['ex'])
                    b.act(v2(ex), v2(ex), AF.Exp, scale=-0.5, r=['ex'], w=['ex'])
                    if g == 0:
                        b.stt('dve', v2(qh), v2(cy[g]), 0.125, v2(ex), ALU.mult, ALU.mult, r=['cy0', 'ex'], w=['qh'])
                    else:
                        b.tt('dve', v2(kh), v2(cy[g]), v2(ex), ALU.mult, r=['cy1', 'ex'], w=['kh'])
            be, gg = beta2[s], g2[s]
            b.ts('dve', be[:, 0:nch], be[:, 0:nch], 1.0, None, ALU.add, r=['beta%d' % s], w=['beta%d' % s])
            b.recip(be[:, 0:nch], be[:, 0:nch], r=['beta%d' % s], w=['beta%d' % s])
            b.act(gg[:, 0:nch], gg[:, 0:nch], AF.Ln, bias=1.0, r=['g%d' % s], w=['g%d' % s])
            b.ts('dve', gg[:, 0:nch], gg[:, 0:nch], negA[0:64, 0:1], None, ALU.mult, r=['g%d' % s, 'negA'], w=['g%d' % s])
            b.mm(PS[2][0:64, 0:nch], U128[0:64, 0:64], gg[:, 0:nch], r=['g%d' % s, 'U128'], w=['ps2'])
            b.mm(PS[2][0:64, 8:8 + nch], ones128[0:64, 0:64], gg[:, 0:nch], r=['g%d' % s, 'ones128'], w=['ps2'])
            b.cp('dve', sc8['gc'][:, 0:nch], PS[2][0:64, 0:nch], r=['ps2'], w=['gc'])
            b.cp('dve', sc8['gtot'][:, 0:nch], PS[2][0:64, 8:8 + nch], r=['ps2'], w=['gtot'])
            b.act(sc8['egc'][:, 0:nch], sc8['gc'][:, 0:nch], AF.Exp, r=['gc'], w=['egc'])
            b.act(sc8['cd'][:, 0:nch], sc8['gtot'][:, 0:nch], AF.Exp, r=['gtot'], w=['cd'])
            b.tt('dve', sc8['etail'][:, 0:nch], sc8['gtot'][:, 0:nch], sc8['gc'][:, 0:nch], ALU.subtract, r=['gtot', 'gc'], w=['etail'])
            b.act(sc8['etail'][:, 0:nch], sc8['etail'][:, 0:nch], AF.Exp, r=['etail'], w=['etail'])
            b.tt('dve', sc8['bege'][:, 0:nch], be[:, 0:nch], sc8['egc'][:, 0:nch], ALU.mult, r=['beta%d' % s, 'egc'], w=['bege'])
            b.ts('dve', sc8['nb'][:, 0:nch], be[:, 0:nch], -1.0, None, ALU.mult, r=['beta%d' % s], w=['nb'])
            b.tt('dve', v3(sq), bcm(U128[0:64, 0:64], nch), bc3(gg[:, 0:nch], 64), ALU.mult, r=['U128', 'g%d' % s], w=['sq'])
            b.mm(PS[2][0:64, 0:W], ones128[0:64, 0:64], v2(sq), r=['sq', 'ones128'], w=['ps2'])
            ps7v = PS[2][0:64, :].rearrange("p (n c) -> p n c", n=8)[:, 0:nch, :]
            b.tt('dve', v3(Dm), bc3(sc8['gc'][:, 0:nch], 64), ps7v, ALU.subtract, r=['gc', 'ps2'], w=['Dm'])
            b.ts('dve', v3(E1), v3(Dm), 0.0, None, ALU.min, r=['Dm'], w=['E1'])
            b.ts('dve', v3(E2), v3(Dm), 0.0, -1.0, ALU.max, ALU.mult, r=['Dm'], w=['E2'])
            b.act(v2(E1), v2(E1), AF.Exp, r=['E1'], w=['E1'])
            b.act(v2(E2), v2(E2), AF.Exp, r=['E2'], w=['E2'])
            b.tt('pool', v3(E1), v3(E1), m_lows[:, 0:nch, :], ALU.mult, r=['E1', 'm_lows'], w=['E1'])
            b.tt('pool', v3(E1), v3(E1), bc3(sc8['nb'][:, 0:nch], 64), ALU.mult, r=['E1', 'nb'], w=['E1'])
            b.tt('pool', v3(E2), v3(E2), m_upi[:, 0:nch, :], ALU.mult, r=['E2', 'm_upi'], w=['E2'])
            pT = PS[7][0:64, :].bitcast(BF16).rearrange("p (n c) -> p n c", n=8)
            for n in range(nch):
                b.tr(pT[:, n, 0:64], kh[:, n, :], identb[0:64, 0:64], r=['kh', 'identb', 'zs%d' % s], w=['ps7'])
                b.tr(pT[:, n, 64:128], vT[:, n, :], identb[0:64, 0:64], r=['vT', 'identb'], w=['ps7'])
            b.tt('dve', v3(kbg), pT[:, 0:nch, 0:64], bc3(sc8['bege'][:, 0:nch], 64), ALU.mult, r=['ps7', 'bege'], w=['kbg'])
            b.tt('dve', v3(ktail), pT[:, 0:nch, 0:64], bc3(sc8['etail'][:, 0:nch], 64), ALU.mult, r=['ps7', 'etail'], w=['ktail'])
            b.tt('dve', v3(vb), pT[:, 0:nch, 64:128], bc3(be[:, 0:nch], 64), ALU.mult, r=['ps7', 'beta%d' % s], w=['vb'])
            psG = PS[2][0:64, :].rearrange("p (n c) -> p n c", n=8)
            for n in range(nch):
                b.mm(psG[:, n, :], kh[:, n, :], kh[:, n, :], r=['kh'], w=['ps2'])
            b.tt('dve', v3(A_[0]), psG[:, 0:nch, :], v3(E1), ALU.mult, r=['ps2', 'E1'], w=['A0'])
            for n in range(nch):
                b.mm(psG[:, n, :], kh[:, n, :], qh[:, n, :], r=['kh', 'qh'], w=['ps2'])
            b.tt('dve', v3(attnT), psG[:, 0:nch, :], v3(E2), ALU.mult, r=['ps2', 'E2'], w=['attnT'])
            pB = PS[7][0:64, 0:256].bitcast(BF16).rearrange("p (n c) -> p n c", n=8)
            for n in range(nch):
                b.tr(pB[:, n, :], A_[0][:, n, :], identb[0:64, 0:64], r=['A0', 'identb', 'kbg', 'ktail', 'vb'], w=['ps7'])
            b.cp('dve', v3(B_[0]), pB[:, 0:nch, :], r=['ps7'], w=['B0'])
            b.tt('pool', v3(R_[0]), v3(B_[0]), I8[:, 0:nch, :], ALU.add, r=['B0', 'I8'], w=['R0'])
            for lv in range(1, 6):
                ca, pa = lv % 2, (lv - 1) % 2
                for n in range(nch):
                    b.mm(psG[:, n, :], B_[pa][:, n, :], A_[pa][:, n, :], r=['A%d' % pa, 'B%d' % pa], w=['ps2'])
                b.cp('dve', v3(A_[ca]), psG[:, 0:nch, :], r=['ps2'], w=['A%d' % ca])
                b.tt('dve', v3(IA), psG[:, 0:nch, :], I8[:, 0:nch, :], ALU.add, r=['ps2', 'I8'], w=['IA'])
                if lv < 5:
                    psB2 = PS[7][0:64, :].rearrange("p (n c) -> p n c", n=8)
                    for n in range(nch):
                        b.mm(psB2[:, n, :], A_[pa][:, n, :], B_[pa][:, n, :], r=['A%d' % pa, 'B%d' % pa], w=['ps7'])
                    b.cp('dve', v3(B_[ca]), psB2[:, 0:nch, :], r=['ps7'], w=['B%d' % ca])
                for n in range(nch):
                    b.mm(psG[:, n, :], IA[:, n, :], R_[pa][:, n, :], r=['IA', 'R%d' % pa], w=['ps2'])
                b.cp('dve', v3(R_[ca]), psG[:, 0:nch, :], r=['ps2'], w=['R%d' % ca])
            TT = R_[1]
            for n in range(nch):
                b.mm(psG[:, n, :], TT[:, n, :], vb[:, n, :], r=['R1', 'vb'], w=['ps2'])
            b.cp('dve', v3(val), psG[:, 0:nch, :], r=['ps2'], w=['val'])
            psK = PS[7][0:64, :].rearrange("p (n c) -> p n c", n=8)
            for n in range(nch):
                b.mm(psK[:, n, :], kbg[:, n, :], TT[:, n, :], r=['R1', 'kbg'], w=['ps7'])
            b.cp('dve', v3(kcdT), psK[:, 0:nch, :], r=['ps7'], w=['kcdT'])
            sk, sbk = 'S%d' % bb, 'Sb%d' % bb
            for n in range(nch):
                p1 = PS[7][0:64, 0:128]
                p2 = PS[2][0:64, 0:128]
                b.mm(p1[:, 0:64], kcdT[:, n, :], Sb[bb][:], r=['kcdT', sbk], w=['ps7'])
                b.mm(p1[:, 64:128], qh[:, n, :], Sb[bb][:], r=['qh', sbk], w=['ps7'])
                b.tt('dve', vn[:], val[:, n, :], p1[:, 0:64], ALU.subtract, r=['val', 'ps7'], w=['vn'])
                b.ts('dve', qS[:], p1[:, 64:128], sc8['egc'][:, n:n + 1], None, ALU.mult, r=['ps7', 'egc'], w=['qS'])
                b.mm(p2[:, 0:64], attnT[:, n, :], vn[:], r=['attnT', 'vn'], w=['ps2'])
                b.mm(p2[:, 64:128], ktail[:, n, :], vn[:], r=['ktail', 'vn'], w=['ps2'])
                b.tt('dve', og[:, n, :], qS[:], p2[:, 0:64], ALU.add, r=['qS', 'ps2'], w=['og'])
                b.stt('dve', Sst[bb][:], Sst[bb][:], sc8['cd'][:, n:n + 1], p2[:, 64:128], ALU.mult, ALU.add,
                      r=[sk, 'cd', 'ps2'], w=[sk])
                b.cp('dve', Sb[bb][:], Sst[bb][:], r=[sk], w=[sbk])
            b.tt('pool', v3(Dm), v3(og), v3(og), ALU.mult, r=['og'], w=['Dm'])
            b.rsum('dve', sc8['ss'][:, 0:nch], v3(Dm), r=['Dm'], w=['ss'])
            b.act(sc8['rs'][:, 0:nch], sc8['ss'][:, 0:nch], AF.Ln, bias=EPS, scale=1.0 / 64, r=['ss'], w=['rs'])
            b.act(sc8['rs'][:, 0:nch], sc8['rs'][:, 0:nch], AF.Exp, scale=-0.5, r=['rs'], w=['rs'])
            b.tt('dve', v3(og), v3(og), bc3(sc8['rs'][:, 0:nch], 64), ALU.mult, r=['og', 'rs'], w=['og'])
            b.tt('pool', v3(og), v3(og), bcm(gnws[:], nch), ALU.mult, r=['og', 'gnws'], w=['og'])
            b.act(v2(ex), v2(zsb[s]), AF.Exp, scale=-1.0, r=['zs%d' % s], w=['ex'])
            b.ts('dve', v2(ex), v2(ex), 1.0, None, ALU.add, r=['ex'], w=['ex'])
            b.recip(v2(ex), v2(ex), r=['ex'], w=['ex'])
            b.tt('pool', v2(ex), v2(ex), v2(zsb[s]), ALU.mult, r=['ex', 'zs%d' % s], w=['ex'])
            b.tt('dve', v3(ogb), v3(og), v3(ex), ALU.mult, r=['og', 'ex'], w=['ogb'])
            pO = PS[7][0:64, 0:256].bitcast(BF16)
            for n in range(nch):
                b.tr(pO[:, n * 64:(n + 1) * 64], ogb[:, n, :], identb[0:64, 0:64], r=['ogb', 'identb', 'kcdT'], w=['ps7'])
            b.cp('dve', oTg[s][:, 0:W], pO[:, 0:W], r=['ps7'], w=['oTg%d' % s])

        def stage_store(ti):
            T.tag = 'store'
            bb, t0, nblk = tiles[ti]
            s = ti % 2
            for t in range(nblk):
                p = (t0 + t) * 128
                for src, rows, key in ((oTf[s], 0, 'oTf%d' % s), (oTg[s], 64, 'oTg%d' % s)):
                    blk = src[:, t * 128:(t + 1) * 128]
                    if p < 4 * Q:
                        j = p // Q
                        it, col = divmod(p - j * Q, TW)
                        d = ((bb * 4 + j) * NTT + it) * 128 + rows
                        b.dma(a2a_in[d:d + 64, col:col + 128], blk, r=[key], w=['a2a_in'])
                        if j >= 1 and p % Q == 0:
                            d = ((bb * 4 + j - 1) * NTT + NT2) * 128 + rows
                            b.dma(a2a_in[d:d + 64, 0:16], blk[:, 0:16], r=[key], w=['a2a_in'])
                    else:
                        d = ((bb * 4 + 3) * NTT + NT2) * 128 + rows
                        b.dma(a2a_in[d:d + 64, 0:16], blk[:, 0:16], r=[key], w=['a2a_in'])

        ztile = b.sb([128, TW], BF16)
        b.memset('pool', ztile[:], 0.0, w=['ztile'])
        for d_ in range(8):
            r_ = (d_ * NTT + NT2) * 128
            b.dma(a2a_in[r_:r_ + 128, :], ztile[:], r=['ztile'], w=['a2a_in'])
        per_tile = (len(pieces) + len(tiles) - 1) // len(tiles)
        for ti in range(len(tiles)):
            stage_A(ti)
            for pc in (pieces[ti * per_tile:(ti + 1) * per_tile] if part == 0 else []):
                conv_emit(*pc)
            stage_B(ti)
            stage_C(ti)
            stage_store(ti)

    if part == 0:
        NBLK = 8 * NTT
        CH = ag_chunk_blocks(TW)
        for k in range((NBLK + CH - 1) // CH):
            nbk = min(CH, NBLK - k * CH)
            src_ap = a2a_in[k * CH * 128:(k * CH + nbk) * 128, :]
            dst_ap = gath[8 * 128 * CH * k:8 * 128 * CH * k + 8 * nbk * 128, :]
            T.op('pool', lambda e, src_ap=src_ap, dst_ap=dst_ap: e.collective_compute(
                "AllGather", ALU.bypass, replica_groups=[list(range(8))], ins=[src_ap], outs=[dst_ap]),
                r=['a2a_in'], w=['gath'], cc=True, cost=300.0, lat=60000.0)
    T.barrier(lambda e: e.memset(bar_t[:], 0.0))
    p1scope.close()
    p2scope = contextlib.ExitStack()
    b.scope = p2scope
    if part == 2:
        for pc in pieces:
            conv_emit(*pc)
    if part != 1:
        b.off = P1_BASE
        h2 = b.sb([128, 4, D], F32)
        hb2 = b.sb([128, 4, D], BF16)
        hT2 = b.sb([128, 8, 512], BF16)
        oT2 = b.sb([128, 8, 512], BF16)
        gT = b.sb([128, 16, 512], BF16)
        mT = b.sb([128, 8, 512], BF16)
        aT = b.sb([128, 22, 512], BF16)
        uprev = b.sb([128, 44, 2], F32)
        ugL = [b.sb([128, 514], F32) for _ in range(2)]
        uuL = [b.sb([128, 514], F32) for _ in range(2)]
        cgL = [b.sb([128, 512], F32) for _ in range(2)]
        cuL = [b.sb([128, 512], F32) for _ in range(2)]
        yTL = [b.sb([128, 512], F32) for _ in range(2)]
        ug, uu, cg, cu, yT = ugL[0], uuL[0], cgL[0], cuL[0], yTL[0]
        gfin = b.sb([128, D], F32)
        gbs = b.sb([128, 16], F32)
        fcws = b.sb([128, 44, 3], F32)
        fcbs = b.sb([128, 44], F32)
        ss2 = b.sb([128, 8], F32)
        rs2 = b.sb([128, 8], F32)
        NWR = 6
        WR = [b.sb([128, 5632], BF16) for _ in range(NWR)]
        wcnt = [0]

        idxs = b.sb([128, 8 * NTT], mybir.dt.int32)
        b.dma(idxs[:], idx[:, :], w=['idxs'])
        b.dma(gfin[:], g_fin[:, :], w=['gfin'])
        b.dma(gbs[:], gbias[:, :], w=['gbs'])
        b.dma(fcws[:], fcw[:, :, :], w=['fcws'])
        b.dma(fcbs[:], fcb[:, :], w=['fcbs'])
        b.memset('pool', uprev[:], 0.0, w=['uprev'])
        b.memset('pool', h2[:], 0.0, w=['h2'])

        def wload(src, kch, c0, ncols):
            s_ = wcnt[0] % NWR
            wcnt[0] += 1
            v = WR[s_][:, 0:kch * ncols].rearrange("p (k n) -> p k n", k=kch)
            rk = [('W', src.tensor.name, k_, cb_) for k_ in range(kch) for cb_ in range(c0 // 512, (c0 + ncols + 511) // 512)]
            b.dma(v, src[:, c0:c0 + ncols].rearrange("(k p) n -> p k n", p=128), r=rk, w=['WR%d' % s_])
            return v, 'WR%d' % s_

        def norm_T(nblk, ntok, rows_ap_fn, key_h):
            T.tag = 'norm'
            b.memset('pool', ss2[:], 0.0, w=['ss2'])
            for t in range(nblk):
                b.act(hb2[:, t, :], h2[:, t, :], AF.Square, accum_out=ss2[:, t:t + 1], r=[key_h], w=['hb2', 'ss2'])
            b.act(rs2[:, 0:nblk], ss2[:, 0:nblk], AF.Ln, bias=EPS, scale=1.0 / D, r=['ss2'], w=['rs2'])
            b.act(rs2[:, 0:nblk], rs2[:, 0:nblk], AF.Exp, scale=-0.5, r=['rs2'], w=['rs2'])
            pst = PS[0][:].bitcast(BF16).rearrange("p (k c) -> p k c", k=8)
            for t in range(nblk):
                b.ts('dve', hb2[:, t, :], h2[:, t, :], rs2[:, t:t + 1], None, ALU.mult, r=[key_h, 'rs2'], w=['hb2'])
                for k in range(8):
                    b.tr(pst[:, k, :], hb2[:, t, k * 128:(k + 1) * 128], identb[:], r=['hb2', 'identb'], w=['ps0'])
                b.cp('dve', hT2[:, :, t * 128:(t + 1) * 128], pst, r=['ps0'], w=['hT2'])

        def ffn_up(ntok, first_halo):
            T.tag = 'ffn'
            for pi in range(11):
                wv_g, kg = wload(wup_b, 8, pi * 256, 256)
                wv_u, ku = wload(wup_b, 8, DFF + pi * 256, 256)
                for cc in range(2):
                    j = pi * 2 + cc
                    q_ = j % 2
                    ug, uu, cg, cu, yT = ugL[q_], uuL[q_], cgL[q_], cuL[q_], yTL[q_]
                    for half, (wv, kk, dst, pb) in enumerate(((wv_g, kg, ug, 1), (wv_u, ku, uu, 2))):
                        jj = j + 22 * half
                        for k in range(8):
                            b.mm(PS[pb][:, 0:ntok], wv[:, k, cc * 128:(cc + 1) * 128], hT2[:, k, 0:ntok], start=(k == 0), stop=(k == 7),
                                 r=[kk, 'hT2'], w=['ps%d' % pb])
                        dk = ('ug%d' if half == 0 else 'uu%d') % q_
                        b.cp('pool', dst[:, 0:2], uprev[:, jj, :], r=['uprev'], w=[dk])
                        b.cp('dve', dst[:, 2:2 + ntok], PS[pb][:, 0:ntok], r=['ps%d' % pb], w=[dk])
                        b.cp('pool', uprev[:, jj, :], dst[:, ntok:ntok + 2], r=[dk], w=['uprev'])
                        if first_halo:
                            continue
                        co = cg if half == 0 else cu
                        ck = ('cg%d' if half == 0 else 'cu%d') % q_
                        b.ts('pool', co[:, 0:ntok], dst[:, 0:ntok], fcws[:, jj, 0:1], fcbs[:, jj:jj + 1], ALU.mult, ALU.add,
                             r=[dk, 'fcws', 'fcbs'], w=[ck])
                        b.stt('dve', co[:, 0:ntok], dst[:, 1:1 + ntok], fcws[:, jj, 1:2], co[:, 0:ntok], ALU.mult, ALU.add,
                              r=[dk, 'fcws', ck], w=[ck])
                        b.stt('dve', co[:, 0:ntok], dst[:, 2:2 + ntok], fcws[:, jj, 2:3], co[:, 0:ntok], ALU.mult, ALU.add,
                              r=[dk, 'fcws', ck], w=[ck])
                    if first_halo:
                        continue
                    b.act(yT[:, 0:ntok], cg[:, 0:ntok], AF.Exp, scale=-1.0, r=['cg%d' % q_], w=['yT%d' % q_])
                    b.ts('dve', yT[:, 0:ntok], yT[:, 0:ntok], 1.0, None, ALU.add, r=['yT%d' % q_], w=['yT%d' % q_])
                    b.recip(yT[:, 0:ntok], yT[:, 0:ntok], r=['yT%d' % q_], w=['yT%d' % q_])
                    b.tt('pool', cg[:, 0:ntok], cg[:, 0:ntok], cu[:, 0:ntok], ALU.mult, r=['cg%d' % q_, 'cu%d' % q_], w=['cg%d' % q_])
                    b.tt('dve', aT[:, j, 0:ntok], cg[:, 0:ntok], yT[:, 0:ntok], ALU.mult, r=['cg%d' % q_, 'yT%d' % q_], w=['aT'])

        def mixer_tail(r0, ntok, nblk):
            T.tag = 'mix'
            np_ = ntok if ntok < 128 else 128
            b.dma(h2[0:np_, 0:nblk, :], x2[r0:r0 + ntok, :].rearrange("(t p) d -> p t d", p=np_), w=['h2'])
            norm_T(nblk, ntok, None, 'h2')
            tix = r0 // TW
            for c in range(8):
                col = tix * 8 + c
                T.op('pool', lambda e, c=c, col=col: e.indirect_dma_start(
                    out=oT2[:, c, 0:TW], out_offset=None, in_=gath[:, :],
                    in_offset=bass.IndirectOffsetOnAxis(ap=idxs[:, col:col + 1], axis=0)),
                    r=['gath', 'idxs'], w=['oT2'], dma=True, cost=1500.0, lat=3000.0)
            for pi in range(4):
                wv, kk = wload(wg_b, 8, pi * 512, 512)
                for cc in range(4):
                    j = pi * 4 + cc
                    pb = 1 + j % 2
                    for k in range(8):
                        b.mm(PS[pb][:, 0:ntok], wv[:, k, cc * 128:(cc + 1) * 128], hT2[:, k, 0:ntok], start=(k == 0), stop=(k == 7),
                             r=[kk, 'hT2'], w=['ps%d' % pb])
                    b.act(yT[:, 0:ntok], PS[pb][:, 0:ntok], AF.Exp, bias=gbs[:, j:j + 1], scale=-1.0,
                          r=['ps%d' % pb, 'gbs'], w=['yT0'])
                    b.ts('dve', yT[:, 0:ntok], yT[:, 0:ntok], 1.0, None, ALU.add, r=['yT0'], w=['yT0'])
                    b.recip(gT[:, j, 0:ntok], yT[:, 0:ntok], r=['yT0'], w=['gT'])
            wf_, kf = wload(wbf_b, 4, 0, 1024)
            wg_, kg_ = wload(wbg_b, 4, 0, 1024)
            for j in range(8):
                for k in range(4):
                    b.mm(PS[1][:, 0:ntok], wf_[:, k, j * 128:(j + 1) * 128], oT2[:, k, 0:ntok], start=(k == 0), stop=(k == 3),
                         r=[kf, 'oT2'], w=['ps1'])
                for k in range(4):
                    b.mm(PS[2][:, 0:ntok], wg_[:, k, j * 128:(j + 1) * 128], oT2[:, 4 + k, 0:ntok], start=(k == 0), stop=(k == 3),
                         r=[kg_, 'oT2'], w=['ps2'])
                b.tt('dve', yT[:, 0:ntok], PS[1][:, 0:ntok], gT[:, j, 0:ntok], ALU.mult, r=['ps1', 'gT'], w=['yT0'])
                b.tt('dve', cg[:, 0:ntok], PS[2][:, 0:ntok], gT[:, 8 + j, 0:ntok], ALU.mult, r=['ps2', 'gT'], w=['cg0'])
                b.tt('pool', mT[:, j, 0:ntok], yT[:, 0:ntok], cg[:, 0:ntok], ALU.add, r=['yT0', 'cg0'], w=['mT'])
            for hc in range(2):
                wv, kk = wload(wo_b, 8, hc * 512, 512)
                for t in range(nblk):
                    nt_ = min(128, ntok - t * 128)
                    pb = 1 + t % 2
                    for k in range(8):
                        b.mm(PS[pb][0:nt_, :], mT[:, k, t * 128:t * 128 + nt_], wv[:, k, :], start=(k == 0), stop=(k == 7),
                             r=[kk, 'mT'], w=['ps%d' % pb])
                    b.tt('dve', h2[0:nt_, t, hc * 512:(hc + 1) * 512], h2[0:nt_, t, hc * 512:(hc + 1) * 512], PS[pb][0:nt_, :], ALU.add,
                         r=['ps%d' % pb, 'h2'], w=['h2'])

        b.ts('dve', gbs[:], gbs[:], -1.0, None, ALU.mult, r=['gbs'], w=['gbs'])

        nc._p2_free = nc.sbuf_bytes_remaining
        p2tiles = [(it * TW, TW) for it in range(NT2)] + [(Q, 16)]
        for r0, ntok in p2tiles:
            nblk = (ntok + 127) // 128
            mixer_tail(r0, ntok, nblk)
            norm_T(nblk, ntok, None, 'h2')
            ffn_up(ntok, False)
            T.tag = 'down'
            for hc in range(4):
                wv, kk = wload(wdn_b, 22, hc * 256, 256)
                for t in range(nblk):
                    nt_ = min(128, ntok - t * 128)
                    pb = 1 + t % 2
                    for k in range(22):
                        b.mm(PS[pb][0:nt_, 0:256], aT[:, k, t * 128:t * 128 + nt_], wv[:, k, :], start=(k == 0), stop=(k == 21),
                             r=[kk, 'aT'], w=['ps%d' % pb])
                    b.tt('dve', h2[0:nt_, t, hc * 256:(hc + 1) * 256], h2[0:nt_, t, hc * 256:(hc + 1) * 256], PS[pb][0:nt_, 0:256], ALU.add,
                         r=['ps%d' % pb, 'h2'], w=['h2'])
            b.memset('pool', ss2[:], 0.0, w=['ss2'])
            for t in range(nblk):
                b.act(hb2[:, t, :], h2[:, t, :], AF.Square, accum_out=ss2[:, t:t + 1], r=['h2'], w=['hb2', 'ss2'])
            b.act(rs2[:, 0:nblk], ss2[:, 0:nblk], AF.Ln, bias=EPS, scale=1.0 / D, r=['ss2'], w=['rs2'])
            b.act(rs2[:, 0:nblk], rs2[:, 0:nblk], AF.Exp, scale=-0.5, r=['rs2'], w=['rs2'])
            for t in range(nblk):
                b.stt('dve', h2[:, t, :], h2[:, t, :], rs2[:, t:t + 1], gfin[:], ALU.mult, ALU.mult, r=['h2', 'rs2', 'gfin'], w=['h2'])
            if r0 == 0:
                b.dma(out[0:112, :], h2[16:128, 0, :], r=['h2'], w=['out'])
                if nblk > 1:
                    b.dma(out[112:ntok - 16, :].rearrange("(t p) d -> p t d", p=128), h2[:, 1:nblk, :], r=['h2'], w=['out'])
            elif ntok == TW:
                b.dma(out[r0 - 16:r0 - 16 + TW, :].rearrange("(t p) d -> p t d", p=128), h2[:, 0:nblk, :], r=['h2'], w=['out'])
            else:
                b.dma(out[Q - 16:Q, :], h2[0:16, 0, :], r=['h2'], w=['out'])
        p2scope.close()

    fkeys = ['out'] if part != 1 else ['a2a_in']
    T.finalize(final_wait_keys=fkeys)
    with contextlib.ExitStack() as st:
        esem = {e: st.enter_context(nc.semaphore("s_" + e)) for e in ENGS}
        dsems = {rg: [st.enter_context(nc.semaphore("d%s%d" % (rg, i))) for i in range(NDMA_SEMS)] for rg in ('s', 'p')}
        ccsem = [st.enter_context(nc.semaphore("ccsem%d" % i)) for i in range(max(1, T.ncc))]
        st.enter_context(nc.allow_low_precision("bf16 matmul operands by design, fp32 accumulation"))
        block = st.enter_context(nc.Block())
        T.emit(esem, dsems, ccsem, block)
    outer.close()
    nc._stats = (T.makespan, T.busy, T.nops)
    return nc


def make_in_maps(inp, SEQ, fused=True):
    L, NB, LP, Q, QH = cfg(SEQ)
    f = np.float32
    x = np.asarray(inp["x"], f)
    meta = np.asarray(inp["meta_tokens"], f)
    xf = np.zeros((2, LP, D), f)
    xf[:, 0:NMETA] = meta[None]
    xf[:, NMETA:L] = x
    w_in = np.asarray(inp["w_in"], f)[0]
    conv_w = np.asarray(inp["gdn_conv_w"], f)[0]
    gm = np.asarray(inp["norm_mix_w"], f)[0]
    gf = np.asarray(inp["norm_ffn_w"], f)[0]
    shared = dict(
        g_mix=np.ascontiguousarray(gm.reshape(8, 128).T),
        g_ffn=np.ascontiguousarray(gf.reshape(8, 128).T),
        g_fin=np.ascontiguousarray(np.broadcast_to(np.asarray(inp["norm_final_w"], f)[None, :], (128, D))),
        wg=np.ascontiguousarray(w_in[:, 3608:3608 + 2048]),
        gbias=np.ascontiguousarray(np.asarray(inp["gate_bias"], f)[0].reshape(16, 128).T),
        wbf=np.ascontiguousarray(np.asarray(inp["w_branch_fox"], f)[0]),
        wbg=np.ascontiguousarray(np.asarray(inp["w_branch_gdn"], f)[0]),
        wo=np.ascontiguousarray(np.asarray(inp["w_out"], f)[0]),
        wup=np.ascontiguousarray(np.asarray(inp["ffn_w_up"], f)[0]),
        fcw=np.ascontiguousarray(np.asarray(inp["ffn_conv_w"], f)[0].T.reshape(44, 128, 3).transpose(1, 0, 2)),
        fcb=np.ascontiguousarray(np.asarray(inp["ffn_conv_b"], f)[0].reshape(44, 128).T),
        wdn=np.ascontiguousarray(np.asarray(inp["ffn_w_down"], f)[0]),
        gnw=np.ascontiguousarray(np.broadcast_to(np.asarray(inp["gdn_norm_w"], f)[0][None, :], (64, 64))),
        xf=xf,
    )
    maps = []
    for c in range(8):
        h = c
        fq = w_in[:, h * 64:(h + 1) * 64]
        fk = w_in[:, 512 + h * 64:512 + (h + 1) * 64]
        fv = w_in[:, 1024 + h * 64:1024 + (h + 1) * 64]
        ff = w_in[:, 1536 + h:1537 + h]
        g0 = 1544
        gq = w_in[:, g0 + h * 64:g0 + (h + 1) * 64]
        gk = w_in[:, g0 + 512 + h * 64:g0 + 512 + (h + 1) * 64]
        gv = w_in[:, g0 + 1024 + h * 64:g0 + 1024 + (h + 1) * 64]
        z = w_in[:, 3080 + h * 64:3080 + (h + 1) * 64]
        bcol = w_in[:, 3592 + h:3593 + h]
        acol = w_in[:, 3600 + h:3601 + h]
        w1f = np.ascontiguousarray(np.concatenate([fq, fq, fk, fk, gq, gk, gv], axis=1))
        w1t = np.ascontiguousarray(np.concatenate([fv, ff, z, bcol, acol], axis=1))
        pp = np.zeros((128, 8), f)
        pp[:, 0] = np.asarray(inp["fgt_bias"], f)[0, h]
        pp[:, 1] = np.asarray(inp["gdn_a_log"], f)[0, h]
        pp[:, 2] = np.asarray(inp["gdn_dt_bias"], f)[0, h]
        cwm = np.zeros((64, 12), f)
        for g in range(3):
            cwm[:, g * 4:(g + 1) * 4] = conv_w[:, g * 512 + h * 64:g * 512 + (h + 1) * 64].T
        bb, j = divmod(c, 4)
        x2 = np.ascontiguousarray(xf[bb, j * Q:j * Q + QH])
        TW = min(512, Q)
        NTT = Q // TW + 1
        idx = np.zeros((128, 8 * NTT), np.int32)
        pa = np.arange(128)
        for it in range(NTT):
            for ch in range(8):
                half, cc = divmod(ch, 4)
                src = 2 * cc + pa // 64
                blk = c * NTT + it
                if fused:
                    base = np.array([gath_row(int(s_), blk, 8 * NTT, ag_chunk_blocks(TW)) for s_ in src])
                else:
                    base = (src * NTT + it) * 128
                idx[:, it * 8 + ch] = base + half * 64 + pa % 64
        m = dict(shared)
        m.update(w1f=w1f, w1t=w1t, pp=pp, convw=cwm, x2=x2, idx=idx)
        maps.append(m)
    return maps


_CACHE = {}


FUSED = False
P1KEYS = ["xf", "w1f", "w1t", "pp", "convw", "gnw", "g_mix"]
P2KEYS = ["x2", "g_mix", "g_ffn", "g_fin", "wg", "gbias", "wbf", "wbg", "wo", "wup", "fcw", "fcb", "wdn", "idx"]


def run(inp, SEQ, fused=None):
    fused = FUSED if fused is None else fused
    L, NB, LP, Q, QH = cfg(SEQ)
    TW = min(512, Q)
    NTT = Q // TW + 1
    maps = make_in_maps(inp, SEQ, fused)
    if fused:
        if (SEQ, 0) not in _CACHE:
            _CACHE[(SEQ, 0)] = build(SEQ, 0)
        res = run_bass_kernel_spmd(_CACHE[(SEQ, 0)], maps, core_ids=list(range(8)))
    else:
        for part in (1, 2):
            if (SEQ, part) not in _CACHE:
                _CACHE[(SEQ, part)] = build(SEQ, part)
        res1 = run_bass_kernel_spmd(_CACHE[(SEQ, 1)], [{k: m[k] for k in P1KEYS} for m in maps], core_ids=list(range(8)))
        sh = [np.asarray(res1.results[c]["a2a_in"]) for c in range(8)]
        maps2 = []
        for c in range(8):
            m = {k: maps[c][k] for k in P2KEYS}
            m["gath"] = np.ascontiguousarray(np.concatenate(
                [sh[s_][c * NTT * 128:(c + 1) * NTT * 128] for s_ in range(8)], axis=0))
            maps2.append(m)
        res = run_bass_kernel_spmd(_CACHE[(SEQ, 2)], maps2, core_ids=list(range(8)))
    out = np.zeros((2, SEQ, D), np.float32)
    for c in range(8):
        bb, j = divmod(c, 4)
        out[bb, j * Q:(j + 1) * Q] = res.results[c]["out"]
    return out, res


def kernel(**inputs):
    SEQ = inputs["x"].shape[1]
    out, _ = run(inputs, SEQ)
    return out
```

```python
import contextlib
import numpy as np
import concourse.bass as bass
import concourse.mybir as mybir
from concourse.bass_utils import run_bass_kernel_spmd

F32 = mybir.dt.float32
BF16 = mybir.dt.bfloat16
AF = mybir.ActivationFunctionType
ALU = mybir.AluOpType
AX = mybir.AxisListType

D = 1024
NMETA = 16
DFF = 2816
EPS = 1e-6
SELF_SYNC = True
NDMA_SEMS = 24
FILL_EVERY = 10 ** 9
FILL_N = 512
HOP_NS = 380.0
ENGS = ['pe', 'act', 'dve', 'pool', 'sp']


class Tracker:
    def __init__(self):
        self.ops = []
        self.lastw = {}
        self.readers = {}
        self.ndma = {'s': 0, 'p': 0}
        self.bar = set()
        self.ncc = 0
        self.tag = ''

    def op(self, eng, fn, r=(), w=(), dma=False, cc=False, cost=100.0, lat=0.0):
        i = len(self.ops)
        w = list(w) + [k for k in r if isinstance(k, str) and k.startswith('ps') and k[2:].isdigit()]
        deps = set(self.bar)
        for k in r:
            j = self.lastw.get(k)
            if j is not None:
                deps.add(j)
        for k in w:
            j = self.lastw.get(k)
            if j is not None:
                deps.add(j)
            deps.update(self.readers.get(k, ()))
        d = None
        if dma:
            ring = 'p' if eng == 'pool' else 's'
            d = (ring, -1)
        ccid = None
        if cc:
            ccid = self.ncc
            self.ncc += 1
        self.ops.append(dict(eng=eng, fn=fn, deps=deps, dma=d, users=False, cc=cc, ccid=ccid, cost=cost, lat=lat, bar=False, tag=self.tag))
        for k in r:
            self.readers.setdefault(k, []).append(i)
        for k in w:
            self.lastw[k] = i
            self.readers[k] = []
        return i

    def barrier(self, fn):
        self.bar = set()
        i = self.op('pool', fn, (), ())
        self.ops[i]['bar'] = True
        self.bar = {i}
        return i

    def finalize(self, final_wait_keys=()):
        import heapq
        ops = self.ops
        n = len(ops)
        fin = set()
        for k in final_wait_keys:
            j = self.lastw.get(k)
            if j is not None:
                fin.add(j)
        succ = [[] for _ in range(n)]
        npend = [0] * n
        for i, o in enumerate(ops):
            dd = range(i) if o['bar'] else o['deps']
            npend[i] = len(dd)
            for j in dd:
                succ[j].append(i)
        ready = [0.0] * n
        start = [0.0] * n
        t_e = {e: 0.0 for e in ENGS}
        fut = {e: [] for e in ENGS}
        avail = {e: [] for e in ENGS}
        for i, o in enumerate(ops):
            if npend[i] == 0:
                heapq.heappush(fut[o['eng']], (0.0, i))
        done = 0
        last_on = {}
        t_prev_end = {}
        rby = [None] * n
        while done < n:
            best = None
            for e in ENGS:
                f, a_ = fut[e], avail[e]
                while f and f[0][0] <= t_e[e]:
                    heapq.heappush(a_, heapq.heappop(f)[1])
                if a_:
                    cand = (t_e[e], a_[0], e, True)
                elif f:
                    cand = (f[0][0], f[0][1], e, False)
                else:
                    continue
                if best is None or cand[:2] < best[:2]:
                    best = cand
            st, i, e, from_avail = best
            if from_avail:
                heapq.heappop(avail[e])
            else:
                heapq.heappop(fut[e])
            o = ops[i]
            start[i] = st
            o['by_eng'] = (st <= t_prev_end.get(e, 0.0) + 1e-9 and e in last_on) and last_on.get(e)
            last_on[e] = i
            t_prev_end[e] = st + o['cost']
            t_e[e] = st + o['cost']
            fi = st + o['cost'] + o['lat']
            for s_ in succ[i]:
                os_ = ops[s_]
                rt = fi + (60.0 if (os_['eng'] == e and o['dma'] is None and not o['cc']) else HOP_NS)
                if rt > ready[s_]:
                    ready[s_] = rt
                    rby[s_] = i
                npend[s_] -= 1
                if npend[s_] == 0:
                    heapq.heappush(fut[os_['eng']], (ready[s_], s_))
            done += 1
        self.makespan = max(t_e.values())
        self.start = start
        self.rby = rby
        self.busy = {e: sum(o['cost'] for o in ops if o['eng'] == e) for e in ENGS}
        self.nops = {e: sum(1 for o in ops if o['eng'] == e) for e in ENGS}
        self.per_eng = {e: [] for e in ENGS}
        for i in sorted(range(n), key=lambda i: (start[i], i)):
            self.per_eng[ops[i]['eng']].append(i)
        pos = {}
        for e in ENGS:
            for p_, i in enumerate(self.per_eng[e]):
                pos[i] = p_
        for i, o in enumerate(ops):
            if not o['bar']:
                continue
            red = set()
            for e in ENGS:
                pre = [j for j in self.per_eng[e] if j < i and ops[j]['dma'] is None and not ops[j]['cc']]
                if pre:
                    red.add(pre[-1])
                for ring in ('s', 'p'):
                    pd = [j for j in self.per_eng[e] if j < i and ops[j]['dma'] is not None and ops[j]['dma'][0] == ring]
                    red.update(pd[-NDMA_SEMS:])
            red.update(j for j in range(i) if ops[j]['cc'])
            o['deps'] = red
        for ring, e in (('s', 'sp'), ('p', 'pool')):
            dma_ops = [i for i in self.per_eng[e] if ops[i]['dma'] is not None]
            assert all(ops[i]['dma'][0] == ring for i in dma_ops)
            for n_, i in enumerate(dma_ops):
                ops[i]['dma'] = (ring, n_)
                if n_ >= NDMA_SEMS:
                    ops[i]['deps'].add(dma_ops[n_ - NDMA_SEMS])
        for e in ENGS:
            assert e in ('sp', 'pool') or all(ops[i]['dma'] is None for i in self.per_eng[e])
        for i, o in enumerate(ops):
            for j in o['deps']:
                dj = ops[j]
                if dj['dma'] is not None or dj['cc']:
                    continue
                if dj['eng'] == o['eng'] and o['dma'] is None and not o['cc']:
                    if o['eng'] == 'pe' or not SELF_SYNC:
                        continue
                dj['users'] = True
        for j in fin:
            if ops[j]['dma'] is None and not ops[j]['cc']:
                ops[j]['users'] = True
        for e in ENGS:
            c_ = 0
            for i in self.per_eng[e]:
                o = ops[i]
                if o['dma'] is None and not o['cc'] and o['users']:
                    c_ += 1
                    o['val'] = c_
        self.fin = fin

    def emit(self, esem, dsems, ccsem, block):
        ops = self.ops

        def target(j):
            dj = ops[j]
            if dj['cc']:
                return ('c', dj['ccid']), ccsem[dj['ccid']], 1
            if dj['dma'] is not None:
                ring, d = dj['dma']
                return ('d', ring, d % NDMA_SEMS), dsems[ring][d % NDMA_SEMS], 16 * (d // NDMA_SEMS + 1)
            return ('e', dj['eng']), esem[dj['eng']], dj['val']

        def run(engname, eng):
            known = {}
            for i in self.per_eng[engname]:
                o = ops[i]
                waits = {}
                for j in o['deps']:
                    dj = ops[j]
                    if (dj['dma'] is None and not dj['cc'] and dj['eng'] == engname
                            and o['dma'] is None and not o['cc']):
                        if engname == 'pe' or not SELF_SYNC:
                            continue
                    key, sem, val = target(j)
                    if known.get(key, 0) >= val:
                        continue
                    if key not in waits or waits[key][1] < val:
                        waits[key] = (sem, val)
                for key, (sem, val) in waits.items():
                    eng.wait_ge(sem, val)
                    known[key] = val
                ins = o['fn'](eng)
                if o['cc']:
                    ins.then_inc(ccsem[o['ccid']])
                elif o['dma'] is not None:
                    ins.then_inc(dsems[o['dma'][0]][o['dma'][1] % NDMA_SEMS], 16)
                elif o['users']:
                    ins.then_inc(esem[engname], 1)
            if engname == 'sp':
                for j in self.fin:
                    key, sem, val = target(j)
                    if known.get(key, 0) >= val:
                        continue
                    eng.wait_ge(sem, val)
                    known[key] = val

        block.tensor(lambda e: run('pe', e))
        block.scalar(lambda e: run('act', e))
        block.vector(lambda e: run('dve', e))
        block.gpsimd(lambda e: run('pool', e))
        block.sync(lambda e: run('sp', e))


class B:
    def __init__(self, nc):
        self.nc = nc
        self.T = Tracker()
        self.off = 0
        self.n = 0
        self.scope = None

    def sb(self, shape, dt):
        t = self.scope.enter_context(self.nc.sbuf_tensor("sb%d" % self.n, list(shape), dt))
        self.n += 1
        return t

    @staticmethod
    def _n(ap):
        sh = ap.shape
        n = 1
        for x in sh[1:]:
            n *= int(x)
        return n

    def _c(self, eng, out, in_=None):
        n = self._n(out)
        if eng == 'act':
            return 155.0 + n * 0.835
        if eng == 'dve':
            f = 0.96
            return 60.0 + n / f
        if eng == 'pool':
            return 100.0 + n / 0.6
        return 100.0

    def mm(self, out, lhsT, rhs, start=True, stop=True, r=(), w=()):
        n = self._n(rhs)
        m = self._n(lhsT)
        f4 = 4.0 if rhs.dtype == F32 else 1.0
        c = 10.0 + m / 1.2 * (2.0 if f4 > 1 else 1.0) + (n / 2.4) * f4
        return self.T.op('pe', lambda e: e.matmul(out, lhsT=lhsT, rhs=rhs, start=start, stop=stop), r, w, cost=c, lat=0.0)

    def tr(self, out, in_, ident, r=(), w=()):
        return self.T.op('pe', lambda e: e.transpose(out=out, in_=in_, identity=ident), r, w, cost=10.0 + 128 / 1.2 + self._n(in_) / 2.4, lat=0.0)

    def act(self, out, in_, func, bias=0.0, scale=1.0, accum_out=None, r=(), w=()):
        c = self._c('act', out)
        if accum_out is None:
            return self.T.op('act', lambda e: e.activation(out=out, in_=in_, func=func, bias=bias, scale=scale), r, w, cost=c)
        return self.T.op('act', lambda e: e.activation(out=out, in_=in_, func=func, bias=bias, scale=scale,
                                                       accum_out=accum_out), r, w, cost=c)

    def ts(self, eng, out, in0, s1, s2, op0, op1=None, r=(), w=()):
        c = self._c(eng, out)
        if op1 is None:
            return self.T.op(eng, lambda e: e.tensor_scalar(out=out, in0=in0, scalar1=s1, scalar2=None, op0=op0), r, w, cost=c)
        return self.T.op(eng, lambda e: e.tensor_scalar(out=out, in0=in0, scalar1=s1, scalar2=s2, op0=op0, op1=op1), r, w, cost=c)

    def tt(self, eng, out, in0, in1, op, r=(), w=()):
        return self.T.op(eng, lambda e: e.tensor_tensor(out=out, in0=in0, in1=in1, op=op), r, w, cost=self._c(eng, out))

    def stt(self, eng, out, in0, scalar, in1, op0, op1, r=(), w=()):
        return self.T.op(eng, lambda e: e.scalar_tensor_tensor(out=out, in0=in0, scalar=scalar, in1=in1,
                                                               op0=op0, op1=op1), r, w, cost=self._c(eng, out))

    def cp(self, eng, out, in_, r=(), w=()):
        return self.T.op(eng, lambda e: e.tensor_copy(out=out, in_=in_), r, w, cost=self._c(eng, out))

    def recip(self, out, in_, r=(), w=()):
        return self.T.op('dve', lambda e: e.reciprocal(out=out, in_=in_), r, w, cost=self._c('dve', out))

    def memset(self, eng, ap, val, w=()):
        return self.T.op(eng, lambda e: e.memset(ap, val), (), w, cost=self._c(eng, ap))

    def asel(self, out, in_, pattern, cmp, fill, base, cm, r=(), w=()):
        return self.T.op('pool', lambda e: e.affine_select(out=out, in_=in_, pattern=pattern, compare_op=cmp,
                                                           fill=fill, base=base, channel_multiplier=cm), r, w, cost=self._c('pool', out))

    def rsum(self, eng, out, in_, r=(), w=()):
        return self.T.op(eng, lambda e: e.reduce_sum(out=out, in_=in_, axis=AX.X), r, w, cost=self._c(eng, in_))

    def dma(self, out, in_, r=(), w=(), eng='sp'):
        sh = out.shape
        nbytes = 1
        for x in sh:
            nbytes *= int(x)
        nbytes *= 4 if out.dtype == F32 else 2
        return self.T.op(eng, lambda e: e.dma_start(out=out, in_=in_), r, w, dma=True, cost=60.0, lat=2000.0 + nbytes / 60.0)


def bc3(ap2, n):
    return ap2.unsqueeze(2).to_broadcast([ap2.shape[0], ap2.shape[1], n])


def bcm(ap2, a):
    return ap2.unsqueeze(1).to_broadcast([ap2.shape[0], a, ap2.shape[1]])


def ag_chunk_blocks(TW):
    import os
    return max(1, (int(os.environ.get("AGKB", "768")) * 1024) // (128 * TW * 2))


def gath_row(src, blk, NBLK, CH):
    k, w = divmod(blk, CH)
    nbk = min(CH, NBLK - k * CH)
    return 8 * 128 * CH * k + src * (nbk * 128) + w * 128


def cfg(SEQ):
    L = SEQ + NMETA
    NB = (L + 127) // 128
    LP = NB * 128
    Q = SEQ // 4
    QH = Q + 16
    return L, NB, LP, Q, QH


NF = 128 + 128 + 64 * 3
NT = 65 + 64 + 2
UP_PIECES = [(i * 512, 512) for i in range(11)]


def build(SEQ, part=0, debug=False):
    L, NB, LP, Q, QH = cfg(SEQ)
    TW = min(512, Q)
    NT2 = Q // TW
    NTT = NT2 + 1
    assert Q % 128 == 0 and Q % TW == 0
    nc = bass.Bass("TRN2", target_bir_lowering=False)

    P1IN = {"xf", "w1f", "w1t", "pp", "convw", "gnw", "g_mix"}
    P2IN = {"x2", "g_mix", "g_ffn", "g_fin", "wg", "gbias", "wbf", "wbg", "wo", "wup", "fcw", "fcb", "wdn"}

    def din(name, shape):
        if part == 1 and name not in P1IN:
            return None
        if part == 2 and name not in P2IN:
            return None
        return nc.dram_tensor(name, list(shape), F32, kind="ExternalInput").ap()

    xf = din("xf", [2, LP, D])
    x2 = din("x2", [QH, D])
    w1f = din("w1f", [D, NF])
    w1t = din("w1t", [D, NT])
    pp = din("pp", [128, 8])
    convw = din("convw", [64, 12])
    gnw = din("gnw", [64, 64])
    g_mix = din("g_mix", [128, 8])
    g_ffn = din("g_ffn", [128, 8])
    g_fin = din("g_fin", [128, D])
    wg = din("wg", [D, 2 * D])
    gbias = din("gbias", [128, 16])
    wbf = din("wbf", [512, D])
    wbg = din("wbg", [512, D])
    wo = din("wo", [D, D])
    wup = din("wup", [D, 2 * DFF])
    fcw = din("fcw", [128, 44, 3])
    fcb = din("fcb", [128, 44])
    wdn = din("wdn", [DFF, D])
    if part != 1:
        out = nc.dram_tensor("out", [Q, D], F32, kind="ExternalOutput").ap()
        idx = nc.dram_tensor("idx", [128, 8 * NTT], mybir.dt.int32, kind="ExternalInput").ap()

    if part == 1:
        a2a_in = nc.dram_tensor("a2a_in", [8 * NTT * 128, TW], BF16, kind="ExternalOutput").ap()
    elif part == 2:
        gath = nc.dram_tensor("gath", [8 * NTT * 128, TW], BF16, kind="ExternalInput").ap()
    else:
        a2a_in = nc.dram_tensor("a2a_in", [8 * NTT * 128, TW], BF16).ap()
        gath = nc.dram_tensor("gath", [8 * 8 * NTT * 128, TW], BF16).ap()
    wg_b = nc.dram_tensor("wg_b", [D, 2 * D], BF16).ap()
    wbf_b = nc.dram_tensor("wbf_b", [512, D], BF16).ap()
    wbg_b = nc.dram_tensor("wbg_b", [512, D], BF16).ap()
    wo_b = nc.dram_tensor("wo_b", [D, D], BF16).ap()
    wup_b = nc.dram_tensor("wup_b", [D, 2 * DFF], BF16).ap()
    wdn_b = nc.dram_tensor("wdn_b", [DFF, D], BF16).ap()

    b = B(nc)
    T = b.T
    outer = contextlib.ExitStack()
    b.scope = outer
    PS = [nc.alloc_psum_tensor("ps%d" % i, [128, 512], F32) for i in range(8)]
    PSB = [nc.alloc_psum_tensor("psb%d" % i, [128, 1024], BF16) for i in range(0)]

    identb = b.sb([128, 128], BF16)
    trium = b.sb([128, 128], BF16)
    U128 = b.sb([128, 128], F32)
    ones128 = b.sb([128, 128], F32)
    m_upi = b.sb([64, 8, 64], F32)
    m_lows = b.sb([64, 8, 64], F32)
    I8 = b.sb([64, 8, 64], BF16)
    tmpf = b.sb([128, 128], F32)
    ppt = b.sb([128, 8], F32)
    negfb = b.sb([128, 1], F32)
    negA = b.sb([128, 1], F32)
    gmix = b.sb([128, 8], F32)
    gffn = b.sb([128, 8], F32)
    bar_t = b.sb([128, 1], F32)
    CONST_END = b.off

    b.memset('pool', tmpf[:], 1.0, w=['tmpf'])
    b.asel(tmpf[:], tmpf[:], [[-1, 128]], ALU.is_equal, 0.0, 0, 1, r=['tmpf'], w=['tmpf'])
    b.cp('dve', identb[:], tmpf[:], r=['tmpf'], w=['identb'])
    b.memset('pool', ones128[:], 1.0, w=['ones128'])
    b.asel(U128[:], ones128[:], [[1, 128]], ALU.is_ge, 0.0, 0, -1, r=['ones128'], w=['U128'])
    b.cp('dve', trium[:], U128[:], r=['U128'], w=['trium'])
    b.memset('pool', m_upi[:], 1.0, w=['m_upi'])
    b.asel(m_upi[:], m_upi[:], [[0, 8], [1, 64]], ALU.is_ge, 0.0, 0, -1, r=['m_upi'], w=['m_upi'])
    b.memset('pool', m_lows[:], 1.0, w=['m_lows'])
    b.asel(m_lows[:], m_lows[:], [[0, 8], [-1, 64]], ALU.is_gt, 0.0, 0, 1, r=['m_lows'], w=['m_lows'])
    i8f = tmpf[0:64, :].rearrange("p (a c) -> p a c", a=2)
    b.memset('pool', tmpf[:], 1.0, w=['tmpf'])
    b.asel(i8f, i8f, [[0, 2], [-1, 64]], ALU.is_equal, 0.0, 0, 1, r=['tmpf'], w=['tmpf'])
    for a in range(4):
        b.cp('dve', I8[:, 2 * a:2 * a + 2, :], i8f, r=['tmpf'], w=['I8'])
    if part != 2:
        b.dma(ppt[:], pp[:, :], w=['ppt'])
    else:
        b.memset('pool', ppt[:], 0.0, w=['ppt'])
    if part != 1:
        b.dma(gffn[:], g_ffn[:, :], w=['gffn'])
    b.dma(gmix[:], g_mix[:, :], w=['gmix'])
    b.ts('dve', negfb[:], ppt[:, 0:1], -1.0, None, ALU.mult, r=['ppt'], w=['negfb'])
    b.act(negA[:], ppt[:, 1:2], AF.Exp, r=['ppt'], w=['negA'])
    b.ts('dve', negA[:], negA[:], -1.0, None, ALU.mult, r=['negA'], w=['negA'])

    P1_BASE = b.off
    NSTG = 6 if part == 2 else 2
    stg = [b.sb([128, 512], F32) for _ in range(NSTG)] if part != 1 else None
    stb = [b.sb([128, 512], BF16) for _ in range(NSTG)] if part != 1 else None
    cnt = [0]

    pieces = []

    def conv_piece(src, dst, r0, c0, nc_, gain):
        pieces.append((src, dst, r0, c0, nc_, gain))

    def conv_emit(src, dst, r0, c0, nc_, gain):
        T.tag = 'conv'
        s = cnt[0] % NSTG
        cnt[0] += 1
        b.dma(stg[s][:, 0:nc_], src[r0:r0 + 128, c0:c0 + nc_], w=['stg%d' % s])
        if gain is None:
            b.cp('pool', stb[s][:, 0:nc_], stg[s][:, 0:nc_], r=['stg%d' % s], w=['stb%d' % s])
        else:
            b.ts('pool', stb[s][:, 0:nc_], stg[s][:, 0:nc_], gain, None, ALU.mult,
                 r=['stg%d' % s, 'gmix', 'gffn'], w=['stb%d' % s])
        b.dma(dst[r0:r0 + 128, c0:c0 + nc_], stb[s][:, 0:nc_], r=['stb%d' % s], w=[('W', dst.tensor.name, r0 // 128, c0 // 512)])

    for k in range(8):
        for c0 in range(0, 2048, 512):
            conv_piece(wg, wg_b, k * 128, c0, 512, gmix[:, k:k + 1])
    for k in range(4):
        for c0 in (0, 512):
            conv_piece(wbf, wbf_b, k * 128, c0, 512, None)
            conv_piece(wbg, wbg_b, k * 128, c0, 512, None)
    for k in range(8):
        for c0 in (0, 512):
            conv_piece(wo, wo_b, k * 128, c0, 512, None)
    for c0 in range(0, 5632, 512):
        for k in range(8):
            conv_piece(wup, wup_b, k * 128, c0, 512, gffn[:, k:k + 1])
    for k in range(22):
        for c0 in (0, 512):
            conv_piece(wdn, wdn_b, k * 128, c0, 512, None)

    p1scope = contextlib.ExitStack()
    b.scope = p1scope
    if part != 2:
        b.off = P1_BASE + 2 * 4096 + 2 * 2048
        w1fs = b.sb([128, 8, NF], BF16)
        w1ts = b.sb([128, 8, NT], BF16)
        cw = b.sb([64, 12], F32)
        gnws = b.sb([64, 64], F32)
        KT = b.sb([128, LP], BF16)
        Vst = b.sb([128, 2, NB, 65], BF16)
        cneg = b.sb([128, 2, NB], F32)
        cend = b.sb([128, 2, NB], F32)
        carry = b.sb([128, 2], F32)
        hbuf = [b.sb([128, 4, D], F32) for _ in range(2)]
        hnb = b.sb([128, 4, D], BF16)
        hnT1 = b.sb([128, 8, 512], BF16)
        hnT = [hnT1, hnT1]
        QT = [b.sb([128, 512], BF16) for _ in range(2)]
        ssq = b.sb([128, 8], F32)
        rstd = b.sb([128, 8], F32)
        PT = [b.sb([128, 512], BF16) for _ in range(4)]
        biasq = [b.sb([128, NB], F32) for _ in range(2)]
        spt = b.sb([128, 4], F32)
        pre = b.sb([128, 4], F32)
        osb = b.sb([65, 512], F32)
        rdn = b.sb([64, 512], F32)
        sel65 = b.sb([65, 64], F32)
        fill_rhs = b.sb([128, 512], BF16)
        oTf = [b.sb([64, 512], BF16) for _ in range(2)]
        oTg = [b.sb([64, 512], BF16) for _ in range(2)]
        cin = [[b.sb([64, 3 + 512], F32) for g in range(3)] for bb in range(2)]
        Sst = [b.sb([64, 64], F32) for bb in range(2)]
        Sb = [b.sb([64, 64], BF16) for bb in range(2)]

        def g512(dt):
            return b.sb([64, 8, 64], dt)

        cy = [g512(F32) for _ in range(3)]
        ex = g512(F32)
        sq = g512(F32)
        qh = g512(BF16)
        kh = g512(BF16)
        vT = g512(BF16)
        vb = g512(BF16)
        kbg = g512(BF16)
        ktail = g512(BF16)
        Dm = g512(F32)
        E1 = g512(F32)
        E2 = g512(F32)
        A_ = [g512(BF16) for _ in range(2)]
        B_ = [g512(BF16) for _ in range(2)]
        IA = g512(BF16)
        R_ = [g512(BF16) for _ in range(2)]
        attnT = g512(BF16)
        val = g512(F32)
        kcdT = g512(BF16)
        og = g512(F32)
        ogb = g512(BF16)
        zsb = [g512(F32) for _ in range(2)]
        beta2 = [b.sb([64, 8], F32) for _ in range(2)]
        g2 = [b.sb([64, 8], F32) for _ in range(2)]
        qS = b.sb([64, 64], F32)
        vn = b.sb([64, 64], BF16)
        sc8 = {nm: b.sb([64, 8], F32) for nm in ['beta', 'g', 'gc', 'gtot', 'egc', 'etail', 'cd', 'bege', 'ss', 'rs', 'nb']}

        w1stage = hbuf[1][:].rearrange("p t d -> p (t d)")[:, 0:8 * NF].rearrange("p (k n) -> p k n", k=8)
        b.dma(w1stage, w1f.rearrange("(k p) n -> p k n", p=128), w=['h1'])
        for k in range(8):
            b.ts('dve', w1fs[:, k, :], w1stage[:, k, :], gmix[:, k:k + 1], None, ALU.mult, r=['h1', 'gmix'], w=['w1fs'])
        b.ts('dve', w1fs[:, :, 0:128], w1fs[:, :, 0:128], 0.125, None, ALU.mult, r=['w1fs'], w=['w1fs'])
        b.dma(w1stage[:, :, 0:NT], w1t.rearrange("(k p) n -> p k n", p=128), r=['w1fs'], w=['h1'])
        for k in range(8):
            b.ts('dve', w1ts[:, k, :], w1stage[:, k, 0:NT], gmix[:, k:k + 1], None, ALU.mult, r=['h1', 'gmix'], w=['w1ts'])
        b.dma(cw[:], convw[:, :], w=['cw'])
        b.dma(gnws[:], gnw[:, :], w=['gnws'])
        b.memset('pool', Vst[:], 1.0, w=['Vst_%d_%d' % (bb_, t_) for bb_ in range(2) for t_ in range(0, NB, 4)])
        b.memset('pool', carry[:], 0.0, w=['carry'])
        b.memset('pool', fill_rhs[:], 1.0, w=['fill_rhs'])
        b.memset('pool', sel65[:], 0.0, w=['sel65'])
        b.memset('pool', sel65[64:65, :], 1.0, w=['sel65'])
        for bb in range(2):
            for g in range(3):
                b.memset('pool', cin[bb][g][:, 0:3], 0.0, w=['cin%d%d' % (bb, g)])
            b.memset('pool', Sst[bb][:], 0.0, w=['S%d' % bb])
            b.memset('pool', Sb[bb][:], 0.0, w=['Sb%d' % bb])

        nc._dbg = dict(KT=KT, Vst=Vst, cneg=cneg, cend=cend, hnT=hnT1, QT0=QT[0], QT1=QT[1], w1fs=w1fs, w1ts=w1ts, hnb=hnb, rstd=rstd, oTf0=oTf[0], oTg0=oTg[0], kh=kh, qh=qh, vT=vT, og=og, val=val, A0=A_[0], E1=E1, E2=E2, R1=R_[1], beta=sc8['beta'], g=sc8['g'], gc=sc8['gc'], S0=Sst[0])
        nc._p1_free = nc.sbuf_bytes_remaining
        tiles = []
        for t0 in range(0, NB, 4):
            for bb in range(2):
                tiles.append((bb, t0, min(4, NB - t0)))

        psrot = [0]

        def stage_A(ti):
            T.tag = 'A'
            bb, t0, nblk = tiles[ti]
            s = ti % 2
            ntok = nblk * 128
            p0 = t0 * 128
            hk, hTk, qk = 'h%d' % s, 'hnT', 'QT%d' % s
            b.dma(hbuf[s][:, 0:nblk, :], xf[bb, p0:p0 + ntok, :].rearrange("(t p) d -> p t d", p=128), w=[hk])
            b.memset('pool', ssq[:], 0.0, w=['ssq'])
            for t in range(nblk):
                b.act(hnb[:, t, :], hbuf[s][:, t, :], AF.Square, accum_out=ssq[:, t:t + 1], r=[hk], w=['hnb', 'ssq'])
            b.act(rstd[:, 0:nblk], ssq[:, 0:nblk], AF.Ln, bias=EPS, scale=1.0 / D, r=['ssq'], w=['rstd'])
            b.act(rstd[:, 0:nblk], rstd[:, 0:nblk], AF.Exp, scale=-0.5, r=['rstd'], w=['rstd'])
            for t in range(nblk):
                b.ts('dve', hnb[:, t, :], hbuf[s][:, t, :], rstd[:, t:t + 1], None, ALU.mult, r=[hk, 'rstd'], w=['hnb'])
            pst = PS[0][:].bitcast(BF16).rearrange("p (k c) -> p k c", k=8)
            for t in range(nblk):
                for k in range(8):
                    b.tr(pst[:, k, :], hnb[:, t, k * 128:(k + 1) * 128], identb[:], r=['hnb', 'identb'], w=['ps0'])
                b.cp('dve', hnT[s][:, :, t * 128:(t + 1) * 128], pst, r=['ps0'], w=[hTk])
            groups = [(0, 128), (128, 128), (256, 64), (320, 64), (384, 64)]
            for gi, (c0, m) in enumerate(groups):
                pb = 1
                psrot[0] += 1
                pk = 'ps%d' % pb
                for k in range(8):
                    b.mm(PS[pb][0:m, 0:ntok], w1fs[:, k, c0:c0 + m], hnT[s][:, k, 0:ntok], start=(k == 0), stop=(k == 7),
                         r=['w1fs', hTk], w=[pk])
                lo = bb * 64
                if gi == 0:
                    b.cp('dve', QT[s][lo:lo + 64, 0:ntok], PS[pb][lo:lo + 64, 0:ntok], r=[pk], w=[qk])
                elif gi == 1:
                    b.cp('dve', KT[lo:lo + 64, p0:p0 + ntok], PS[pb][lo:lo + 64, 0:ntok], r=[pk], w=['KT_%d_%d' % (bb, t0)])
                else:
                    g = gi - 2
                    b.cp('dve', cin[bb][g][:, 3:3 + ntok], PS[pb][0:64, 0:ntok], r=[pk], w=['cin%d%d' % (bb, g)])
            psv = PS[0][:, 0:260].rearrange("p (t c) -> p t c", t=4)
            for t in range(nblk):
                for k in range(8):
                    b.mm(psv[:, t, :], hnT[s][:, k, t * 128:(t + 1) * 128], w1ts[:, k, 0:65], start=(k == 0), stop=(k == 7),
                         r=['w1ts', hTk], w=['ps0'])
            b.cp('dve', Vst[:, bb, t0:t0 + nblk, 0:64], psv[:, 0:nblk, 0:64], r=['ps0'], w=['Vst_%d_%d' % (bb, t0)])
            b.act(spt[:, 0:nblk], psv[:, 0:nblk, 64], AF.Exp, bias=negfb[:, 0:1], scale=-1.0, r=['ps0', 'negfb'], w=['spt'])
            b.act(spt[:, 0:nblk], spt[:, 0:nblk], AF.Ln, bias=1.0, r=['spt'], w=['spt'])
            b.mm(PS[0][:, 264:264 + nblk], U128[:], spt[:, 0:nblk], r=['U128', 'spt'], w=['ps0'])
            b.mm(PS[0][:, 272:272 + nblk], ones128[:], spt[:, 0:nblk], r=['ones128', 'spt'], w=['ps0'])
            for t in range(nblk):
                prev = carry[:, bb:bb + 1] if t == 0 else cend[:, bb, t0 + t - 1:t0 + t]
                b.tt('dve', cend[:, bb, t0 + t:t0 + t + 1], PS[0][:, 272 + t:273 + t], prev, ALU.add,
                     r=['ps0', 'carry', 'cend_%d_%d' % (bb, t0), 'cend_%d_%d' % (bb, max(t0 - 4, 0))], w=['cend_%d_%d' % (bb, t0)])
                b.tt('dve', cneg[:, bb, t0 + t:t0 + t + 1], PS[0][:, 264 + t:265 + t], prev, ALU.add,
                     r=['ps0', 'carry', 'cend_%d_%d' % (bb, t0), 'cend_%d_%d' % (bb, max(t0 - 4, 0))], w=['cneg_%d_%d' % (bb, t0)])
            b.cp('dve', carry[:, bb:bb + 1], cend[:, bb, t0 + nblk - 1:t0 + nblk], r=['cend_%d_%d' % (bb, t0)], w=['carry'])
            nch = nblk * 2
            for c4 in range(0, nch, 4):
                n4 = min(4, nch - c4)
                pz = PS[1][0:64, 0:264].rearrange("p (n c) -> p n c", n=4)
                for n in range(n4):
                    cn = c4 + n
                    for k in range(8):
                        b.mm(pz[:, n, :], hnT[s][:, k, cn * 64:(cn + 1) * 64], w1ts[:, k, 65:131], start=(k == 0), stop=(k == 7),
                             r=['w1ts', hTk], w=['ps1'])
                b.cp('dve', zsb[s][:, c4:c4 + n4, :], pz[:, 0:n4, 0:64], r=['ps1'], w=['zs%d' % s])
                b.act(beta2[s][:, c4:c4 + n4], pz[:, 0:n4, 64], AF.Exp, scale=-1.0, r=['ps1'], w=['beta%d' % s])
                b.act(g2[s][:, c4:c4 + n4], pz[:, 0:n4, 65], AF.Exp, bias=ppt[0:64, 2:3], r=['ps1', 'ppt'], w=['g%d' % s])

        def stage_B(ti):
            T.tag = 'B'
            bb, t0, nblk = tiles[ti]
            s = ti % 2
            qk = 'QT%d' % s
            lo = bb * 64
            nkb = t0 + nblk
            NQ = nblk * 128
            halves = [(h0, min(2, nblk - h0)) for h0 in range(0, nblk, 2)]
            for hi, (h0, hn_) in enumerate(halves):
                b.ts('dve', biasq[hi][:, 0:nkb], cneg[:, bb, 0:nkb], cend[:, bb, t0 + h0:t0 + h0 + 1], None, ALU.subtract,
                     r=['cneg_%d_%d' % (bb, t_) for t_ in range(0, nkb, 4)] + ['cend_%d_%d' % (bb, t0)], w=['biasq%d' % hi])
            psoT = PS[6][0:65, 0:NQ]
            for kb in range(nkb):
                qlo = max(0, kb - t0)
                sl = 4 + kb % 2
                ps_s = PS[sl][:, 0:512]
                pts = kb % 4
                c0 = qlo * 128
                b.mm(ps_s[:, c0:NQ], KT[lo:lo + 64, kb * 128:(kb + 1) * 128], QT[s][lo:lo + 64, c0:NQ],
                     r=['KT_%d_%d' % (bb, kb // 4 * 4), qk], w=['ps%d' % sl, 'fillpos'])
                if kb % FILL_EVERY == 0:
                    b.mm(PS[3][:, 0:FILL_N], identb[:], fill_rhs[:, 0:FILL_N], r=['identb', 'fill_rhs', 'fillpos'], w=['ps3'])
                for hi, (h0, hn_) in enumerate(halves):
                    a0 = max(c0, h0 * 128)
                    a1 = (h0 + hn_) * 128
                    if a0 >= a1:
                        continue
                    b.act(PT[pts][:, a0:a1], ps_s[:, a0:a1], AF.Exp, bias=biasq[hi][:, kb:kb + 1],
                          r=['ps%d' % sl, 'biasq%d' % hi], w=['PT%d' % pts])
                if qlo > 0:
                    b.memset('pool', PT[pts][:, 0:c0], 0.0, w=['PT%d' % pts])
                if kb >= t0:
                    j = kb - t0
                    b.tt('pool', PT[pts][:, j * 128:(j + 1) * 128], PT[pts][:, j * 128:(j + 1) * 128], trium[:], ALU.mult,
                         r=['PT%d' % pts, 'trium'], w=['PT%d' % pts])
                b.mm(psoT, Vst[:, bb, kb, :], PT[pts][:, 0:NQ], start=(kb == 0), stop=(kb == nkb - 1),
                     r=['PT%d' % pts, 'Vst_%d_%d' % (bb, kb // 4 * 4)], w=['ps6'])
            b.cp('dve', osb[:, 0:NQ], psoT, r=['ps6'], w=['osb'])
            b.mm(PS[6][0:64, 0:NQ], sel65[:, :], osb[:, 0:NQ], r=['osb', 'sel65'], w=['ps6'])
            b.recip(rdn[:, 0:NQ], PS[6][0:64, 0:NQ], r=['ps6'], w=['rdn'])
            b.tt('dve', oTf[s][:, 0:NQ], osb[0:64, 0:NQ], rdn[:, 0:NQ], ALU.mult, r=['osb', 'rdn'], w=['oTf%d' % s])

        def stage_C(ti):
            T.tag = 'C'
            bb, t0, nblk = tiles[ti]
            s = ti % 2
            ntok = nblk * 128
            nch = nblk * 2
            W = nch * 64

            def v3(t):
                return t[:, 0:nch, :]

            def v2(t):
                return t[:].rearrange("p n c -> p (n c)")[:, 0:W]

            for g in range(3):
                ck = 'cin%d%d' % (bb, g)
                b.ts('pool', v2(cy[g]), cin[bb][g][:, 0:W], cw[:, g * 4:g * 4 + 1], None, ALU.mult, r=[ck, 'cw'], w=['cy%d' % g])
                for j in range(1, 4):
                    b.stt('dve', v2(cy[g]), cin[bb][g][:, j:j + W], cw[:, g * 4 + j:g * 4 + j + 1], v2(cy[g]), ALU.mult, ALU.add,
                          r=[ck, 'cw', 'cy%d' % g], w=['cy%d' % g])
                b.cp('pool', cin[bb][g][:, 0:3], cin[bb][g][:, W:W + 3], r=[ck], w=[ck])
                b.act(v2(ex), v2(cy[g]), AF.Exp, scale=-1.0, r=['cy%d' % g], w=['ex'])
                b.ts('dve', v2(ex), v2(ex), 1.0, None, ALU.add, r=['ex'], w=['ex'])
                b.recip(v2(ex), v2(ex), r=['ex'], w=['ex'])
                if g == 2:
                    b.tt('dve', v2(vT), v2(cy[g]), v2(ex), ALU.mult, r=['ex', 'cy2'], w=['vT'])
                else:
                    b.tt('dve', v2(cy[g]), v2(cy[g]), v2(ex), ALU.mult, r=['ex', 'cy%d' % g], w=['cy%d' % g])
                    b.tt('pool', v2(sq), v2(cy[g]), v2(cy[g]), ALU.mult, r=['cy%d' % g], w=['sq'])
                    b.mm(PS[2][0:64, 0:W], ones128[0:64, 0:64], v2(sq), r=['sq', 'ones128'], w=['ps2'])
                    b.act(v2(ex), PS[2][0:64, 0:W], AF.Ln, bias=EPS, r=['ps2'], w=['ex'])
                    b.act(v2(ex), v2(ex), AF.Exp, scale=-0.5, r=['ex'], w=['ex'])
                    if g == 0:
                        b.stt('dve', v2(qh), v2(cy[g]), 0.125, v2(ex), ALU.mult, ALU.mult, r=['cy0', 'ex'], w=['qh'])
                    else:
                        b.tt('dve', v2(kh), v2(cy[g]), v2(ex), ALU.mult, r=['cy1', 'ex'], w=['kh'])
            be, gg = beta2[s], g2[s]
            b.ts('dve', be[:, 0:nch], be[:, 0:nch], 1.0, None, ALU.add, r=['beta%d' % s], w=['beta%d' % s])
            b.recip(be[:, 0:nch], be[:, 0:nch], r=['beta%d' % s], w=['beta%d' % s])
            b.act(gg[:, 0:nch], gg[:, 0:nch], AF.Ln, bias=1.0, r=['g%d' % s], w=['g%d' % s])
            b.ts('dve', gg[:, 0:nch], gg[:, 0:nch], negA[0:64, 0:1], None, ALU.mult, r=['g%d' % s, 'negA'], w=['g%d' % s])
            b.mm(PS[2][0:64, 0:nch], U128[0:64, 0:64], gg[:, 0:nch], r=['g%d' % s, 'U128'], w=['ps2'])
            b.mm(PS[2][0:64, 8:8 + nch], ones128[0:64, 0:64], gg[:, 0:nch], r=['g%d' % s, 'ones128'], w=['ps2'])
            b.cp('dve', sc8['gc'][:, 0:nch], PS[2][0:64, 0:nch], r=['ps2'], w=['gc'])
            b.cp('dve', sc8['gtot'][:, 0:nch], PS[2][0:64, 8:8 + nch], r=['ps2'], w=['gtot'])
            b.act(sc8['egc'][:, 0:nch], sc8['gc'][:, 0:nch], AF.Exp, r=['gc'], w=['egc'])
            b.act(sc8['cd'][:, 0:nch], sc8['gtot'][:, 0:nch], AF.Exp, r=['gtot'], w=['cd'])
            b.tt('dve', sc8['etail'][:, 0:nch], sc8['gtot'][:, 0:nch], sc8['gc'][:, 0:nch], ALU.subtract, r=['gtot', 'gc'], w=['etail'])
            b.act(sc8['etail'][:, 0:nch], sc8['etail'][:, 0:nch], AF.Exp, r=['etail'], w=['etail'])
            b.tt('dve', sc8['bege'][:, 0:nch], be[:, 0:nch], sc8['egc'][:, 0:nch], ALU.mult, r=['beta%d' % s, 'egc'], w=['bege'])
            b.ts('dve', sc8['nb'][:, 0:nch], be[:, 0:nch], -1.0, None, ALU.mult, r=['beta%d' % s], w=['nb'])
            b.tt('dve', v3(sq), bcm(U128[0:64, 0:64], nch), bc3(gg[:, 0:nch], 64), ALU.mult, r=['U128', 'g%d' % s], w=['sq'])
            b.mm(PS[2][0:64, 0:W], ones128[0:64, 0:64], v2(sq), r=['sq', 'ones128'], w=['ps2'])
            ps7v = PS[2][0:64, :].rearrange("p (n c) -> p n c", n=8)[:, 0:nch, :]
            b.tt('dve', v3(Dm), bc3(sc8['gc'][:, 0:nch], 64), ps7v, ALU.subtract, r=['gc', 'ps2'], w=['Dm'])
            b.ts('dve', v3(E1), v3(Dm), 0.0, None, ALU.min, r=['Dm'], w=['E1'])
            b.ts('dve', v3(E2), v3(Dm), 0.0, -1.0, ALU.max, ALU.mult, r=['Dm'], w=['E2'])
            b.act(v2(E1), v2(E1), AF.Exp, r=['E1'], w=['E1'])
            b.act(v2(E2), v2(E2), AF.Exp, r=['E2'], w=['E2'])
            b.tt('pool', v3(E1), v3(E1), m_lows[:, 0:nch, :], ALU.mult, r=['E1', 'm_lows'], w=['E1'])
            b.tt('pool', v3(E1), v3(E1), bc3(sc8['nb'][:, 0:nch], 64), ALU.mult, r=['E1', 'nb'], w=['E1'])
            b.tt('pool', v3(E2), v3(E2), m_upi[:, 0:nch, :], ALU.mult, r=['E2', 'm_upi'], w=['E2'])
            pT = PS[7][0:64, :].bitcast(BF16).rearrange("p (n c) -> p n c", n=8)
            for n in range(nch):
                b.tr(pT[:, n, 0:64], kh[:, n, :], identb[0:64, 0:64], r=['kh', 'identb', 'zs%d' % s], w=['ps7'])
                b.tr(pT[:, n, 64:128], vT[:, n, :], identb[0:64, 0:64], r=['vT', 'identb'], w=['ps7'])
            b.tt('dve', v3(kbg), pT[:, 0:nch, 0:64], bc3(sc8['bege'][:, 0:nch], 64), ALU.mult, r=['ps7', 'bege'], w=['kbg'])
            b.tt('dve', v3(ktail), pT[:, 0:nch, 0:64], bc3(sc8['etail'][:, 0:nch], 64), ALU.mult, r=['ps7', 'etail'], w=['ktail'])
            b.tt('dve', v3(vb), pT[:, 0:nch, 64:128], bc3(be[:, 0:nch], 64), ALU.mult, r=['ps7', 'beta%d' % s], w=['vb'])
            psG = PS[2][0:64, :].rearrange("p (n c) -> p n c", n=8)
            for n in range(nch):
                b.mm(psG[:, n, :], kh[:, n, :], kh[:, n, :], r=['kh'], w=['ps2'])
            b.tt('dve', v3(A_[0]), psG[:, 0:nch, :], v3(E1), ALU.mult, r=['ps2', 'E1'], w=['A0'])
            for n in range(nch):
                b.mm(psG[:, n, :], kh[:, n, :], qh[:, n, :], r=['kh', 'qh'], w=['ps2'])
            b.tt('dve', v3(attnT), psG[:, 0:nch, :], v3(E2), ALU.mult, r=['ps2', 'E2'], w=['attnT'])
            pB = PS[7][0:64, 0:256].bitcast(BF16).rearrange("p (n c) -> p n c", n=8)
            for n in range(nch):
                b.tr(pB[:, n, :], A_[0][:, n, :], identb[0:64, 0:64], r=['A0', 'identb', 'kbg', 'ktail', 'vb'], w=['ps7'])
            b.cp('dve', v3(B_[0]), pB[:, 0:nch, :], r=['ps7'], w=['B0'])
            b.tt('pool', v3(R_[0]), v3(B_[0]), I8[:, 0:nch, :], ALU.add, r=['B0', 'I8'], w=['R0'])
            for lv in range(1, 6):
                ca, pa = lv % 2, (lv - 1) % 2
                for n in range(nch):
                    b.mm(psG[:, n, :], B_[pa][:, n, :], A_[pa][:, n, :], r=['A%d' % pa, 'B%d' % pa], w=['ps2'])
                b.cp('dve', v3(A_[ca]), psG[:, 0:nch, :], r=['ps2'], w=['A%d' % ca])
                b.tt('dve', v3(IA), psG[:, 0:nch, :], I8[:, 0:nch, :], ALU.add, r=['ps2', 'I8'], w=['IA'])
                if lv < 5:
                    psB2 = PS[7][0:64, :].rearrange("p (n c) -> p n c", n=8)
                    for n in range(nch):
                        b.mm(psB2[:, n, :], A_[pa][:, n, :], B_[pa][:, n, :], r=['A%d' % pa, 'B%d' % pa], w=['ps7'])
                    b.cp('dve', v3(B_[ca]), psB2[:, 0:nch, :], r=['ps7'], w=['B%d' % ca])
                for n in range(nch):
                    b.mm(psG[:, n, :], IA[:, n, :], R_[pa][:, n, :], r=['IA', 'R%d' % pa], w=['ps2'])
                b.cp('dve', v3(R_[ca]), psG[:, 0:nch, :], r=['ps2'], w=['R%d' % ca])
            TT = R_[1]
            for n in range(nch):
                b.mm(psG[:, n, :], TT[:, n, :], vb[:, n, :], r=['R1', 'vb'], w=['ps2'])
            b.cp('dve', v3(val), psG[:, 0:nch, :], r=['ps2'], w=['val'])
            psK = PS[7][0:64, :].rearrange("p (n c) -> p n c", n=8)
            for n in range(nch):
                b.mm(psK[:, n, :], kbg[:, n, :], TT[:, n, :], r=['R1', 'kbg'], w=['ps7'])
            b.cp('dve', v3(kcdT), psK[:, 0:nch, :], r=['ps7'], w=['kcdT'])
            sk, sbk = 'S%d' % bb, 'Sb%d' % bb
            for n in range(nch):
                p1 = PS[7][0:64, 0:128]
                p2 = PS[2][0:64, 0:128]
                b.mm(p1[:, 0:64], kcdT[:, n, :], Sb[bb][:], r=['kcdT', sbk], w=['ps7'])
                b.mm(p1[:, 64:128], qh[:, n, :], Sb[bb][:], r=['qh', sbk], w=['ps7'])
                b.tt('dve', vn[:], val[:, n, :], p1[:, 0:64], ALU.subtract, r=['val', 'ps7'], w=['vn'])
                b.ts('dve', qS[:], p1[:, 64:128], sc8['egc'][:, n:n + 1], None, ALU.mult, r=['ps7', 'egc'], w=['qS'])
                b.mm(p2[:, 0:64], attnT[:, n, :], vn[:], r=['attnT', 'vn'], w=['ps2'])
                b.mm(p2[:, 64:128], ktail[:, n, :], vn[:], r=['ktail', 'vn'], w=['ps2'])
                b.tt('dve', og[:, n, :], qS[:], p2[:, 0:64], ALU.add, r=['qS', 'ps2'], w=['og'])
                b.stt('dve', Sst[bb][:], Sst[bb][:], sc8['cd'][:, n:n + 1], p2[:, 64:128], ALU.mult, ALU.add,
                      r=[sk, 'cd', 'ps2'], w=[sk])
                b.cp('dve', Sb[bb][:], Sst[bb][:], r=[sk], w=[sbk])
            b.tt('pool', v3(Dm), v3(og), v3(og), ALU.mult, r=['og'], w=['Dm'])
            b.rsum('dve', sc8['ss'][:, 0:nch], v3(Dm), r=['Dm'], w=['ss'])
            b.act(sc8['rs'][:, 0:nch], sc8['ss'][:, 0:nch], AF.Ln, bias=EPS, scale=1.0 / 64, r=['ss'], w=['rs'])
            b.act(sc8['rs'][:, 0:nch], sc8['rs'][:, 0:nch], AF.Exp, scale=-0.5, r=['rs'], w=['rs'])
            b.tt('dve', v3(og), v3(og), bc3(sc8['rs'][:, 0:nch], 64), ALU.mult, r=['og', 'rs'], w=['og'])
            b.tt('pool', v3(og), v3(og), bcm(gnws[:], nch), ALU.mult, r=['og', 'gnws'], w=['og'])
            b.act(v2(ex), v2(zsb[s]), AF.Exp, scale=-1.0, r=['zs%d' % s], w=['ex'])
            b.ts('dve', v2(ex), v2(ex), 1.0, None, ALU.add, r=['ex'], w=['ex'])
            b.recip(v2(ex), v2(ex), r=['ex'], w=['ex'])
            b.tt('pool', v2(ex), v2(ex), v2(zsb[s]), ALU.mult, r=['ex', 'zs%d' % s], w=['ex'])
            b.tt('dve', v3(ogb), v3(og), v3(ex), ALU.mult, r=['og', 'ex'], w=['ogb'])
            pO = PS[7][0:64, 0:256].bitcast(BF16)
            for n in range(nch):
                b.tr(pO[:, n * 64:(n + 1) * 64], ogb[:, n, :], identb[0:64, 0:64], r=['ogb', 'identb', 'kcdT'], w=['ps7'])
            b.cp('dve', oTg[s][:, 0:W], pO[:, 0:W], r=['ps7'], w=['oTg%d' % s])

        def stage_store(ti):
            T.tag = 'store'
            bb, t0, nblk = tiles[ti]
            s = ti % 2
            for t in range(nblk):
                p = (t0 + t) * 128
                for src, rows, key in ((oTf[s], 0, 'oTf%d' % s), (oTg[s], 64, 'oTg%d' % s)):
                    blk = src[:, t * 128:(t + 1) * 128]
                    if p < 4 * Q:
                        j = p // Q
                        it, col = divmod(p - j * Q, TW)
                        d = ((bb * 4 + j) * NTT + it) * 128 + rows
                        b.dma(a2a_in[d:d + 64, col:col + 128], blk, r=[key], w=['a2a_in'])
                        if j >= 1 and p % Q == 0:
                            d = ((bb * 4 + j - 1) * NTT + NT2) * 128 + rows
                            b.dma(a2a_in[d:d + 64, 0:16], blk[:, 0:16], r=[key], w=['a2a_in'])
                    else:
                        d = ((bb * 4 + 3) * NTT + NT2) * 128 + rows
                        b.dma(a2a_in[d:d + 64, 0:16], blk[:, 0:16], r=[key], w=['a2a_in'])

        ztile = b.sb([128, TW], BF16)
        b.memset('pool', ztile[:], 0.0, w=['ztile'])
        for d_ in range(8):
            r_ = (d_ * NTT + NT2) * 128
            b.dma(a2a_in[r_:r_ + 128, :], ztile[:], r=['ztile'], w=['a2a_in'])
        per_tile = (len(pieces) + len(tiles) - 1) // len(tiles)
        for ti in range(len(tiles)):
            stage_A(ti)
            for pc in (pieces[ti * per_tile:(ti + 1) * per_tile] if part == 0 else []):
                conv_emit(*pc)
            stage_B(ti)
            stage_C(ti)
            stage_store(ti)

    if part == 0:
        NBLK = 8 * NTT
        CH = ag_chunk_blocks(TW)
        for k in range((NBLK + CH - 1) // CH):
            nbk = min(CH, NBLK - k * CH)
            src_ap = a2a_in[k * CH * 128:(k * CH + nbk) * 128, :]
            dst_ap = gath[8 * 128 * CH * k:8 * 128 * CH * k + 8 * nbk * 128, :]
            T.op('pool', lambda e, src_ap=src_ap, dst_ap=dst_ap: e.collective_compute(
                "AllGather", ALU.bypass, replica_groups=[list(range(8))], ins=[src_ap], outs=[dst_ap]),
                r=['a2a_in'], w=['gath'], cc=True, cost=300.0, lat=60000.0)
    T.barrier(lambda e: e.memset(bar_t[:], 0.0))
    p1scope.close()
    p2scope = contextlib.ExitStack()
    b.scope = p2scope
    if part == 2:
        for pc in pieces:
            conv_emit(*pc)
    if part != 1:
        b.off = P1_BASE
        h2 = b.sb([128, 4, D], F32)
        hb2 = b.sb([128, 4, D], BF16)
        hT2 = b.sb([128, 8, 512], BF16)
        oT2 = b.sb([128, 8, 512], BF16)
        gT = b.sb([128, 16, 512], BF16)
        mT = b.sb([128, 8, 512], BF16)
        aT = b.sb([128, 22, 512], BF16)
        uprev = b.sb([128, 44, 2], F32)
        ugL = [b.sb([128, 514], F32) for _ in range(2)]
        uuL = [b.sb([128, 514], F32) for _ in range(2)]
        cgL = [b.sb([128, 512], F32) for _ in range(2)]
        cuL = [b.sb([128, 512], F32) for _ in range(2)]
        yTL = [b.sb([128, 512], F32) for _ in range(2)]
        ug, uu, cg, cu, yT = ugL[0], uuL[0], cgL[0], cuL[0], yTL[0]
        gfin = b.sb([128, D], F32)
        gbs = b.sb([128, 16], F32)
        fcws = b.sb([128, 44, 3], F32)
        fcbs = b.sb([128, 44], F32)
        ss2 = b.sb([128, 8], F32)
        rs2 = b.sb([128, 8], F32)
        NWR = 6
        WR = [b.sb([128, 5632], BF16) for _ in range(NWR)]
        wcnt = [0]

        idxs = b.sb([128, 8 * NTT], mybir.dt.int32)
        b.dma(idxs[:], idx[:, :], w=['idxs'])
        b.dma(gfin[:], g_fin[:, :], w=['gfin'])
        b.dma(gbs[:], gbias[:, :], w=['gbs'])
        b.dma(fcws[:], fcw[:, :, :], w=['fcws'])
        b.dma(fcbs[:], fcb[:, :], w=['fcbs'])
        b.memset('pool', uprev[:], 0.0, w=['uprev'])
        b.memset('pool', h2[:], 0.0, w=['h2'])

        def wload(src, kch, c0, ncols):
            s_ = wcnt[0] % NWR
            wcnt[0] += 1
            v = WR[s_][:, 0:kch * ncols].rearrange("p (k n) -> p k n", k=kch)
            rk = [('W', src.tensor.name, k_, cb_) for k_ in range(kch) for cb_ in range(c0 // 512, (c0 + ncols + 511) // 512)]
            b.dma(v, src[:, c0:c0 + ncols].rearrange("(k p) n -> p k n", p=128), r=rk, w=['WR%d' % s_])
            return v, 'WR%d' % s_

        def norm_T(nblk, ntok, rows_ap_fn, key_h):
            T.tag = 'norm'
            b.memset('pool', ss2[:], 0.0, w=['ss2'])
            for t in range(nblk):
                b.act(hb2[:, t, :], h2[:, t, :], AF.Square, accum_out=ss2[:, t:t + 1], r=[key_h], w=['hb2', 'ss2'])
            b.act(rs2[:, 0:nblk], ss2[:, 0:nblk], AF.Ln, bias=EPS, scale=1.0 / D, r=['ss2'], w=['rs2'])
            b.act(rs2[:, 0:nblk], rs2[:, 0:nblk], AF.Exp, scale=-0.5, r=['rs2'], w=['rs2'])
            pst = PS[0][:].bitcast(BF16).rearrange("p (k c) -> p k c", k=8)
            for t in range(nblk):
                b.ts('dve', hb2[:, t, :], h2[:, t, :], rs2[:, t:t + 1], None, ALU.mult, r=[key_h, 'rs2'], w=['hb2'])
                for k in range(8):
                    b.tr(pst[:, k, :], hb2[:, t, k * 128:(k + 1) * 128], identb[:], r=['hb2', 'identb'], w=['ps0'])
                b.cp('dve', hT2[:, :, t * 128:(t + 1) * 128], pst, r=['ps0'], w=['hT2'])

        def ffn_up(ntok, first_halo):
            T.tag = 'ffn'
            for pi in range(11):
                wv_g, kg = wload(wup_b, 8, pi * 256, 256)
                wv_u, ku = wload(wup_b, 8, DFF + pi * 256, 256)
                for cc in range(2):
                    j = pi * 2 + cc
                    q_ = j % 2
                    ug, uu, cg, cu, yT = ugL[q_], uuL[q_], cgL[q_], cuL[q_], yTL[q_]
                    for half, (wv, kk, dst, pb) in enumerate(((wv_g, kg, ug, 1), (wv_u, ku, uu, 2))):
                        jj = j + 22 * half
                        for k in range(8):
                            b.mm(PS[pb][:, 0:ntok], wv[:, k, cc * 128:(cc + 1) * 128], hT2[:, k, 0:ntok], start=(k == 0), stop=(k == 7),
                                 r=[kk, 'hT2'], w=['ps%d' % pb])
                        dk = ('ug%d' if half == 0 else 'uu%d') % q_
                        b.cp('pool', dst[:, 0:2], uprev[:, jj, :], r=['uprev'], w=[dk])
                        b.cp('dve', dst[:, 2:2 + ntok], PS[pb][:, 0:ntok], r=['ps%d' % pb], w=[dk])
                        b.cp('pool', uprev[:, jj, :], dst[:, ntok:ntok + 2], r=[dk], w=['uprev'])
                        if first_halo:
                            continue
                        co = cg if half == 0 else cu
                        ck = ('cg%d' if half == 0 else 'cu%d') % q_
                        b.ts('pool', co[:, 0:ntok], dst[:, 0:ntok], fcws[:, jj, 0:1], fcbs[:, jj:jj + 1], ALU.mult, ALU.add,
                             r=[dk, 'fcws', 'fcbs'], w=[ck])
                        b.stt('dve', co[:, 0:ntok], dst[:, 1:1 + ntok], fcws[:, jj, 1:2], co[:, 0:ntok], ALU.mult, ALU.add,
                              r=[dk, 'fcws', ck], w=[ck])
                        b.stt('dve', co[:, 0:ntok], dst[:, 2:2 + ntok], fcws[:, jj, 2:3], co[:, 0:ntok], ALU.mult, ALU.add,
                              r=[dk, 'fcws', ck], w=[ck])
                    if first_halo:
                        continue
                    b.act(yT[:, 0:ntok], cg[:, 0:ntok], AF.Sigmoid, r=['cg%d' % q_], w=['yT%d' % q_])
                    b.tt('pool', cg[:, 0:ntok], cg[:, 0:ntok], cu[:, 0:ntok], ALU.mult, r=['cg%d' % q_, 'cu%d' % q_], w=['cg%d' % q_])
                    b.tt('dve', aT[:, j, 0:ntok], cg[:, 0:ntok], yT[:, 0:ntok], ALU.mult, r=['cg%d' % q_, 'yT%d' % q_], w=['aT'])

        def mixer_tail(r0, ntok, nblk):
            T.tag = 'mix'
            np_ = ntok if ntok < 128 else 128
            b.dma(h2[0:np_, 0:nblk, :], x2[r0:r0 + ntok, :].rearrange("(t p) d -> p t d", p=np_), w=['h2'])
            norm_T(nblk, ntok, None, 'h2')
            tix = r0 // TW
            for c in range(8):
                col = tix * 8 + c
                T.op('pool', lambda e, c=c, col=col: e.indirect_dma_start(
                    out=oT2[:, c, 0:TW], out_offset=None, in_=gath[:, :],
                    in_offset=bass.IndirectOffsetOnAxis(ap=idxs[:, col:col + 1], axis=0)),
                    r=['gath', 'idxs'], w=['oT2'], dma=True, cost=1500.0, lat=3000.0)
            for pi in range(4):
                wv, kk = wload(wg_b, 8, pi * 512, 512)
                for cc in range(4):
                    j = pi * 4 + cc
                    pb = 1 + j % 2
                    for k in range(8):
                        b.mm(PS[pb][:, 0:ntok], wv[:, k, cc * 128:(cc + 1) * 128], hT2[:, k, 0:ntok], start=(k == 0), stop=(k == 7),
                             r=[kk, 'hT2'], w=['ps%d' % pb])
                    b.act(gT[:, j, 0:ntok], PS[pb][:, 0:ntok], AF.Sigmoid, bias=gbs[:, j:j + 1], scale=1.0,
                          r=['ps%d' % pb, 'gbs'], w=['gT'])
            wf_, kf = wload(wbf_b, 4, 0, 1024)
            wg_, kg_ = wload(wbg_b, 4, 0, 1024)
            for j in range(8):
                for k in range(4):
                    b.mm(PS[1][:, 0:ntok], wf_[:, k, j * 128:(j + 1) * 128], oT2[:, k, 0:ntok], start=(k == 0), stop=(k == 3),
                         r=[kf, 'oT2'], w=['ps1'])
                for k in range(4):
                    b.mm(PS[2][:, 0:ntok], wg_[:, k, j * 128:(j + 1) * 128], oT2[:, 4 + k, 0:ntok], start=(k == 0), stop=(k == 3),
                         r=[kg_, 'oT2'], w=['ps2'])
                b.tt('dve', yT[:, 0:ntok], PS[1][:, 0:ntok], gT[:, j, 0:ntok], ALU.mult, r=['ps1', 'gT'], w=['yT0'])
                b.tt('dve', cg[:, 0:ntok], PS[2][:, 0:ntok], gT[:, 8 + j, 0:ntok], ALU.mult, r=['ps2', 'gT'], w=['cg0'])
                b.tt('pool', mT[:, j, 0:ntok], yT[:, 0:ntok], cg[:, 0:ntok], ALU.add, r=['yT0', 'cg0'], w=['mT'])
            for hc in range(2):
                wv, kk = wload(wo_b, 8, hc * 512, 512)
                for t in range(nblk):
                    nt_ = min(128, ntok - t * 128)
                    pb = 1 + t % 2
                    for k in range(8):
                        b.mm(PS[pb][0:nt_, :], mT[:, k, t * 128:t * 128 + nt_], wv[:, k, :], start=(k == 0), stop=(k == 7),
                             r=[kk, 'mT'], w=['ps%d' % pb])
                    b.tt('dve', h2[0:nt_, t, hc * 512:(hc + 1) * 512], h2[0:nt_, t, hc * 512:(hc + 1) * 512], PS[pb][0:nt_, :], ALU.add,
                         r=['ps%d' % pb, 'h2'], w=['h2'])


        nc._p2_free = nc.sbuf_bytes_remaining
        p2tiles = [(it * TW, TW) for it in range(NT2)] + [(Q, 16)]
        for r0, ntok in p2tiles:
            nblk = (ntok + 127) // 128
            mixer_tail(r0, ntok, nblk)
            norm_T(nblk, ntok, None, 'h2')
            ffn_up(ntok, False)
            T.tag = 'down'
            for hc in range(4):
                wv, kk = wload(wdn_b, 22, hc * 256, 256)
                for t in range(nblk):
                    nt_ = min(128, ntok - t * 128)
                    pb = 1 + t % 2
                    for k in range(22):
                        b.mm(PS[pb][0:nt_, 0:256], aT[:, k, t * 128:t * 128 + nt_], wv[:, k, :], start=(k == 0), stop=(k == 21),
                             r=[kk, 'aT'], w=['ps%d' % pb])
                    b.tt('dve', h2[0:nt_, t, hc * 256:(hc + 1) * 256], h2[0:nt_, t, hc * 256:(hc + 1) * 256], PS[pb][0:nt_, 0:256], ALU.add,
                         r=['ps%d' % pb, 'h2'], w=['h2'])
            b.memset('pool', ss2[:], 0.0, w=['ss2'])
            for t in range(nblk):
                b.act(hb2[:, t, :], h2[:, t, :], AF.Square, accum_out=ss2[:, t:t + 1], r=['h2'], w=['hb2', 'ss2'])
            b.act(rs2[:, 0:nblk], ss2[:, 0:nblk], AF.Ln, bias=EPS, scale=1.0 / D, r=['ss2'], w=['rs2'])
            b.act(rs2[:, 0:nblk], rs2[:, 0:nblk], AF.Exp, scale=-0.5, r=['rs2'], w=['rs2'])
            for t in range(nblk):
                b.stt('dve', h2[:, t, :], h2[:, t, :], rs2[:, t:t + 1], gfin[:], ALU.mult, ALU.mult, r=['h2', 'rs2', 'gfin'], w=['h2'])
            if r0 == 0:
                b.dma(out[0:112, :], h2[16:128, 0, :], r=['h2'], w=['out'])
                if nblk > 1:
                    b.dma(out[112:ntok - 16, :].rearrange("(t p) d -> p t d", p=128), h2[:, 1:nblk, :], r=['h2'], w=['out'])
            elif ntok == TW:
                b.dma(out[r0 - 16:r0 - 16 + TW, :].rearrange("(t p) d -> p t d", p=128), h2[:, 0:nblk, :], r=['h2'], w=['out'])
            else:
                b.dma(out[Q - 16:Q, :], h2[0:16, 0, :], r=['h2'], w=['out'])
        p2scope.close()

    fkeys = ['out'] if part != 1 else ['a2a_in']
    T.finalize(final_wait_keys=fkeys)
    with contextlib.ExitStack() as st:
        esem = {e: st.enter_context(nc.semaphore("s_" + e)) for e in ENGS}
        dsems = {rg: [st.enter_context(nc.semaphore("d%s%d" % (rg, i))) for i in range(NDMA_SEMS)] for rg in ('s', 'p')}
        ccsem = [st.enter_context(nc.semaphore("ccsem%d" % i)) for i in range(max(1, T.ncc))]
        st.enter_context(nc.allow_low_precision("bf16 matmul operands by design, fp32 accumulation"))
        block = st.enter_context(nc.Block())
        T.emit(esem, dsems, ccsem, block)
    outer.close()
    nc._stats = (T.makespan, T.busy, T.nops)
    return nc


def make_in_maps(inp, SEQ, fused=True):
    L, NB, LP, Q, QH = cfg(SEQ)
    f = np.float32
    x = np.asarray(inp["x"], f)
    meta = np.asarray(inp["meta_tokens"], f)
    xf = np.zeros((2, LP, D), f)
    xf[:, 0:NMETA] = meta[None]
    xf[:, NMETA:L] = x
    w_in = np.asarray(inp["w_in"], f)[0]
    conv_w = np.asarray(inp["gdn_conv_w"], f)[0]
    gm = np.asarray(inp["norm_mix_w"], f)[0]
    gf = np.asarray(inp["norm_ffn_w"], f)[0]
    shared = dict(
        g_mix=np.ascontiguousarray(gm.reshape(8, 128).T),
        g_ffn=np.ascontiguousarray(gf.reshape(8, 128).T),
        g_fin=np.ascontiguousarray(np.broadcast_to(np.asarray(inp["norm_final_w"], f)[None, :], (128, D))),
        wg=np.ascontiguousarray(w_in[:, 3608:3608 + 2048]),
        gbias=np.ascontiguousarray(np.asarray(inp["gate_bias"], f)[0].reshape(16, 128).T),
        wbf=np.ascontiguousarray(np.asarray(inp["w_branch_fox"], f)[0]),
        wbg=np.ascontiguousarray(np.asarray(inp["w_branch_gdn"], f)[0]),
        wo=np.ascontiguousarray(np.asarray(inp["w_out"], f)[0]),
        wup=np.ascontiguousarray(np.asarray(inp["ffn_w_up"], f)[0]),
        fcw=np.ascontiguousarray(np.asarray(inp["ffn_conv_w"], f)[0].T.reshape(44, 128, 3).transpose(1, 0, 2)),
        fcb=np.ascontiguousarray(np.asarray(inp["ffn_conv_b"], f)[0].reshape(44, 128).T),
        wdn=np.ascontiguousarray(np.asarray(inp["ffn_w_down"], f)[0]),
        gnw=np.ascontiguousarray(np.broadcast_to(np.asarray(inp["gdn_norm_w"], f)[0][None, :], (64, 64))),
        xf=xf,
    )
    maps = []
    for c in range(8):
        h = c
        fq = w_in[:, h * 64:(h + 1) * 64]
        fk = w_in[:, 512 + h * 64:512 + (h + 1) * 64]
        fv = w_in[:, 1024 + h * 64:1024 + (h + 1) * 64]
        ff = w_in[:, 1536 + h:1537 + h]
        g0 = 1544
        gq = w_in[:, g0 + h * 64:g0 + (h + 1) * 64]
        gk = w_in[:, g0 + 512 + h * 64:g0 + 512 + (h + 1) * 64]
        gv = w_in[:, g0 + 1024 + h * 64:g0 + 1024 + (h + 1) * 64]
        z = w_in[:, 3080 + h * 64:3080 + (h + 1) * 64]
        bcol = w_in[:, 3592 + h:3593 + h]
        acol = w_in[:, 3600 + h:3601 + h]
        w1f = np.ascontiguousarray(np.concatenate([fq, fq, fk, fk, gq, gk, gv], axis=1))
        w1t = np.ascontiguousarray(np.concatenate([fv, ff, z, bcol, acol], axis=1))
        pp = np.zeros((128, 8), f)
        pp[:, 0] = np.asarray(inp["fgt_bias"], f)[0, h]
        pp[:, 1] = np.asarray(inp["gdn_a_log"], f)[0, h]
        pp[:, 2] = np.asarray(inp["gdn_dt_bias"], f)[0, h]
        cwm = np.zeros((64, 12), f)
        for g in range(3):
            cwm[:, g * 4:(g + 1) * 4] = conv_w[:, g * 512 + h * 64:g * 512 + (h + 1) * 64].T
        bb, j = divmod(c, 4)
        x2 = np.ascontiguousarray(xf[bb, j * Q:j * Q + QH])
        TW = min(512, Q)
        NTT = Q // TW + 1
        idx = np.zeros((128, 8 * NTT), np.int32)
        pa = np.arange(128)
        for it in range(NTT):
            for ch in range(8):
                half, cc = divmod(ch, 4)
                src = 2 * cc + pa // 64
                blk = c * NTT + it
                if fused:
                    base = np.array([gath_row(int(s_), blk, 8 * NTT, ag_chunk_blocks(TW)) for s_ in src])
                else:
                    base = (src * NTT + it) * 128
                idx[:, it * 8 + ch] = base + half * 64 + pa % 64
        m = dict(shared)
        m.update(w1f=w1f, w1t=w1t, pp=pp, convw=cwm, x2=x2, idx=idx)
        maps.append(m)
    return maps


_CACHE = {}


FUSED = False
P1KEYS = ["xf", "w1f", "w1t", "pp", "convw", "gnw", "g_mix"]
P2KEYS = ["x2", "g_mix", "g_ffn", "g_fin", "wg", "gbias", "wbf", "wbg", "wo", "wup", "fcw", "fcb", "wdn", "idx"]


def run(inp, SEQ, fused=None):
    fused = FUSED if fused is None else fused
    L, NB, LP, Q, QH = cfg(SEQ)
    TW = min(512, Q)
    NTT = Q // TW + 1
    maps = make_in_maps(inp, SEQ, fused)
    if fused:
        if (SEQ, 0) not in _CACHE:
            _CACHE[(SEQ, 0)] = build(SEQ, 0)
        res = run_bass_kernel_spmd(_CACHE[(SEQ, 0)], maps, core_ids=list(range(8)))
    else:
        for part in (1, 2):
            if (SEQ, part) not in _CACHE:
                _CACHE[(SEQ, part)] = build(SEQ, part)
        res1 = run_bass_kernel_spmd(_CACHE[(SEQ, 1)], [{k: m[k] for k in P1KEYS} for m in maps], core_ids=list(range(8)))
        sh = [np.asarray(res1.results[c]["a2a_in"]) for c in range(8)]
        maps2 = []
        for c in range(8):
            m = {k: maps[c][k] for k in P2KEYS}
            m["gath"] = np.ascontiguousarray(np.concatenate(
                [sh[s_][c * NTT * 128:(c + 1) * NTT * 128] for s_ in range(8)], axis=0))
            maps2.append(m)
        res = run_bass_kernel_spmd(_CACHE[(SEQ, 2)], maps2, core_ids=list(range(8)))
    out = np.zeros((2, SEQ, D), np.float32)
    for c in range(8):
        bb, j = divmod(c, 4)
        out[bb, j * Q:(j + 1) * Q] = res.results[c]["out"]
    return out, res


def kernel(**inputs):
    SEQ = inputs["x"].shape[1]
    out, _ = run(inputs, SEQ)
    return out
```

```python
import contextlib
import numpy as np
import concourse.bass as bass
import concourse.mybir as mybir
from concourse.bass_utils import run_bass_kernel_spmd

F32 = mybir.dt.float32
BF16 = mybir.dt.bfloat16
AF = mybir.ActivationFunctionType
ALU = mybir.AluOpType
AX = mybir.AxisListType

D = 1024
NMETA = 16
DFF = 2816
EPS = 1e-6
SELF_SYNC = True
NDMA_SEMS = 24
FILL_EVERY = 10 ** 9
FILL_N = 512
HOP_NS = 380.0
ENGS = ['pe', 'act', 'dve', 'pool', 'sp']


class Tracker:
    def __init__(self):
        self.ops = []
        self.lastw = {}
        self.readers = {}
        self.ndma = {'s': 0, 'p': 0}
        self.bar = set()
        self.ncc = 0
        self.tag = ''

    def op(self, eng, fn, r=(), w=(), dma=False, cc=False, cost=100.0, lat=0.0):
        i = len(self.ops)
        w = list(w) + [k for k in r if isinstance(k, str) and k.startswith('ps') and k[2:].isdigit()]
        deps = set(self.bar)
        for k in r:
            j = self.lastw.get(k)
            if j is not None:
                deps.add(j)
        for k in w:
            j = self.lastw.get(k)
            if j is not None:
                deps.add(j)
            deps.update(self.readers.get(k, ()))
        d = None
        if dma:
            ring = 'p' if eng == 'pool' else 's'
            d = (ring, -1)
        ccid = None
        if cc:
            ccid = self.ncc
            self.ncc += 1
        self.ops.append(dict(eng=eng, fn=fn, deps=deps, dma=d, users=False, cc=cc, ccid=ccid, cost=cost, lat=lat, bar=False, tag=self.tag))
        for k in r:
            self.readers.setdefault(k, []).append(i)
        for k in w:
            self.lastw[k] = i
            self.readers[k] = []
        return i

    def barrier(self, fn):
        self.bar = set()
        i = self.op('pool', fn, (), ())
        self.ops[i]['bar'] = True
        self.bar = {i}
        return i

    def finalize(self, final_wait_keys=()):
        import heapq
        ops = self.ops
        n = len(ops)
        fin = set()
        for k in final_wait_keys:
            j = self.lastw.get(k)
            if j is not None:
                fin.add(j)
        succ = [[] for _ in range(n)]
        npend = [0] * n
        for i, o in enumerate(ops):
            dd = range(i) if o['bar'] else o['deps']
            npend[i] = len(dd)
            for j in dd:
                succ[j].append(i)
        ready = [0.0] * n
        start = [0.0] * n
        t_e = {e: 0.0 for e in ENGS}
        fut = {e: [] for e in ENGS}
        avail = {e: [] for e in ENGS}
        for i, o in enumerate(ops):
            if npend[i] == 0:
                heapq.heappush(fut[o['eng']], (0.0, i))
        done = 0
        last_on = {}
        t_prev_end = {}
        rby = [None] * n
        while done < n:
            best = None
            for e in ENGS:
                f, a_ = fut[e], avail[e]
                while f and f[0][0] <= t_e[e]:
                    heapq.heappush(a_, heapq.heappop(f)[1])
                if a_:
                    cand = (t_e[e], a_[0], e, True)
                elif f:
                    cand = (f[0][0], f[0][1], e, False)
                else:
                    continue
                if best is None or cand[:2] < best[:2]:
                    best = cand
            st, i, e, from_avail = best
            if from_avail:
                heapq.heappop(avail[e])
            else:
                heapq.heappop(fut[e])
            o = ops[i]
            start[i] = st
            o['by_eng'] = (st <= t_prev_end.get(e, 0.0) + 1e-9 and e in last_on) and last_on.get(e)
            last_on[e] = i
            t_prev_end[e] = st + o['cost']
            t_e[e] = st + o['cost']
            fi = st + o['cost'] + o['lat']
            for s_ in succ[i]:
                os_ = ops[s_]
                rt = fi + (60.0 if (os_['eng'] == e and o['dma'] is None and not o['cc']) else HOP_NS)
                if rt > ready[s_]:
                    ready[s_] = rt
                    rby[s_] = i
                npend[s_] -= 1
                if npend[s_] == 0:
                    heapq.heappush(fut[os_['eng']], (ready[s_], s_))
            done += 1
        self.makespan = max(t_e.values())
        self.start = start
        self.rby = rby
        self.busy = {e: sum(o['cost'] for o in ops if o['eng'] == e) for e in ENGS}
        self.nops = {e: sum(1 for o in ops if o['eng'] == e) for e in ENGS}
        self.per_eng = {e: [] for e in ENGS}
        for i in sorted(range(n), key=lambda i: (start[i], i)):
            self.per_eng[ops[i]['eng']].append(i)
        pos = {}
        for e in ENGS:
            for p_, i in enumerate(self.per_eng[e]):
                pos[i] = p_
        for i, o in enumerate(ops):
            if not o['bar']:
                continue
            red = set()
            for e in ENGS:
                pre = [j for j in self.per_eng[e] if j < i and ops[j]['dma'] is None and not ops[j]['cc']]
                if pre:
                    red.add(pre[-1])
                for ring in ('s', 'p'):
                    pd = [j for j in self.per_eng[e] if j < i and ops[j]['dma'] is not None and ops[j]['dma'][0] == ring]
                    red.update(pd[-NDMA_SEMS:])
            red.update(j for j in range(i) if ops[j]['cc'])
            o['deps'] = red
        for ring, e in (('s', 'sp'), ('p', 'pool')):
            dma_ops = [i for i in self.per_eng[e] if ops[i]['dma'] is not None]
            assert all(ops[i]['dma'][0] == ring for i in dma_ops)
            for n_, i in enumerate(dma_ops):
                ops[i]['dma'] = (ring, n_)
                if n_ >= NDMA_SEMS:
                    ops[i]['deps'].add(dma_ops[n_ - NDMA_SEMS])
        for e in ENGS:
            assert e in ('sp', 'pool') or all(ops[i]['dma'] is None for i in self.per_eng[e])
        for i, o in enumerate(ops):
            for j in o['deps']:
                dj = ops[j]
                if dj['dma'] is not None or dj['cc']:
                    continue
                if dj['eng'] == o['eng'] and o['dma'] is None and not o['cc']:
                    if o['eng'] == 'pe' or not SELF_SYNC:
                        continue
                dj['users'] = True
        for j in fin:
            if ops[j]['dma'] is None and not ops[j]['cc']:
                ops[j]['users'] = True
        for e in ENGS:
            c_ = 0
            for i in self.per_eng[e]:
                o = ops[i]
                if o['dma'] is None and not o['cc'] and o['users']:
                    c_ += 1
                    o['val'] = c_
        self.fin = fin

    def emit(self, esem, dsems, ccsem, block):
        ops = self.ops

        def target(j):
            dj = ops[j]
            if dj['cc']:
                return ('c', dj['ccid']), ccsem[dj['ccid']], 1
            if dj['dma'] is not None:
                ring, d = dj['dma']
                return ('d', ring, d % NDMA_SEMS), dsems[ring][d % NDMA_SEMS], 16 * (d // NDMA_SEMS + 1)
            return ('e', dj['eng']), esem[dj['eng']], dj['val']

        def run(engname, eng):
            known = {}
            for i in self.per_eng[engname]:
                o = ops[i]
                waits = {}
                for j in o['deps']:
                    dj = ops[j]
                    if (dj['dma'] is None and not dj['cc'] and dj['eng'] == engname
                            and o['dma'] is None and not o['cc']):
                        if engname == 'pe' or not SELF_SYNC:
                            continue
                    key, sem, val = target(j)
                    if known.get(key, 0) >= val:
                        continue
                    if key not in waits or waits[key][1] < val:
                        waits[key] = (sem, val)
                for key, (sem, val) in waits.items():
                    eng.wait_ge(sem, val)
                    known[key] = val
                ins = o['fn'](eng)
                if o['cc']:
                    ins.then_inc(ccsem[o['ccid']])
                elif o['dma'] is not None:
                    ins.then_inc(dsems[o['dma'][0]][o['dma'][1] % NDMA_SEMS], 16)
                elif o['users']:
                    ins.then_inc(esem[engname], 1)
            if engname == 'sp':
                for j in self.fin:
                    key, sem, val = target(j)
                    if known.get(key, 0) >= val:
                        continue
                    eng.wait_ge(sem, val)
                    known[key] = val

        block.tensor(lambda e: run('pe', e))
        block.scalar(lambda e: run('act', e))
        block.vector(lambda e: run('dve', e))
        block.gpsimd(lambda e: run('pool', e))
        block.sync(lambda e: run('sp', e))


class B:
    def __init__(self, nc):
        self.nc = nc
        self.T = Tracker()
        self.off = 0
        self.n = 0
        self.scope = None

    def sb(self, shape, dt):
        t = self.scope.enter_context(self.nc.sbuf_tensor("sb%d" % self.n, list(shape), dt))
        self.n += 1
        return t

    @staticmethod
    def _n(ap):
        sh = ap.shape
        n = 1
        for x in sh[1:]:
            n *= int(x)
        return n

    def _c(self, eng, out, in_=None):
        n = self._n(out)
        if eng == 'act':
            return 155.0 + n * 0.835
        if eng == 'dve':
            f = 0.96
            return 60.0 + n / f
        if eng == 'pool':
            return 100.0 + n / 0.6
        return 100.0

    def mm(self, out, lhsT, rhs, start=True, stop=True, r=(), w=()):
        n = self._n(rhs)
        m = self._n(lhsT)
        f4 = 4.0 if rhs.dtype == F32 else 1.0
        c = 10.0 + m / 1.2 * (2.0 if f4 > 1 else 1.0) + (n / 2.4) * f4
        return self.T.op('pe', lambda e: e.matmul(out, lhsT=lhsT, rhs=rhs, start=start, stop=stop), r, w, cost=c, lat=0.0)

    def tr(self, out, in_, ident, r=(), w=()):
        return self.T.op('pe', lambda e: e.transpose(out=out, in_=in_, identity=ident), r, w, cost=10.0 + 128 / 1.2 + self._n(in_) / 2.4, lat=0.0)

    def act(self, out, in_, func, bias=0.0, scale=1.0, accum_out=None, r=(), w=()):
        c = self._c('act', out)
        if accum_out is None:
            return self.T.op('act', lambda e: e.activation(out=out, in_=in_, func=func, bias=bias, scale=scale), r, w, cost=c)
        return self.T.op('act', lambda e: e.activation(out=out, in_=in_, func=func, bias=bias, scale=scale,
                                                       accum_out=accum_out), r, w, cost=c)

    def ts(self, eng, out, in0, s1, s2, op0, op1=None, r=(), w=()):
        c = self._c(eng, out)
        if op1 is None:
            return self.T.op(eng, lambda e: e.tensor_scalar(out=out, in0=in0, scalar1=s1, scalar2=None, op0=op0), r, w, cost=c)
        return self.T.op(eng, lambda e: e.tensor_scalar(out=out, in0=in0, scalar1=s1, scalar2=s2, op0=op0, op1=op1), r, w, cost=c)

    def tt(self, eng, out, in0, in1, op, r=(), w=()):
        return self.T.op(eng, lambda e: e.tensor_tensor(out=out, in0=in0, in1=in1, op=op), r, w, cost=self._c(eng, out))

    def stt(self, eng, out, in0, scalar, in1, op0, op1, r=(), w=()):
        return self.T.op(eng, lambda e: e.scalar_tensor_tensor(out=out, in0=in0, scalar=scalar, in1=in1,
                                                               op0=op0, op1=op1), r, w, cost=self._c(eng, out))

    def cp(self, eng, out, in_, r=(), w=()):
        return self.T.op(eng, lambda e: e.tensor_copy(out=out, in_=in_), r, w, cost=self._c(eng, out))

    def recip(self, out, in_, r=(), w=()):
        return self.T.op('dve', lambda e: e.reciprocal(out=out, in_=in_), r, w, cost=self._c('dve', out))

    def memset(self, eng, ap, val, w=()):
        return self.T.op(eng, lambda e: e.memset(ap, val), (), w, cost=self._c(eng, ap))

    def asel(self, out, in_, pattern, cmp, fill, base, cm, r=(), w=()):
        return self.T.op('pool', lambda e: e.affine_select(out=out, in_=in_, pattern=pattern, compare_op=cmp,
                                                           fill=fill, base=base, channel_multiplier=cm), r, w, cost=self._c('pool', out))

    def rsum(self, eng, out, in_, r=(), w=()):
        return self.T.op(eng, lambda e: e.reduce_sum(out=out, in_=in_, axis=AX.X), r, w, cost=self._c(eng, in_))

    def dma(self, out, in_, r=(), w=(), eng='sp'):
        sh = out.shape
        nbytes = 1
        for x in sh:
            nbytes *= int(x)
        nbytes *= 4 if out.dtype == F32 else 2
        return self.T.op(eng, lambda e: e.dma_start(out=out, in_=in_), r, w, dma=True, cost=60.0, lat=2000.0 + nbytes / 60.0)


def bc3(ap2, n):
    return ap2.unsqueeze(2).to_broadcast([ap2.shape[0], ap2.shape[1], n])


def bcm(ap2, a):
    return ap2.unsqueeze(1).to_broadcast([ap2.shape[0], a, ap2.shape[1]])


def ag_chunk_blocks(TW):
    import os
    return max(1, (int(os.environ.get("AGKB", "768")) * 1024) // (128 * TW * 2))


def gath_row(src, blk, NBLK, CH):
    k, w = divmod(blk, CH)
    nbk = min(CH, NBLK - k * CH)
    return 8 * 128 * CH * k + src * (nbk * 128) + w * 128


def cfg(SEQ):
    L = SEQ + NMETA
    NB = (L + 127) // 128
    LP = NB * 128
    Q = SEQ // 4
    QH = Q + 16
    return L, NB, LP, Q, QH


NF = 128 + 128 + 64 * 3
NT = 65 + 64 + 2
UP_PIECES = [(i * 512, 512) for i in range(11)]


def build(SEQ, part=0, debug=False):
    L, NB, LP, Q, QH = cfg(SEQ)
    TW = min(512, Q)
    NT2 = Q // TW
    NTT = NT2 + 1
    assert Q % 128 == 0 and Q % TW == 0
    nc = bass.Bass("TRN2", target_bir_lowering=False)

    P1IN = {"xf", "w1f", "w1t", "pp", "convw", "gnw", "g_mix"}
    P2IN = {"x2", "g_mix", "g_ffn", "g_fin", "wg", "gbias", "wbf", "wbg", "wo", "wup", "fcw", "fcb", "wdn"}

    def din(name, shape):
        if part == 1 and name not in P1IN:
            return None
        if part == 2 and name not in P2IN:
            return None
        return nc.dram_tensor(name, list(shape), F32, kind="ExternalInput").ap()

    xf = din("xf", [2, LP, D])
    x2 = din("x2", [QH, D])
    w1f = din("w1f", [D, NF])
    w1t = din("w1t", [D, NT])
    pp = din("pp", [128, 8])
    convw = din("convw", [64, 12])
    gnw = din("gnw", [64, 64])
    g_mix = din("g_mix", [128, 8])
    g_ffn = din("g_ffn", [128, 8])
    g_fin = din("g_fin", [128, D])
    wg = din("wg", [D, 2 * D])
    gbias = din("gbias", [128, 16])
    wbf = din("wbf", [512, D])
    wbg = din("wbg", [512, D])
    wo = din("wo", [D, D])
    wup = din("wup", [D, 2 * DFF])
    fcw = din("fcw", [128, 44, 3])
    fcb = din("fcb", [128, 44])
    wdn = din("wdn", [DFF, D])
    if part != 1:
        out = nc.dram_tensor("out", [Q, D], F32, kind="ExternalOutput").ap()
        idx = nc.dram_tensor("idx", [128, 8 * NTT], mybir.dt.int32, kind="ExternalInput").ap()

    if part == 1:
        a2a_in = nc.dram_tensor("a2a_in", [8 * NTT * 128, TW], BF16, kind="ExternalOutput").ap()
    elif part == 2:
        gath = nc.dram_tensor("gath", [8 * NTT * 128, TW], BF16, kind="ExternalInput").ap()
    else:
        a2a_in = nc.dram_tensor("a2a_in", [8 * NTT * 128, TW], BF16).ap()
        gath = nc.dram_tensor("gath", [8 * 8 * NTT * 128, TW], BF16).ap()
    wg_b = nc.dram_tensor("wg_b", [D, 2 * D], BF16).ap()
    wbf_b = nc.dram_tensor("wbf_b", [512, D], BF16).ap()
    wbg_b = nc.dram_tensor("wbg_b", [512, D], BF16).ap()
    wo_b = nc.dram_tensor("wo_b", [D, D], BF16).ap()
    wup_b = nc.dram_tensor("wup_b", [D, 2 * DFF], BF16).ap()
    wdn_b = nc.dram_tensor("wdn_b", [DFF, D], BF16).ap()

    b = B(nc)
    T = b.T
    outer = contextlib.ExitStack()
    b.scope = outer
    PS = [nc.alloc_psum_tensor("ps%d" % i, [128, 512], F32) for i in range(8)]
    PSB = [nc.alloc_psum_tensor("psb%d" % i, [128, 1024], BF16) for i in range(0)]

    identb = b.sb([128, 128], BF16)
    trium = b.sb([128, 128], BF16)
    U128 = b.sb([128, 128], F32)
    ones128 = b.sb([128, 128], F32)
    m_upi = b.sb([64, 8, 64], F32)
    m_lows = b.sb([64, 8, 64], F32)
    I8 = b.sb([64, 8, 64], BF16)
    tmpf = b.sb([128, 128], F32)
    ppt = b.sb([128, 8], F32)
    negfb = b.sb([128, 1], F32)
    negA = b.sb([128, 1], F32)
    gmix = b.sb([128, 8], F32)
    gffn = b.sb([128, 8], F32)
    bar_t = b.sb([128, 1], F32)
    CONST_END = b.off

    b.memset('pool', tmpf[:], 1.0, w=['tmpf'])
    b.asel(tmpf[:], tmpf[:], [[-1, 128]], ALU.is_equal, 0.0, 0, 1, r=['tmpf'], w=['tmpf'])
    b.cp('dve', identb[:], tmpf[:], r=['tmpf'], w=['identb'])
    b.memset('pool', ones128[:], 1.0, w=['ones128'])
    b.asel(U128[:], ones128[:], [[1, 128]], ALU.is_ge, 0.0, 0, -1, r=['ones128'], w=['U128'])
    b.cp('dve', trium[:], U128[:], r=['U128'], w=['trium'])
    b.memset('pool', m_upi[:], 1.0, w=['m_upi'])
    b.asel(m_upi[:], m_upi[:], [[0, 8], [1, 64]], ALU.is_ge, 0.0, 0, -1, r=['m_upi'], w=['m_upi'])
    b.memset('pool', m_lows[:], 1.0, w=['m_lows'])
    b.asel(m_lows[:], m_lows[:], [[0, 8], [-1, 64]], ALU.is_gt, 0.0, 0, 1, r=['m_lows'], w=['m_lows'])
    i8f = tmpf[0:64, :].rearrange("p (a c) -> p a c", a=2)
    b.memset('pool', tmpf[:], 1.0, w=['tmpf'])
    b.asel(i8f, i8f, [[0, 2], [-1, 64]], ALU.is_equal, 0.0, 0, 1, r=['tmpf'], w=['tmpf'])
    for a in range(4):
        b.cp('dve', I8[:, 2 * a:2 * a + 2, :], i8f, r=['tmpf'], w=['I8'])
    if part != 2:
        b.dma(ppt[:], pp[:, :], w=['ppt'])
    else:
        b.memset('pool', ppt[:], 0.0, w=['ppt'])
    if part != 1:
        b.dma(gffn[:], g_ffn[:, :], w=['gffn'])
    b.dma(gmix[:], g_mix[:, :], w=['gmix'])
    b.ts('dve', negfb[:], ppt[:, 0:1], -1.0, None, ALU.mult, r=['ppt'], w=['negfb'])
    b.act(negA[:], ppt[:, 1:2], AF.Exp, r=['ppt'], w=['negA'])
    b.ts('dve', negA[:], negA[:], -1.0, None, ALU.mult, r=['negA'], w=['negA'])

    P1_BASE = b.off
    NSTG = 6 if part == 2 else 2
    stg = [b.sb([128, 512], F32) for _ in range(NSTG)] if part != 1 else None
    stb = [b.sb([128, 512], BF16) for _ in range(NSTG)] if part != 1 else None
    cnt = [0]

    pieces = []

    def conv_piece(src, dst, r0, c0, nc_, gain):
        pieces.append((src, dst, r0, c0, nc_, gain))

    def conv_emit(src, dst, r0, c0, nc_, gain):
        T.tag = 'conv'
        s = cnt[0] % NSTG
        cnt[0] += 1
        b.dma(stg[s][:, 0:nc_], src[r0:r0 + 128, c0:c0 + nc_], w=['stg%d' % s])
        if gain is None:
            b.cp('pool', stb[s][:, 0:nc_], stg[s][:, 0:nc_], r=['stg%d' % s], w=['stb%d' % s])
        else:
            b.ts('pool', stb[s][:, 0:nc_], stg[s][:, 0:nc_], gain, None, ALU.mult,
                 r=['stg%d' % s, 'gmix', 'gffn'], w=['stb%d' % s])
        b.dma(dst[r0:r0 + 128, c0:c0 + nc_], stb[s][:, 0:nc_], r=['stb%d' % s], w=[('W', dst.tensor.name, r0 // 128, c0 // 512)])

    for k in range(8):
        for c0 in range(0, 2048, 512):
            conv_piece(wg, wg_b, k * 128, c0, 512, gmix[:, k:k + 1])
    for k in range(4):
        for c0 in (0, 512):
            conv_piece(wbf, wbf_b, k * 128, c0, 512, None)
            conv_piece(wbg, wbg_b, k * 128, c0, 512, None)
    for k in range(8):
        for c0 in (0, 512):
            conv_piece(wo, wo_b, k * 128, c0, 512, None)
    for c0 in range(0, 5632, 512):
        for k in range(8):
            conv_piece(wup, wup_b, k * 128, c0, 512, gffn[:, k:k + 1])
    for k in range(22):
        for c0 in (0, 512):
            conv_piece(wdn, wdn_b, k * 128, c0, 512, None)

    p1scope = contextlib.ExitStack()
    b.scope = p1scope
    if part != 2:
        b.off = P1_BASE + 2 * 4096 + 2 * 2048
        w1fs = b.sb([128, 8, NF], BF16)
        w1ts = b.sb([128, 8, NT], BF16)
        cw = b.sb([64, 12], F32)
        gnws = b.sb([64, 64], F32)
        KT = b.sb([128, LP], BF16)
        Vst = b.sb([128, 2, NB, 65], BF16)
        cneg = b.sb([128, 2, NB], F32)
        cend = b.sb([128, 2, NB], F32)
        carry = b.sb([128, 2], F32)
        hbuf = [b.sb([128, 4, D], F32) for _ in range(2)]
        hnb = b.sb([128, 4, D], BF16)
        hnT1 = b.sb([128, 8, 512], BF16)
        hnT = [hnT1, hnT1]
        QT = [b.sb([128, 512], BF16) for _ in range(2)]
        ssq = b.sb([128, 8], F32)
        rstd = b.sb([128, 8], F32)
        PT = [b.sb([128, 512], BF16) for _ in range(4)]
        biasq = [b.sb([128, NB], F32) for _ in range(2)]
        spt = b.sb([128, 4], F32)
        pre = b.sb([128, 4], F32)
        osb = b.sb([65, 512], F32)
        rdn = b.sb([64, 512], F32)
        sel65 = b.sb([65, 64], F32)
        fill_rhs = b.sb([128, 512], BF16)
        oTf = [b.sb([64, 512], BF16) for _ in range(2)]
        oTg = [b.sb([64, 512], BF16) for _ in range(2)]
        cin = [[b.sb([64, 3 + 512], F32) for g in range(3)] for bb in range(2)]
        Sst = [b.sb([64, 64], F32) for bb in range(2)]
        Sb = [b.sb([64, 64], BF16) for bb in range(2)]

        def g512(dt):
            return b.sb([64, 8, 64], dt)

        cy = [g512(F32) for _ in range(3)]
        ex = g512(F32)
        sq = g512(F32)
        qh = g512(BF16)
        kh = g512(BF16)
        vT = g512(BF16)
        vb = g512(BF16)
        kbg = g512(BF16)
        ktail = g512(BF16)
        Dm = g512(F32)
        E1 = g512(F32)
        E2 = g512(F32)
        A_ = [g512(BF16) for _ in range(2)]
        B_ = [g512(BF16) for _ in range(2)]
        IA = g512(BF16)
        R_ = [g512(BF16) for _ in range(2)]
        attnT = g512(BF16)
        val = g512(F32)
        kcdT = g512(BF16)
        og = g512(F32)
        ogb = g512(BF16)
        zsb = [g512(F32) for _ in range(2)]
        beta2 = [b.sb([64, 8], F32) for _ in range(2)]
        g2 = [b.sb([64, 8], F32) for _ in range(2)]
        qS = b.sb([64, 64], F32)
        vn = b.sb([64, 64], BF16)
        sc8 = {nm: b.sb([64, 8], F32) for nm in ['beta', 'g', 'gc', 'gtot', 'egc', 'etail', 'cd', 'bege', 'ss', 'rs', 'nb']}

        w1stage = hbuf[1][:].rearrange("p t d -> p (t d)")[:, 0:8 * NF].rearrange("p (k n) -> p k n", k=8)
        b.dma(w1stage, w1f.rearrange("(k p) n -> p k n", p=128), w=['h1'])
        for k in range(8):
            b.ts('dve', w1fs[:, k, :], w1stage[:, k, :], gmix[:, k:k + 1], None, ALU.mult, r=['h1', 'gmix'], w=['w1fs'])
        b.ts('dve', w1fs[:, :, 0:128], w1fs[:, :, 0:128], 0.125, None, ALU.mult, r=['w1fs'], w=['w1fs'])
        b.dma(w1stage[:, :, 0:NT], w1t.rearrange("(k p) n -> p k n", p=128), r=['w1fs'], w=['h1'])
        for k in range(8):
            b.ts('dve', w1ts[:, k, :], w1stage[:, k, 0:NT], gmix[:, k:k + 1], None, ALU.mult, r=['h1', 'gmix'], w=['w1ts'])
        b.dma(cw[:], convw[:, :], w=['cw'])
        b.dma(gnws[:], gnw[:, :], w=['gnws'])
        b.memset('pool', Vst[:], 1.0, w=['Vst_%d_%d' % (bb_, t_) for bb_ in range(2) for t_ in range(0, NB, 4)])
        b.memset('pool', carry[:], 0.0, w=['carry'])
        b.memset('pool', fill_rhs[:], 1.0, w=['fill_rhs'])
        b.memset('pool', sel65[:], 0.0, w=['sel65'])
        b.memset('pool', sel65[64:65, :], 1.0, w=['sel65'])
        for bb in range(2):
            for g in range(3):
                b.memset('pool', cin[bb][g][:, 0:3], 0.0, w=['cin%d%d' % (bb, g)])
            b.memset('pool', Sst[bb][:], 0.0, w=['S%d' % bb])
            b.memset('pool', Sb[bb][:], 0.0, w=['Sb%d' % bb])

        nc._dbg = dict(KT=KT, Vst=Vst, cneg=cneg, cend=cend, hnT=hnT1, QT0=QT[0], QT1=QT[1], w1fs=w1fs, w1ts=w1ts, hnb=hnb, rstd=rstd, oTf0=oTf[0], oTg0=oTg[0], kh=kh, qh=qh, vT=vT, og=og, val=val, A0=A_[0], E1=E1, E2=E2, R1=R_[1], beta=sc8['beta'], g=sc8['g'], gc=sc8['gc'], S0=Sst[0])
        nc._p1_free = nc.sbuf_bytes_remaining
        tiles = []
        for t0 in range(0, NB, 4):
            for bb in range(2):
                tiles.append((bb, t0, min(4, NB - t0)))

        psrot = [0]

        def stage_A(ti):
            T.tag = 'A'
            bb, t0, nblk = tiles[ti]
            s = ti % 2
            ntok = nblk * 128
            p0 = t0 * 128
            hk, hTk, qk = 'h%d' % s, 'hnT', 'QT%d' % s
            b.dma(hbuf[s][:, 0:nblk, :], xf[bb, p0:p0 + ntok, :].rearrange("(t p) d -> p t d", p=128), w=[hk])
            b.memset('pool', ssq[:], 0.0, w=['ssq'])
            for t in range(nblk):
                b.act(hnb[:, t, :], hbuf[s][:, t, :], AF.Square, accum_out=ssq[:, t:t + 1], r=[hk], w=['hnb', 'ssq'])
            b.act(rstd[:, 0:nblk], ssq[:, 0:nblk], AF.Ln, bias=EPS, scale=1.0 / D, r=['ssq'], w=['rstd'])
            b.act(rstd[:, 0:nblk], rstd[:, 0:nblk], AF.Exp, scale=-0.5, r=['rstd'], w=['rstd'])
            for t in range(nblk):
                b.ts('dve', hnb[:, t, :], hbuf[s][:, t, :], rstd[:, t:t + 1], None, ALU.mult, r=[hk, 'rstd'], w=['hnb'])
            pst = PS[0][:].bitcast(BF16).rearrange("p (k c) -> p k c", k=8)
            for t in range(nblk):
                for k in range(8):
                    b.tr(pst[:, k, :], hnb[:, t, k * 128:(k + 1) * 128], identb[:], r=['hnb', 'identb'], w=['ps0'])
                b.cp('dve', hnT[s][:, :, t * 128:(t + 1) * 128], pst, r=['ps0'], w=[hTk])
            groups = [(0, 128), (128, 128), (256, 64), (320, 64), (384, 64)]
            for gi, (c0, m) in enumerate(groups):
                pb = 1
                psrot[0] += 1
                pk = 'ps%d' % pb
                for k in range(8):
                    b.mm(PS[pb][0:m, 0:ntok], w1fs[:, k, c0:c0 + m], hnT[s][:, k, 0:ntok], start=(k == 0), stop=(k == 7),
                         r=['w1fs', hTk], w=[pk])
                lo = bb * 64
                if gi == 0:
                    b.cp('dve', QT[s][lo:lo + 64, 0:ntok], PS[pb][lo:lo + 64, 0:ntok], r=[pk], w=[qk])
                elif gi == 1:
                    b.cp('dve', KT[lo:lo + 64, p0:p0 + ntok], PS[pb][lo:lo + 64, 0:ntok], r=[pk], w=['KT_%d_%d' % (bb, t0)])
                else:
                    g = gi - 2
                    b.cp('dve', cin[bb][g][:, 3:3 + ntok], PS[pb][0:64, 0:ntok], r=[pk], w=['cin%d%d' % (bb, g)])
            psv = PS[0][:, 0:260].rearrange("p (t c) -> p t c", t=4)
            for t in range(nblk):
                for k in range(8):
                    b.mm(psv[:, t, :], hnT[s][:, k, t * 128:(t + 1) * 128], w1ts[:, k, 0:65], start=(k == 0), stop=(k == 7),
                         r=['w1ts', hTk], w=['ps0'])
            b.cp('dve', Vst[:, bb, t0:t0 + nblk, 0:64], psv[:, 0:nblk, 0:64], r=['ps0'], w=['Vst_%d_%d' % (bb, t0)])
            b.act(spt[:, 0:nblk], psv[:, 0:nblk, 64], AF.Exp, bias=negfb[:, 0:1], scale=-1.0, r=['ps0', 'negfb'], w=['spt'])
            b.act(spt[:, 0:nblk], spt[:, 0:nblk], AF.Ln, bias=1.0, r=['spt'], w=['spt'])
            b.mm(PS[0][:, 264:264 + nblk], U128[:], spt[:, 0:nblk], r=['U128', 'spt'], w=['ps0'])
            b.mm(PS[0][:, 272:272 + nblk], ones128[:], spt[:, 0:nblk], r=['ones128', 'spt'], w=['ps0'])
            for t in range(nblk):
                prev = carry[:, bb:bb + 1] if t == 0 else cend[:, bb, t0 + t - 1:t0 + t]
                b.tt('dve', cend[:, bb, t0 + t:t0 + t + 1], PS[0][:, 272 + t:273 + t], prev, ALU.add,
                     r=['ps0', 'carry', 'cend_%d_%d' % (bb, t0), 'cend_%d_%d' % (bb, max(t0 - 4, 0))], w=['cend_%d_%d' % (bb, t0)])
                b.tt('dve', cneg[:, bb, t0 + t:t0 + t + 1], PS[0][:, 264 + t:265 + t], prev, ALU.add,
                     r=['ps0', 'carry', 'cend_%d_%d' % (bb, t0), 'cend_%d_%d' % (bb, max(t0 - 4, 0))], w=['cneg_%d_%d' % (bb, t0)])
            b.cp('dve', carry[:, bb:bb + 1], cend[:, bb, t0 + nblk - 1:t0 + nblk], r=['cend_%d_%d' % (bb, t0)], w=['carry'])
            nch = nblk * 2
            for c4 in range(0, nch, 4):
                n4 = min(4, nch - c4)
                pz = PS[1][0:64, 0:264].rearrange("p (n c) -> p n c", n=4)
                for n in range(n4):
                    cn = c4 + n
                    for k in range(8):
                        b.mm(pz[:, n, :], hnT[s][:, k, cn * 64:(cn + 1) * 64], w1ts[:, k, 65:131], start=(k == 0), stop=(k == 7),
                             r=['w1ts', hTk], w=['ps1'])
                b.cp('dve', zsb[s][:, c4:c4 + n4, :], pz[:, 0:n4, 0:64], r=['ps1'], w=['zs%d' % s])
                b.act(beta2[s][:, c4:c4 + n4], pz[:, 0:n4, 64], AF.Exp, scale=-1.0, r=['ps1'], w=['beta%d' % s])
                b.act(g2[s][:, c4:c4 + n4], pz[:, 0:n4, 65], AF.Exp, bias=ppt[0:64, 2:3], r=['ps1', 'ppt'], w=['g%d' % s])

        def stage_B(ti):
            T.tag = 'B'
            bb, t0, nblk = tiles[ti]
            s = ti % 2
            qk = 'QT%d' % s
            lo = bb * 64
            nkb = t0 + nblk
            NQ = nblk * 128
            halves = [(h0, min(2, nblk - h0)) for h0 in range(0, nblk, 2)]
            for hi, (h0, hn_) in enumerate(halves):
                b.ts('dve', biasq[hi][:, 0:nkb], cneg[:, bb, 0:nkb], cend[:, bb, t0 + h0:t0 + h0 + 1], None, ALU.subtract,
                     r=['cneg_%d_%d' % (bb, t_) for t_ in range(0, nkb, 4)] + ['cend_%d_%d' % (bb, t0)], w=['biasq%d' % hi])
            psoT = PS[6][0:65, 0:NQ]
            for kb in range(nkb):
                qlo = max(0, kb - t0)
                sl = 4 + kb % 2
                ps_s = PS[sl][:, 0:512]
                pts = kb % 4
                c0 = qlo * 128
                b.mm(ps_s[:, c0:NQ], KT[lo:lo + 64, kb * 128:(kb + 1) * 128], QT[s][lo:lo + 64, c0:NQ],
                     r=['KT_%d_%d' % (bb, kb // 4 * 4), qk], w=['ps%d' % sl, 'fillpos'])
                if kb % FILL_EVERY == 0:
                    b.mm(PS[3][:, 0:FILL_N], identb[:], fill_rhs[:, 0:FILL_N], r=['identb', 'fill_rhs', 'fillpos'], w=['ps3'])
                for hi, (h0, hn_) in enumerate(halves):
                    a0 = max(c0, h0 * 128)
                    a1 = (h0 + hn_) * 128
                    if a0 >= a1:
                        continue
                    b.act(PT[pts][:, a0:a1], ps_s[:, a0:a1], AF.Exp, bias=biasq[hi][:, kb:kb + 1],
                          r=['ps%d' % sl, 'biasq%d' % hi], w=['PT%d' % pts])
                if qlo > 0:
                    b.memset('pool', PT[pts][:, 0:c0], 0.0, w=['PT%d' % pts])
                if kb >= t0:
                    j = kb - t0
                    b.tt('pool', PT[pts][:, j * 128:(j + 1) * 128], PT[pts][:, j * 128:(j + 1) * 128], trium[:], ALU.mult,
                         r=['PT%d' % pts, 'trium'], w=['PT%d' % pts])
                b.mm(psoT, Vst[:, bb, kb, :], PT[pts][:, 0:NQ], start=(kb == 0), stop=(kb == nkb - 1),
                     r=['PT%d' % pts, 'Vst_%d_%d' % (bb, kb // 4 * 4)], w=['ps6'])
            b.cp('dve', osb[:, 0:NQ], psoT, r=['ps6'], w=['osb'])
            b.mm(PS[6][0:64, 0:NQ], sel65[:, :], osb[:, 0:NQ], r=['osb', 'sel65'], w=['ps6'])
            b.recip(rdn[:, 0:NQ], PS[6][0:64, 0:NQ], r=['ps6'], w=['rdn'])
            b.tt('dve', oTf[s][:, 0:NQ], osb[0:64, 0:NQ], rdn[:, 0:NQ], ALU.mult, r=['osb', 'rdn'], w=['oTf%d' % s])

        def stage_C(ti):
            T.tag = 'C'
            bb, t0, nblk = tiles[ti]
            s = ti % 2
            ntok = nblk * 128
            nch = nblk * 2
            W = nch * 64

            def v3(t):
                return t[:, 0:nch, :]

            def v2(t):
                return t[:].rearrange("p n c -> p (n c)")[:, 0:W]

            for g in range(3):
                ck = 'cin%d%d' % (bb, g)
                b.ts('dve', v2(cy[g]), cin[bb][g][:, 0:W], cw[:, g * 4:g * 4 + 1], None, ALU.mult, r=[ck, 'cw'], w=['cy%d' % g])
                for j in range(1, 4):
                    b.stt('dve', v2(cy[g]), cin[bb][g][:, j:j + W], cw[:, g * 4 + j:g * 4 + j + 1], v2(cy[g]), ALU.mult, ALU.add,
                          r=[ck, 'cw', 'cy%d' % g], w=['cy%d' % g])
                b.cp('pool', cin[bb][g][:, 0:3], cin[bb][g][:, W:W + 3], r=[ck], w=[ck])
                b.act(v2(ex), v2(cy[g]), AF.Exp, scale=-1.0, r=['cy%d' % g], w=['ex'])
                b.ts('dve', v2(ex), v2(ex), 1.0, None, ALU.add, r=['ex'], w=['ex'])
                b.recip(v2(ex), v2(ex), r=['ex'], w=['ex'])
                if g == 2:
                    b.tt('dve', v2(vT), v2(cy[g]), v2(ex), ALU.mult, r=['ex', 'cy2'], w=['vT'])
                else:
                    b.tt('dve', v2(cy[g]), v2(cy[g]), v2(ex), ALU.mult, r=['ex', 'cy%d' % g], w=['cy%d' % g])
                    b.tt('pool', v2(sq), v2(cy[g]), v2(cy[g]), ALU.mult, r=['cy%d' % g], w=['sq'])
                    b.mm(PS[2][0:64, 0:W], ones128[0:64, 0:64], v2(sq), r=['sq', 'ones128'], w=['ps2'])
                    b.act(v2(ex), PS[2][0:64, 0:W], AF.Ln, bias=EPS, r=['ps2'], w=['ex'])
                    b.act(v2(ex), v2(ex), AF.Exp, scale=-0.5, r=['ex'], w=['ex'])
                    if g == 0:
                        b.stt('dve', v2(qh), v2(cy[g]), 0.125, v2(ex), ALU.mult, ALU.mult, r=['cy0', 'ex'], w=['qh'])
                    else:
                        b.tt('dve', v2(kh), v2(cy[g]), v2(ex), ALU.mult, r=['cy1', 'ex'], w=['kh'])
            be, gg = beta2[s], g2[s]
            b.ts('dve', be[:, 0:nch], be[:, 0:nch], 1.0, None, ALU.add, r=['beta%d' % s], w=['beta%d' % s])
            b.recip(be[:, 0:nch], be[:, 0:nch], r=['beta%d' % s], w=['beta%d' % s])
            b.act(gg[:, 0:nch], gg[:, 0:nch], AF.Ln, bias=1.0, r=['g%d' % s], w=['g%d' % s])
            b.ts('dve', gg[:, 0:nch], gg[:, 0:nch], negA[0:64, 0:1], None, ALU.mult, r=['g%d' % s, 'negA'], w=['g%d' % s])
            b.mm(PS[2][0:64, 0:nch], U128[0:64, 0:64], gg[:, 0:nch], r=['g%d' % s, 'U128'], w=['ps2'])
            b.mm(PS[2][0:64, 8:8 + nch], ones128[0:64, 0:64], gg[:, 0:nch], r=['g%d' % s, 'ones128'], w=['ps2'])
            b.cp('dve', sc8['gc'][:, 0:nch], PS[2][0:64, 0:nch], r=['ps2'], w=['gc'])
            b.cp('dve', sc8['gtot'][:, 0:nch], PS[2][0:64, 8:8 + nch], r=['ps2'], w=['gtot'])
            b.act(sc8['egc'][:, 0:nch], sc8['gc'][:, 0:nch], AF.Exp, r=['gc'], w=['egc'])
            b.act(sc8['cd'][:, 0:nch], sc8['gtot'][:, 0:nch], AF.Exp, r=['gtot'], w=['cd'])
            b.tt('dve', sc8['etail'][:, 0:nch], sc8['gtot'][:, 0:nch], sc8['gc'][:, 0:nch], ALU.subtract, r=['gtot', 'gc'], w=['etail'])
            b.act(sc8['etail'][:, 0:nch], sc8['etail'][:, 0:nch], AF.Exp, r=['etail'], w=['etail'])
            b.tt('dve', sc8['bege'][:, 0:nch], be[:, 0:nch], sc8['egc'][:, 0:nch], ALU.mult, r=['beta%d' % s, 'egc'], w=['bege'])
            b.ts('dve', sc8['nb'][:, 0:nch], be[:, 0:nch], -1.0, None, ALU.mult, r=['beta%d' % s], w=['nb'])
            b.tt('dve', v3(sq), bcm(U128[0:64, 0:64], nch), bc3(gg[:, 0:nch], 64), ALU.mult, r=['U128', 'g%d' % s], w=['sq'])
            b.mm(PS[2][0:64, 0:W], ones128[0:64, 0:64], v2(sq), r=['sq', 'ones128'], w=['ps2'])
            ps7v = PS[2][0:64, :].rearrange("p (n c) -> p n c", n=8)[:, 0:nch, :]
            b.tt('dve', v3(Dm), bc3(sc8['gc'][:, 0:nch], 64), ps7v, ALU.subtract, r=['gc', 'ps2'], w=['Dm'])
            b.ts('dve', v3(E1), v3(Dm), 0.0, None, ALU.min, r=['Dm'], w=['E1'])
            b.ts('dve', v3(E2), v3(Dm), 0.0, -1.0, ALU.max, ALU.mult, r=['Dm'], w=['E2'])
            b.act(v2(E1), v2(E1), AF.Exp, r=['E1'], w=['E1'])
            b.act(v2(E2), v2(E2), AF.Exp, r=['E2'], w=['E2'])
            b.tt('pool', v3(E1), v3(E1), m_lows[:, 0:nch, :], ALU.mult, r=['E1', 'm_lows'], w=['E1'])
            b.tt('pool', v3(E1), v3(E1), bc3(sc8['nb'][:, 0:nch], 64), ALU.mult, r=['E1', 'nb'], w=['E1'])
            b.tt('pool', v3(E2), v3(E2), m_upi[:, 0:nch, :], ALU.mult, r=['E2', 'm_upi'], w=['E2'])
            pT = PS[7][0:64, :].bitcast(BF16).rearrange("p (n c) -> p n c", n=8)
            for n in range(nch):
                b.tr(pT[:, n, 0:64], kh[:, n, :], identb[0:64, 0:64], r=['kh', 'identb', 'zs%d' % s], w=['ps7'])
                b.tr(pT[:, n, 64:128], vT[:, n, :], identb[0:64, 0:64], r=['vT', 'identb'], w=['ps7'])
            b.tt('dve', v3(kbg), pT[:, 0:nch, 0:64], bc3(sc8['bege'][:, 0:nch], 64), ALU.mult, r=['ps7', 'bege'], w=['kbg'])
            b.tt('dve', v3(ktail), pT[:, 0:nch, 0:64], bc3(sc8['etail'][:, 0:nch], 64), ALU.mult, r=['ps7', 'etail'], w=['ktail'])
            b.tt('dve', v3(vb), pT[:, 0:nch, 64:128], bc3(be[:, 0:nch], 64), ALU.mult, r=['ps7', 'beta%d' % s], w=['vb'])
            psG = PS[2][0:64, :].rearrange("p (n c) -> p n c", n=8)
            for n in range(nch):
                b.mm(psG[:, n, :], kh[:, n, :], kh[:, n, :], r=['kh'], w=['ps2'])
            b.tt('dve', v3(A_[0]), psG[:, 0:nch, :], v3(E1), ALU.mult, r=['ps2', 'E1'], w=['A0'])
            for n in range(nch):
                b.mm(psG[:, n, :], kh[:, n, :], qh[:, n, :], r=['kh', 'qh'], w=['ps2'])
            b.tt('dve', v3(attnT), psG[:, 0:nch, :], v3(E2), ALU.mult, r=['ps2', 'E2'], w=['attnT'])
            pB = PS[7][0:64, 0:256].bitcast(BF16).rearrange("p (n c) -> p n c", n=8)
            for n in range(nch):
                b.tr(pB[:, n, :], A_[0][:, n, :], identb[0:64, 0:64], r=['A0', 'identb', 'kbg', 'ktail', 'vb'], w=['ps7'])
            b.cp('dve', v3(B_[0]), pB[:, 0:nch, :], r=['ps7'], w=['B0'])
            b.tt('pool', v3(R_[0]), v3(B_[0]), I8[:, 0:nch, :], ALU.add, r=['B0', 'I8'], w=['R0'])
            for lv in range(1, 6):
                ca, pa = lv % 2, (lv - 1) % 2
                for n in range(nch):
                    b.mm(psG[:, n, :], B_[pa][:, n, :], A_[pa][:, n, :], r=['A%d' % pa, 'B%d' % pa], w=['ps2'])
                b.cp('dve', v3(A_[ca]), psG[:, 0:nch, :], r=['ps2'], w=['A%d' % ca])
                b.tt('dve', v3(IA), psG[:, 0:nch, :], I8[:, 0:nch, :], ALU.add, r=['ps2', 'I8'], w=['IA'])
                if lv < 5:
                    psB2 = PS[7][0:64, :].rearrange("p (n c) -> p n c", n=8)
                    for n in range(nch):
                        b.mm(psB2[:, n, :], A_[pa][:, n, :], B_[pa][:, n, :], r=['A%d' % pa, 'B%d' % pa], w=['ps7'])
                    b.cp('dve', v3(B_[ca]), psB2[:, 0:nch, :], r=['ps7'], w=['B%d' % ca])
                for n in range(nch):
                    b.mm(psG[:, n, :], IA[:, n, :], R_[pa][:, n, :], r=['IA', 'R%d' % pa], w=['ps2'])
                b.cp('dve', v3(R_[ca]), psG[:, 0:nch, :], r=['ps2'], w=['R%d' % ca])
            TT = R_[1]
            for n in range(nch):
                b.mm(psG[:, n, :], TT[:, n, :], vb[:, n, :], r=['R1', 'vb'], w=['ps2'])
            b.cp('dve', v3(val), psG[:, 0:nch, :], r=['ps2'], w=['val'])
            psK = PS[7][0:64, :].rearrange("p (n c) -> p n c", n=8)
            for n in range(nch):
                b.mm(psK[:, n, :], kbg[:, n, :], TT[:, n, :], r=['R1', 'kbg'], w=['ps7'])
            b.cp('dve', v3(kcdT), psK[:, 0:nch, :], r=['ps7'], w=['kcdT'])
            sk, sbk = 'S%d' % bb, 'Sb%d' % bb
            for n in range(nch):
                p1 = PS[7][0:64, 0:128]
                p2 = PS[2][0:64, 0:128]
                b.mm(p1[:, 0:64], kcdT[:, n, :], Sb[bb][:], r=['kcdT', sbk], w=['ps7'])
                b.mm(p1[:, 64:128], qh[:, n, :], Sb[bb][:], r=['qh', sbk], w=['ps7'])
                b.tt('dve', vn[:], val[:, n, :], p1[:, 0:64], ALU.subtract, r=['val', 'ps7'], w=['vn'])
                b.ts('dve', qS[:], p1[:, 64:128], sc8['egc'][:, n:n + 1], None, ALU.mult, r=['ps7', 'egc'], w=['qS'])
                b.mm(p2[:, 0:64], attnT[:, n, :], vn[:], r=['attnT', 'vn'], w=['ps2'])
                b.mm(p2[:, 64:128], ktail[:, n, :], vn[:], r=['ktail', 'vn'], w=['ps2'])
                b.tt('dve', og[:, n, :], qS[:], p2[:, 0:64], ALU.add, r=['qS', 'ps2'], w=['og'])
                b.stt('dve', Sst[bb][:], Sst[bb][:], sc8['cd'][:, n:n + 1], p2[:, 64:128], ALU.mult, ALU.add,
                      r=[sk, 'cd', 'ps2'], w=[sk])
                b.cp('dve', Sb[bb][:], Sst[bb][:], r=[sk], w=[sbk])
            b.tt('pool', v3(Dm), v3(og), v3(og), ALU.mult, r=['og'], w=['Dm'])
            b.rsum('dve', sc8['ss'][:, 0:nch], v3(Dm), r=['Dm'], w=['ss'])
            b.act(sc8['rs'][:, 0:nch], sc8['ss'][:, 0:nch], AF.Ln, bias=EPS, scale=1.0 / 64, r=['ss'], w=['rs'])
            b.act(sc8['rs'][:, 0:nch], sc8['rs'][:, 0:nch], AF.Exp, scale=-0.5, r=['rs'], w=['rs'])
            b.tt('dve', v3(og), v3(og), bc3(sc8['rs'][:, 0:nch], 64), ALU.mult, r=['og', 'rs'], w=['og'])
            b.tt('pool', v3(og), v3(og), bcm(gnws[:], nch), ALU.mult, r=['og', 'gnws'], w=['og'])
            b.act(v2(ex), v2(zsb[s]), AF.Exp, scale=-1.0, r=['zs%d' % s], w=['ex'])
            b.ts('dve', v2(ex), v2(ex), 1.0, None, ALU.add, r=['ex'], w=['ex'])
            b.recip(v2(ex), v2(ex), r=['ex'], w=['ex'])
            b.tt('pool', v2(ex), v2(ex), v2(zsb[s]), ALU.mult, r=['ex', 'zs%d' % s], w=['ex'])
            b.tt('dve', v3(ogb), v3(og), v3(ex), ALU.mult, r=['og', 'ex'], w=['ogb'])
            pO = PS[7][0:64, 0:256].bitcast(BF16)
            for n in range(nch):
                b.tr(pO[:, n * 64:(n + 1) * 64], ogb[:, n, :], identb[0:64, 0:64], r=['ogb', 'identb', 'kcdT'], w=['ps7'])
            b.cp('dve', oTg[s][:, 0:W], pO[:, 0:W], r=['ps7'], w=['oTg%d' % s])

        def stage_store(ti):
            T.tag = 'store'
            bb, t0, nblk = tiles[ti]
            s = ti % 2
            for t in range(nblk):
                p = (t0 + t) * 128
                for src, rows, key in ((oTf[s], 0, 'oTf%d' % s), (oTg[s], 64, 'oTg%d' % s)):
                    blk = src[:, t * 128:(t + 1) * 128]
                    if p < 4 * Q:
                        j = p // Q
                        it, col = divmod(p - j * Q, TW)
                        d = ((bb * 4 + j) * NTT + it) * 128 + rows
                        b.dma(a2a_in[d:d + 64, col:col + 128], blk, r=[key], w=['a2a_in'])
                        if j >= 1 and p % Q == 0:
                            d = ((bb * 4 + j - 1) * NTT + NT2) * 128 + rows
                            b.dma(a2a_in[d:d + 64, 0:16], blk[:, 0:16], r=[key], w=['a2a_in'])
                    else:
                        d = ((bb * 4 + 3) * NTT + NT2) * 128 + rows
                        b.dma(a2a_in[d:d + 64, 0:16], blk[:, 0:16], r=[key], w=['a2a_in'])

        ztile = b.sb([128, TW], BF16)
        b.memset('pool', ztile[:], 0.0, w=['ztile'])
        for d_ in range(8):
            r_ = (d_ * NTT + NT2) * 128
            b.dma(a2a_in[r_:r_ + 128, :], ztile[:], r=['ztile'], w=['a2a_in'])
        per_tile = (len(pieces) + len(tiles) - 1) // len(tiles)
        for ti in range(len(tiles)):
            stage_A(ti)
            for pc in (pieces[ti * per_tile:(ti + 1) * per_tile] if part == 0 else []):
                conv_emit(*pc)
            stage_B(ti)
            stage_C(ti)
            stage_store(ti)

    if part == 0:
        NBLK = 8 * NTT
        CH = ag_chunk_blocks(TW)
        for k in range((NBLK + CH - 1) // CH):
            nbk = min(CH, NBLK - k * CH)
            src_ap = a2a_in[k * CH * 128:(k * CH + nbk) * 128, :]
            dst_ap = gath[8 * 128 * CH * k:8 * 128 * CH * k + 8 * nbk * 128, :]
            T.op('pool', lambda e, src_ap=src_ap, dst_ap=dst_ap: e.collective_compute(
                "AllGather", ALU.bypass, replica_groups=[list(range(8))], ins=[src_ap], outs=[dst_ap]),
                r=['a2a_in'], w=['gath'], cc=True, cost=300.0, lat=60000.0)
    T.barrier(lambda e: e.memset(bar_t[:], 0.0))
    p1scope.close()
    p2scope = contextlib.ExitStack()
    b.scope = p2scope
    if part == 2:
        for pc in pieces:
            conv_emit(*pc)
    if part != 1:
        b.off = P1_BASE
        h2 = b.sb([128, 4, D], F32)
        hb2 = b.sb([128, 4, D], BF16)
        hT2 = b.sb([128, 8, 512], BF16)
        oT2 = b.sb([128, 8, 512], BF16)
        gT = b.sb([128, 16, 512], BF16)
        mT = b.sb([128, 8, 512], BF16)
        aT = b.sb([128, 22, 512], BF16)
        uprev = b.sb([128, 44, 2], F32)
        ugL = [b.sb([128, 514], F32) for _ in range(2)]
        uuL = [b.sb([128, 514], F32) for _ in range(2)]
        cgL = [b.sb([128, 512], F32) for _ in range(2)]
        cuL = [b.sb([128, 512], F32) for _ in range(2)]
        yTL = [b.sb([128, 512], F32) for _ in range(2)]
        ug, uu, cg, cu, yT = ugL[0], uuL[0], cgL[0], cuL[0], yTL[0]
        gfin = b.sb([128, D], F32)
        gbs = b.sb([128, 16], F32)
        fcws = b.sb([128, 44, 3], F32)
        fcbs = b.sb([128, 44], F32)
        ss2 = b.sb([128, 8], F32)
        rs2 = b.sb([128, 8], F32)
        NWR = 6
        WR = [b.sb([128, 5632], BF16) for _ in range(NWR)]
        wcnt = [0]

        idxs = b.sb([128, 8 * NTT], mybir.dt.int32)
        b.dma(idxs[:], idx[:, :], w=['idxs'])
        b.dma(gfin[:], g_fin[:, :], w=['gfin'])
        b.dma(gbs[:], gbias[:, :], w=['gbs'])
        b.dma(fcws[:], fcw[:, :, :], w=['fcws'])
        b.dma(fcbs[:], fcb[:, :], w=['fcbs'])
        b.memset('pool', uprev[:], 0.0, w=['uprev'])
        b.memset('pool', h2[:], 0.0, w=['h2'])

        def wload(src, kch, c0, ncols):
            s_ = wcnt[0] % NWR
            wcnt[0] += 1
            v = WR[s_][:, 0:kch * ncols].rearrange("p (k n) -> p k n", k=kch)
            rk = [('W', src.tensor.name, k_, cb_) for k_ in range(kch) for cb_ in range(c0 // 512, (c0 + ncols + 511) // 512)]
            b.dma(v, src[:, c0:c0 + ncols].rearrange("(k p) n -> p k n", p=128), r=rk, w=['WR%d' % s_])
            return v, 'WR%d' % s_

        def norm_T(nblk, ntok, rows_ap_fn, key_h):
            T.tag = 'norm'
            b.memset('pool', ss2[:], 0.0, w=['ss2'])
            for t in range(nblk):
                b.act(hb2[:, t, :], h2[:, t, :], AF.Square, accum_out=ss2[:, t:t + 1], r=[key_h], w=['hb2', 'ss2'])
            b.act(rs2[:, 0:nblk], ss2[:, 0:nblk], AF.Ln, bias=EPS, scale=1.0 / D, r=['ss2'], w=['rs2'])
            b.act(rs2[:, 0:nblk], rs2[:, 0:nblk], AF.Exp, scale=-0.5, r=['rs2'], w=['rs2'])
            pst = PS[0][:].bitcast(BF16).rearrange("p (k c) -> p k c", k=8)
            for t in range(nblk):
                b.ts('dve', hb2[:, t, :], h2[:, t, :], rs2[:, t:t + 1], None, ALU.mult, r=[key_h, 'rs2'], w=['hb2'])
                for k in range(8):
                    b.tr(pst[:, k, :], hb2[:, t, k * 128:(k + 1) * 128], identb[:], r=['hb2', 'identb'], w=['ps0'])
                b.cp('dve', hT2[:, :, t * 128:(t + 1) * 128], pst, r=['ps0'], w=['hT2'])

        def ffn_up(ntok, first_halo):
            T.tag = 'ffn'
            for pi in range(11):
                wv_g, kg = wload(wup_b, 8, pi * 256, 256)
                wv_u, ku = wload(wup_b, 8, DFF + pi * 256, 256)
                for cc in range(2):
                    j = pi * 2 + cc
                    q_ = j % 2
                    ug, uu, cg, cu, yT = ugL[q_], uuL[q_], cgL[q_], cuL[q_], yTL[q_]
                    for half, (wv, kk, dst, pb) in enumerate(((wv_g, kg, ug, 1), (wv_u, ku, uu, 2))):
                        jj = j + 22 * half
                        for k in range(8):
                            b.mm(PS[pb][:, 0:ntok], wv[:, k, cc * 128:(cc + 1) * 128], hT2[:, k, 0:ntok], start=(k == 0), stop=(k == 7),
                                 r=[kk, 'hT2'], w=['ps%d' % pb])
                        dk = ('ug%d' if half == 0 else 'uu%d') % q_
                        b.cp('pool', dst[:, 0:2], uprev[:, jj, :], r=['uprev'], w=[dk])
                        b.cp('dve', dst[:, 2:2 + ntok], PS[pb][:, 0:ntok], r=['ps%d' % pb], w=[dk])
                        b.cp('pool', uprev[:, jj, :], dst[:, ntok:ntok + 2], r=[dk], w=['uprev'])
                        if first_halo:
                            continue
                        co = cg if half == 0 else cu
                        ck = ('cg%d' if half == 0 else 'cu%d') % q_
                        b.ts('dve', co[:, 0:ntok], dst[:, 0:ntok], fcws[:, jj, 0:1], fcbs[:, jj:jj + 1], ALU.mult, ALU.add,
                             r=[dk, 'fcws', 'fcbs'], w=[ck])
                        b.stt('dve', co[:, 0:ntok], dst[:, 1:1 + ntok], fcws[:, jj, 1:2], co[:, 0:ntok], ALU.mult, ALU.add,
                              r=[dk, 'fcws', ck], w=[ck])
                        b.stt('dve', co[:, 0:ntok], dst[:, 2:2 + ntok], fcws[:, jj, 2:3], co[:, 0:ntok], ALU.mult, ALU.add,
                              r=[dk, 'fcws', ck], w=[ck])
                    if first_halo:
                        continue
                    b.act(yT[:, 0:ntok], cg[:, 0:ntok], AF.Sigmoid, r=['cg%d' % q_], w=['yT%d' % q_])
                    b.tt('pool', cg[:, 0:ntok], cg[:, 0:ntok], cu[:, 0:ntok], ALU.mult, r=['cg%d' % q_, 'cu%d' % q_], w=['cg%d' % q_])
                    b.tt('dve', aT[:, j, 0:ntok], cg[:, 0:ntok], yT[:, 0:ntok], ALU.mult, r=['cg%d' % q_, 'yT%d' % q_], w=['aT'])

        def mixer_tail(r0, ntok, nblk):
            T.tag = 'mix'
            np_ = ntok if ntok < 128 else 128
            b.dma(h2[0:np_, 0:nblk, :], x2[r0:r0 + ntok, :].rearrange("(t p) d -> p t d", p=np_), w=['h2'])
            norm_T(nblk, ntok, None, 'h2')
            tix = r0 // TW
            for c in range(8):
                col = tix * 8 + c
                T.op('pool', lambda e, c=c, col=col: e.indirect_dma_start(
                    out=oT2[:, c, 0:TW], out_offset=None, in_=gath[:, :],
                    in_offset=bass.IndirectOffsetOnAxis(ap=idxs[:, col:col + 1], axis=0)),
                    r=['gath', 'idxs'], w=['oT2'], dma=True, cost=1500.0, lat=3000.0)
            for pi in range(4):
                wv, kk = wload(wg_b, 8, pi * 512, 512)
                for cc in range(4):
                    j = pi * 4 + cc
                    pb = 1 + j % 2
                    for k in range(8):
                        b.mm(PS[pb][:, 0:ntok], wv[:, k, cc * 128:(cc + 1) * 128], hT2[:, k, 0:ntok], start=(k == 0), stop=(k == 7),
                             r=[kk, 'hT2'], w=['ps%d' % pb])
                    b.act(gT[:, j, 0:ntok], PS[pb][:, 0:ntok], AF.Sigmoid, bias=gbs[:, j:j + 1], scale=1.0,
                          r=['ps%d' % pb, 'gbs'], w=['gT'])
            wf_, kf = wload(wbf_b, 4, 0, 1024)
            wg_, kg_ = wload(wbg_b, 4, 0, 1024)
            for j in range(8):
                for k in range(4):
                    b.mm(PS[1][:, 0:ntok], wf_[:, k, j * 128:(j + 1) * 128], oT2[:, k, 0:ntok], start=(k == 0), stop=(k == 3),
                         r=[kf, 'oT2'], w=['ps1'])
                for k in range(4):
                    b.mm(PS[2][:, 0:ntok], wg_[:, k, j * 128:(j + 1) * 128], oT2[:, 4 + k, 0:ntok], start=(k == 0), stop=(k == 3),
                         r=[kg_, 'oT2'], w=['ps2'])
                b.tt('dve', yT[:, 0:ntok], PS[1][:, 0:ntok], gT[:, j, 0:ntok], ALU.mult, r=['ps1', 'gT'], w=['yT0'])
                b.tt('dve', cg[:, 0:ntok], PS[2][:, 0:ntok], gT[:, 8 + j, 0:ntok], ALU.mult, r=['ps2', 'gT'], w=['cg0'])
                b.tt('pool', mT[:, j, 0:ntok], yT[:, 0:ntok], cg[:, 0:ntok], ALU.add, r=['yT0', 'cg0'], w=['mT'])
            for hc in range(2):
                wv, kk = wload(wo_b, 8, hc * 512, 512)
                for t in range(nblk):
                    nt_ = min(128, ntok - t * 128)
                    pb = 1 + t % 2
                    for k in range(8):
                        b.mm(PS[pb][0:nt_, :], mT[:, k, t * 128:t * 128 + nt_], wv[:, k, :], start=(k == 0), stop=(k == 7),
                             r=[kk, 'mT'], w=['ps%d' % pb])
                    b.tt('dve', h2[0:nt_, t, hc * 512:(hc + 1) * 512], h2[0:nt_, t, hc * 512:(hc + 1) * 512], PS[pb][0:nt_, :], ALU.add,
                         r=['ps%d' % pb, 'h2'], w=['h2'])


        nc._p2_free = nc.sbuf_bytes_remaining
        p2tiles = [(it * TW, TW) for it in range(NT2)] + [(Q, 16)]
        for r0, ntok in p2tiles:
            nblk = (ntok + 127) // 128
            mixer_tail(r0, ntok, nblk)
            norm_T(nblk, ntok, None, 'h2')
            ffn_up(ntok, False)
            T.tag = 'down'
            for hc in range(4):
                wv, kk = wload(wdn_b, 22, hc * 256, 256)
                for t in range(nblk):
                    nt_ = min(128, ntok - t * 128)
                    pb = 1 + t % 2
                    for k in range(22):
                        b.mm(PS[pb][0:nt_, 0:256], aT[:, k, t * 128:t * 128 + nt_], wv[:, k, :], start=(k == 0), stop=(k == 21),
                             r=[kk, 'aT'], w=['ps%d' % pb])
                    b.tt('dve', h2[0:nt_, t, hc * 256:(hc + 1) * 256], h2[0:nt_, t, hc * 256:(hc + 1) * 256], PS[pb][0:nt_, 0:256], ALU.add,
                         r=['ps%d' % pb, 'h2'], w=['h2'])
            b.memset('pool', ss2[:], 0.0, w=['ss2'])
            for t in range(nblk):
                b.act(hb2[:, t, :], h2[:, t, :], AF.Square, accum_out=ss2[:, t:t + 1], r=['h2'], w=['hb2', 'ss2'])
            b.act(rs2[:, 0:nblk], ss2[:, 0:nblk], AF.Ln, bias=EPS, scale=1.0 / D, r=['ss2'], w=['rs2'])
            b.act(rs2[:, 0:nblk], rs2[:, 0:nblk], AF.Exp, scale=-0.5, r=['rs2'], w=['rs2'])
            for t in range(nblk):
                b.stt('dve', h2[:, t, :], h2[:, t, :], rs2[:, t:t + 1], gfin[:], ALU.mult, ALU.mult, r=['h2', 'rs2', 'gfin'], w=['h2'])
            if r0 == 0:
                b.dma(out[0:112, :], h2[16:128, 0, :], r=['h2'], w=['out'])
                if nblk > 1:
                    b.dma(out[112:ntok - 16, :].rearrange("(t p) d -> p t d", p=128), h2[:, 1:nblk, :], r=['h2'], w=['out'])
            elif ntok == TW:
                b.dma(out[r0 - 16:r0 - 16 + TW, :].rearrange("(t p) d -> p t d", p=128), h2[:, 0:nblk, :], r=['h2'], w=['out'])
            else:
                b.dma(out[Q - 16:Q, :], h2[0:16, 0, :], r=['h2'], w=['out'])
        p2scope.close()

    fkeys = ['out'] if part != 1 else ['a2a_in']
    T.finalize(final_wait_keys=fkeys)
    with contextlib.ExitStack() as st:
        esem = {e: st.enter_context(nc.semaphore("s_" + e)) for e in ENGS}
        dsems = {rg: [st.enter_context(nc.semaphore("d%s%d" % (rg, i))) for i in range(NDMA_SEMS)] for rg in ('s', 'p')}
        ccsem = [st.enter_context(nc.semaphore("ccsem%d" % i)) for i in range(max(1, T.ncc))]
        st.enter_context(nc.allow_low_precision("bf16 matmul operands by design, fp32 accumulation"))
        block = st.enter_context(nc.Block())
        T.emit(esem, dsems, ccsem, block)
    outer.close()
    nc._stats = (T.makespan, T.busy, T.nops)
    return nc


def make_in_maps(inp, SEQ, fused=True):
    L, NB, LP, Q, QH = cfg(SEQ)
    f = np.float32
    x = np.asarray(inp["x"], f)
    meta = np.asarray(inp["meta_tokens"], f)
    xf = np.zeros((2, LP, D), f)
    xf[:, 0:NMETA] = meta[None]
    xf[:, NMETA:L] = x
    w_in = np.asarray(inp["w_in"], f)[0]
    conv_w = np.asarray(inp["gdn_conv_w"], f)[0]
    gm = np.asarray(inp["norm_mix_w"], f)[0]
    gf = np.asarray(inp["norm_ffn_w"], f)[0]
    shared = dict(
        g_mix=np.ascontiguousarray(gm.reshape(8, 128).T),
        g_ffn=np.ascontiguousarray(gf.reshape(8, 128).T),
        g_fin=np.ascontiguousarray(np.broadcast_to(np.asarray(inp["norm_final_w"], f)[None, :], (128, D))),
        wg=np.ascontiguousarray(w_in[:, 3608:3608 + 2048]),
        gbias=np.ascontiguousarray(np.asarray(inp["gate_bias"], f)[0].reshape(16, 128).T),
        wbf=np.ascontiguousarray(np.asarray(inp["w_branch_fox"], f)[0]),
        wbg=np.ascontiguousarray(np.asarray(inp["w_branch_gdn"], f)[0]),
        wo=np.ascontiguousarray(np.asarray(inp["w_out"], f)[0]),
        wup=np.ascontiguousarray(np.asarray(inp["ffn_w_up"], f)[0]),
        fcw=np.ascontiguousarray(np.asarray(inp["ffn_conv_w"], f)[0].T.reshape(44, 128, 3).transpose(1, 0, 2)),
        fcb=np.ascontiguousarray(np.asarray(inp["ffn_conv_b"], f)[0].reshape(44, 128).T),
        wdn=np.ascontiguousarray(np.asarray(inp["ffn_w_down"], f)[0]),
        gnw=np.ascontiguousarray(np.broadcast_to(np.asarray(inp["gdn_norm_w"], f)[0][None, :], (64, 64))),
        xf=xf,
    )
    maps = []
    for c in range(8):
        h = c
        fq = w_in[:, h * 64:(h + 1) * 64]
        fk = w_in[:, 512 + h * 64:512 + (h + 1) * 64]
        fv = w_in[:, 1024 + h * 64:1024 + (h + 1) * 64]
        ff = w_in[:, 1536 + h:1537 + h]
        g0 = 1544
        gq = w_in[:, g0 + h * 64:g0 + (h + 1) * 64]
        gk = w_in[:, g0 + 512 + h * 64:g0 + 512 + (h + 1) * 64]
        gv = w_in[:, g0 + 1024 + h * 64:g0 + 1024 + (h + 1) * 64]
        z = w_in[:, 3080 + h * 64:3080 + (h + 1) * 64]
        bcol = w_in[:, 3592 + h:3593 + h]
        acol = w_in[:, 3600 + h:3601 + h]
        w1f = np.ascontiguousarray(np.concatenate([fq, fq, fk, fk, gq, gk, gv], axis=1))
        w1t = np.ascontiguousarray(np.concatenate([fv, ff, z, bcol, acol], axis=1))
        pp = np.zeros((128, 8), f)
        pp[:, 0] = np.asarray(inp["fgt_bias"], f)[0, h]
        pp[:, 1] = np.asarray(inp["gdn_a_log"], f)[0, h]
        pp[:, 2] = np.asarray(inp["gdn_dt_bias"], f)[0, h]
        cwm = np.zeros((64, 12), f)
        for g in range(3):
            cwm[:, g * 4:(g + 1) * 4] = conv_w[:, g * 512 + h * 64:g * 512 + (h + 1) * 64].T
        bb, j = divmod(c, 4)
        x2 = np.ascontiguousarray(xf[bb, j * Q:j * Q + QH])
        TW = min(512, Q)
        NTT = Q // TW + 1
        idx = np.zeros((128, 8 * NTT), np.int32)
        pa = np.arange(128)
        for it in range(NTT):
            for ch in range(8):
                half, cc = divmod(ch, 4)
                src = 2 * cc + pa // 64
                blk = c * NTT + it
                if fused:
                    base = np.array([gath_row(int(s_), blk, 8 * NTT, ag_chunk_blocks(TW)) for s_ in src])
                else:
                    base = (src * NTT + it) * 128
                idx[:, it * 8 + ch] = base + half * 64 + pa % 64
        m = dict(shared)
        m.update(w1f=w1f, w1t=w1t, pp=pp, convw=cwm, x2=x2, idx=idx)
        maps.append(m)
    return maps


_CACHE = {}


FUSED = False
P1KEYS = ["xf", "w1f", "w1t", "pp", "convw", "gnw", "g_mix"]
P2KEYS = ["x2", "g_mix", "g_ffn", "g_fin", "wg", "gbias", "wbf", "wbg", "wo", "wup", "fcw", "fcb", "wdn", "idx"]


def run(inp, SEQ, fused=None):
    fused = FUSED if fused is None else fused
    L, NB, LP, Q, QH = cfg(SEQ)
    TW = min(512, Q)
    NTT = Q // TW + 1
    maps = make_in_maps(inp, SEQ, fused)
    if fused:
        if (SEQ, 0) not in _CACHE:
            _CACHE[(SEQ, 0)] = build(SEQ, 0)
        res = run_bass_kernel_spmd(_CACHE[(SEQ, 0)], maps, core_ids=list(range(8)))
    else:
        for part in (1, 2):
            if (SEQ, part) not in _CACHE:
                _CACHE[(SEQ, part)] = build(SEQ, part)
        res1 = run_bass_kernel_spmd(_CACHE[(SEQ, 1)], [{k: m[k] for k in P1KEYS} for m in maps], core_ids=list(range(8)))
        sh = [np.asarray(res1.results[c]["a2a_in"]) for c in range(8)]
        maps2 = []
        for c in range(8):
            m = {k: maps[c][k] for k in P2KEYS}
            m["gath"] = np.ascontiguousarray(np.concatenate(
                [sh[s_][c * NTT * 128:(c + 1) * NTT * 128] for s_ in range(8)], axis=0))
            maps2.append(m)
        res = run_bass_kernel_spmd(_CACHE[(SEQ, 2)], maps2, core_ids=list(range(8)))
    out = np.zeros((2, SEQ, D), np.float32)
    for c in range(8):
        bb, j = divmod(c, 4)
        out[bb, j * Q:(j + 1) * Q] = res.results[c]["out"]
    return out, res


def kernel(**inputs):
    SEQ = inputs["x"].shape[1]
    out, _ = run(inputs, SEQ)
    return out
```

```python
import contextlib
import numpy as np
import concourse.bass as bass
import concourse.mybir as mybir
from concourse.bass_utils import run_bass_kernel_spmd

F32 = mybir.dt.float32
BF16 = mybir.dt.bfloat16
AF = mybir.ActivationFunctionType
ALU = mybir.AluOpType
AX = mybir.AxisListType

D = 1024
NMETA = 16
DFF = 2816
EPS = 1e-6
SELF_SYNC = True
NDMA_SEMS = 24
FILL_EVERY = 10 ** 9
FILL_N = 512
HOP_NS = 380.0
ENGS = ['pe', 'act', 'dve', 'pool', 'sp']


class Tracker:
    def __init__(self):
        self.ops = []
        self.lastw = {}
        self.readers = {}
        self.ndma = {'s': 0, 'p': 0}
        self.bar = set()
        self.ncc = 0
        self.tag = ''

    def op(self, eng, fn, r=(), w=(), dma=False, cc=False, cost=100.0, lat=0.0):
        i = len(self.ops)
        w = list(w) + [k for k in r if isinstance(k, str) and k.startswith('ps') and k[2:].isdigit()]
        deps = set(self.bar)
        for k in r:
            j = self.lastw.get(k)
            if j is not None:
                deps.add(j)
        for k in w:
            j = self.lastw.get(k)
            if j is not None:
                deps.add(j)
            deps.update(self.readers.get(k, ()))
        d = None
        if dma:
            ring = 'p' if eng == 'pool' else 's'
            d = (ring, -1)
        ccid = None
        if cc:
            ccid = self.ncc
            self.ncc += 1
        self.ops.append(dict(eng=eng, fn=fn, deps=deps, dma=d, users=False, cc=cc, ccid=ccid, cost=cost, lat=lat, bar=False, tag=self.tag))
        for k in r:
            self.readers.setdefault(k, []).append(i)
        for k in w:
            self.lastw[k] = i
            self.readers[k] = []
        return i

    def barrier(self, fn):
        self.bar = set()
        i = self.op('pool', fn, (), ())
        self.ops[i]['bar'] = True
        self.bar = {i}
        return i

    def finalize(self, final_wait_keys=()):
        import heapq
        ops = self.ops
        n = len(ops)
        fin = set()
        for k in final_wait_keys:
            j = self.lastw.get(k)
            if j is not None:
                fin.add(j)
        succ = [[] for _ in range(n)]
        npend = [0] * n
        for i, o in enumerate(ops):
            dd = range(i) if o['bar'] else o['deps']
            npend[i] = len(dd)
            for j in dd:
                succ[j].append(i)
        ready = [0.0] * n
        start = [0.0] * n
        t_e = {e: 0.0 for e in ENGS}
        fut = {e: [] for e in ENGS}
        avail = {e: [] for e in ENGS}
        for i, o in enumerate(ops):
            if npend[i] == 0:
                heapq.heappush(fut[o['eng']], (0.0, i))
        done = 0
        last_on = {}
        t_prev_end = {}
        rby = [None] * n
        while done < n:
            best = None
            for e in ENGS:
                f, a_ = fut[e], avail[e]
                while f and f[0][0] <= t_e[e]:
                    heapq.heappush(a_, heapq.heappop(f)[1])
                if a_:
                    cand = (t_e[e], a_[0], e, True)
                elif f:
                    cand = (f[0][0], f[0][1], e, False)
                else:
                    continue
                if best is None or cand[:2] < best[:2]:
                    best = cand
            st, i, e, from_avail = best
            if from_avail:
                heapq.heappop(avail[e])
            else:
                heapq.heappop(fut[e])
            o = ops[i]
            start[i] = st
            o['by_eng'] = (st <= t_prev_end.get(e, 0.0) + 1e-9 and e in last_on) and last_on.get(e)
            last_on[e] = i
            t_prev_end[e] = st + o['cost']
            t_e[e] = st + o['cost']
            fi = st + o['cost'] + o['lat']
            for s_ in succ[i]:
                os_ = ops[s_]
                rt = fi + (60.0 if (os_['eng'] == e and o['dma'] is None and not o['cc']) else HOP_NS)
                if rt > ready[s_]:
                    ready[s_] = rt
                    rby[s_] = i
                npend[s_] -= 1
                if npend[s_] == 0:
                    heapq.heappush(fut[os_['eng']], (ready[s_], s_))
            done += 1
        self.makespan = max(t_e.values())
        self.start = start
        self.rby = rby
        self.busy = {e: sum(o['cost'] for o in ops if o['eng'] == e) for e in ENGS}
        self.nops = {e: sum(1 for o in ops if o['eng'] == e) for e in ENGS}
        self.per_eng = {e: [] for e in ENGS}
        for i in sorted(range(n), key=lambda i: (start[i], i)):
            self.per_eng[ops[i]['eng']].append(i)
        pos = {}
        for e in ENGS:
            for p_, i in enumerate(self.per_eng[e]):
                pos[i] = p_
        for i, o in enumerate(ops):
            if not o['bar']:
                continue
            red = set()
            for e in ENGS:
                pre = [j for j in self.per_eng[e] if j < i and ops[j]['dma'] is None and not ops[j]['cc']]
                if pre:
                    red.add(pre[-1])
                for ring in ('s', 'p'):
                    pd = [j for j in self.per_eng[e] if j < i and ops[j]['dma'] is not None and ops[j]['dma'][0] == ring]
                    red.update(pd[-NDMA_SEMS:])
            red.update(j for j in range(i) if ops[j]['cc'])
            o['deps'] = red
        for ring, e in (('s', 'sp'), ('p', 'pool')):
            dma_ops = [i for i in self.per_eng[e] if ops[i]['dma'] is not None]
            assert all(ops[i]['dma'][0] == ring for i in dma_ops)
            for n_, i in enumerate(dma_ops):
                ops[i]['dma'] = (ring, n_)
                if n_ >= NDMA_SEMS:
                    ops[i]['deps'].add(dma_ops[n_ - NDMA_SEMS])
        for e in ENGS:
            assert e in ('sp', 'pool') or all(ops[i]['dma'] is None for i in self.per_eng[e])
        for i, o in enumerate(ops):
            for j in o['deps']:
                dj = ops[j]
                if dj['dma'] is not None or dj['cc']:
                    continue
                if dj['eng'] == o['eng'] and o['dma'] is None and not o['cc']:
                    if o['eng'] == 'pe' or not SELF_SYNC:
                        continue
                dj['users'] = True
        for j in fin:
            if ops[j]['dma'] is None and not ops[j]['cc']:
                ops[j]['users'] = True
        for e in ENGS:
            c_ = 0
            for i in self.per_eng[e]:
                o = ops[i]
                if o['dma'] is None and not o['cc'] and o['users']:
                    c_ += 1
                    o['val'] = c_
        self.fin = fin

    def emit(self, esem, dsems, ccsem, block):
        ops = self.ops

        def target(j):
            dj = ops[j]
            if dj['cc']:
                return ('c', dj['ccid']), ccsem[dj['ccid']], 1
            if dj['dma'] is not None:
                ring, d = dj['dma']
                return ('d', ring, d % NDMA_SEMS), dsems[ring][d % NDMA_SEMS], 16 * (d // NDMA_SEMS + 1)
            return ('e', dj['eng']), esem[dj['eng']], dj['val']

        def run(engname, eng):
            known = {}
            for i in self.per_eng[engname]:
                o = ops[i]
                waits = {}
                for j in o['deps']:
                    dj = ops[j]
                    if (dj['dma'] is None and not dj['cc'] and dj['eng'] == engname
                            and o['dma'] is None and not o['cc']):
                        if engname == 'pe' or not SELF_SYNC:
                            continue
                    key, sem, val = target(j)
                    if known.get(key, 0) >= val:
                        continue
                    if key not in waits or waits[key][1] < val:
                        waits[key] = (sem, val)
                for key, (sem, val) in waits.items():
                    eng.wait_ge(sem, val)
                    known[key] = val
                ins = o['fn'](eng)
                if o['cc']:
                    ins.then_inc(ccsem[o['ccid']])
                elif o['dma'] is not None:
                    ins.then_inc(dsems[o['dma'][0]][o['dma'][1] % NDMA_SEMS], 16)
                elif o['users']:
                    ins.then_inc(esem[engname], 1)
            if engname == 'sp':
                for j in self.fin:
                    key, sem, val = target(j)
                    if known.get(key, 0) >= val:
                        continue
                    eng.wait_ge(sem, val)
                    known[key] = val

        block.tensor(lambda e: run('pe', e))
        block.scalar(lambda e: run('act', e))
        block.vector(lambda e: run('dve', e))
        block.gpsimd(lambda e: run('pool', e))
        block.sync(lambda e: run('sp', e))


class B:
    def __init__(self, nc):
        self.nc = nc
        self.T = Tracker()
        self.off = 0
        self.n = 0
        self.scope = None

    def sb(self, shape, dt):
        t = self.scope.enter_context(self.nc.sbuf_tensor("sb%d" % self.n, list(shape), dt))
        self.n += 1
        return t

    @staticmethod
    def _n(ap):
        sh = ap.shape
        n = 1
        for x in sh[1:]:
            n *= int(x)
        return n

    def _c(self, eng, out, in_=None):
        n = self._n(out)
        if eng == 'act':
            return 155.0 + n * 0.835
        if eng == 'dve':
            f = 0.96
            return 60.0 + n / f
        if eng == 'pool':
            return 100.0 + n / 0.6
        return 100.0

    def mm(self, out, lhsT, rhs, start=True, stop=True, r=(), w=()):
        n = self._n(rhs)
        m = self._n(lhsT)
        f4 = 4.0 if rhs.dtype == F32 else 1.0
        c = 10.0 + m / 1.2 * (2.0 if f4 > 1 else 1.0) + (n / 2.4) * f4
        return self.T.op('pe', lambda e: e.matmul(out, lhsT=lhsT, rhs=rhs, start=start, stop=stop), r, w, cost=c, lat=0.0)

    def tr(self, out, in_, ident, r=(), w=()):
        return self.T.op('pe', lambda e: e.transpose(out=out, in_=in_, identity=ident), r, w, cost=10.0 + 128 / 1.2 + self._n(in_) / 2.4, lat=0.0)

    def act(self, out, in_, func, bias=0.0, scale=1.0, accum_out=None, r=(), w=()):
        c = self._c('act', out)
        if accum_out is None:
            return self.T.op('act', lambda e: e.activation(out=out, in_=in_, func=func, bias=bias, scale=scale), r, w, cost=c)
        return self.T.op('act', lambda e: e.activation(out=out, in_=in_, func=func, bias=bias, scale=scale,
                                                       accum_out=accum_out), r, w, cost=c)

    def ts(self, eng, out, in0, s1, s2, op0, op1=None, r=(), w=()):
        c = self._c(eng, out)
        if op1 is None:
            return self.T.op(eng, lambda e: e.tensor_scalar(out=out, in0=in0, scalar1=s1, scalar2=None, op0=op0), r, w, cost=c)
        return self.T.op(eng, lambda e: e.tensor_scalar(out=out, in0=in0, scalar1=s1, scalar2=s2, op0=op0, op1=op1), r, w, cost=c)

    def tt(self, eng, out, in0, in1, op, r=(), w=()):
        return self.T.op(eng, lambda e: e.tensor_tensor(out=out, in0=in0, in1=in1, op=op), r, w, cost=self._c(eng, out))

    def stt(self, eng, out, in0, scalar, in1, op0, op1, r=(), w=()):
        return self.T.op(eng, lambda e: e.scalar_tensor_tensor(out=out, in0=in0, scalar=scalar, in1=in1,
                                                               op0=op0, op1=op1), r, w, cost=self._c(eng, out))

    def cp(self, eng, out, in_, r=(), w=()):
        return self.T.op(eng, lambda e: e.tensor_copy(out=out, in_=in_), r, w, cost=self._c(eng, out))

    def recip(self, out, in_, r=(), w=()):
        return self.T.op('dve', lambda e: e.reciprocal(out=out, in_=in_), r, w, cost=self._c('dve', out))

    def memset(self, eng, ap, val, w=()):
        return self.T.op(eng, lambda e: e.memset(ap, val), (), w, cost=self._c(eng, ap))

    def asel(self, out, in_, pattern, cmp, fill, base, cm, r=(), w=()):
        return self.T.op('pool', lambda e: e.affine_select(out=out, in_=in_, pattern=pattern, compare_op=cmp,
                                                           fill=fill, base=base, channel_multiplier=cm), r, w, cost=self._c('pool', out))

    def rsum(self, eng, out, in_, r=(), w=()):
        return self.T.op(eng, lambda e: e.reduce_sum(out=out, in_=in_, axis=AX.X), r, w, cost=self._c(eng, in_))

    def dma(self, out, in_, r=(), w=(), eng='sp'):
        sh = out.shape
        nbytes = 1
        for x in sh:
            nbytes *= int(x)
        nbytes *= 4 if out.dtype == F32 else 2
        return self.T.op(eng, lambda e: e.dma_start(out=out, in_=in_), r, w, dma=True, cost=60.0, lat=2000.0 + nbytes / 60.0)


def bc3(ap2, n):
    return ap2.unsqueeze(2).to_broadcast([ap2.shape[0], ap2.shape[1], n])


def bcm(ap2, a):
    return ap2.unsqueeze(1).to_broadcast([ap2.shape[0], a, ap2.shape[1]])


def ag_chunk_blocks(TW):
    import os
    return max(1, (int(os.environ.get("AGKB", "768")) * 1024) // (128 * TW * 2))


def gath_row(src, blk, NBLK, CH):
    k, w = divmod(blk, CH)
    nbk = min(CH, NBLK - k * CH)
    return 8 * 128 * CH * k + src * (nbk * 128) + w * 128


def cfg(SEQ):
    L = SEQ + NMETA
    NB = (L + 127) // 128
    LP = NB * 128
    Q = SEQ // 4
    QH = Q + 16
    return L, NB, LP, Q, QH


NF = 128 + 128 + 64 * 3
NT = 65 + 64 + 2
UP_PIECES = [(i * 512, 512) for i in range(11)]


def build(SEQ, part=0, debug=False):
    L, NB, LP, Q, QH = cfg(SEQ)
    TW = min(512, Q)
    NT2 = Q // TW
    NTT = NT2 + 1
    assert Q % 128 == 0 and Q % TW == 0
    nc = bass.Bass("TRN2", target_bir_lowering=False)

    P1IN = {"xf", "w1f", "w1t", "pp", "convw", "gnw", "g_mix"}
    P2IN = {"x2", "g_mix", "g_ffn", "g_fin", "wg", "gbias", "wbf", "wbg", "wo", "wup", "fcw", "fcb", "wdn"}

    def din(name, shape):
        if part == 1 and name not in P1IN:
            return None
        if part == 2 and name not in P2IN:
            return None
        return nc.dram_tensor(name, list(shape), F32, kind="ExternalInput").ap()

    xf = din("xf", [2, LP, D])
    x2 = din("x2", [QH, D])
    w1f = din("w1f", [D, NF])
    w1t = din("w1t", [D, NT])
    pp = din("pp", [128, 8])
    convw = din("convw", [64, 12])
    gnw = din("gnw", [64, 64])
    g_mix = din("g_mix", [128, 8])
    g_ffn = din("g_ffn", [128, 8])
    g_fin = din("g_fin", [128, D])
    wg = din("wg", [D, 2 * D])
    gbias = din("gbias", [128, 16])
    wbf = din("wbf", [512, D])
    wbg = din("wbg", [512, D])
    wo = din("wo", [D, D])
    wup = din("wup", [D, 2 * DFF])
    fcw = din("fcw", [128, 44, 3])
    fcb = din("fcb", [128, 44])
    wdn = din("wdn", [DFF, D])
    if part != 1:
        out = nc.dram_tensor("out", [Q, D], F32, kind="ExternalOutput").ap()
        idx = nc.dram_tensor("idx", [128, 8 * NTT], mybir.dt.int32, kind="ExternalInput").ap()

    if part == 1:
        a2a_in = nc.dram_tensor("a2a_in", [8 * NTT * 128, TW], BF16, kind="ExternalOutput").ap()
    elif part == 2:
        gath = nc.dram_tensor("gath", [8 * NTT * 128, TW], BF16, kind="ExternalInput").ap()
    else:
        a2a_in = nc.dram_tensor("a2a_in", [8 * NTT * 128, TW], BF16).ap()
        gath = nc.dram_tensor("gath", [8 * 8 * NTT * 128, TW], BF16).ap()
    wg_b = nc.dram_tensor("wg_b", [D, 2 * D], BF16).ap()
    wbf_b = nc.dram_tensor("wbf_b", [512, D], BF16).ap()
    wbg_b = nc.dram_tensor("wbg_b", [512, D], BF16).ap()
    wo_b = nc.dram_tensor("wo_b", [D, D], BF16).ap()
    wup_b = nc.dram_tensor("wup_b", [D, 2 * DFF], BF16).ap()
    wdn_b = nc.dram_tensor("wdn_b", [DFF, D], BF16).ap()

    b = B(nc)
    T = b.T
    outer = contextlib.ExitStack()
    b.scope = outer
    PS = [nc.alloc_psum_tensor("ps%d" % i, [128, 512], F32) for i in range(8)]
    PSB = [nc.alloc_psum_tensor("psb%d" % i, [128, 1024], BF16) for i in range(0)]

    identb = b.sb([128, 128], BF16)
    trium = b.sb([128, 128], BF16)
    U128 = b.sb([128, 128], F32)
    ones128 = b.sb([128, 128], F32)
    m_upi = b.sb([64, 8, 64], F32)
    m_lows = b.sb([64, 8, 64], F32)
    I8 = b.sb([64, 8, 64], BF16)
    tmpf = b.sb([128, 128], F32)
    ppt = b.sb([128, 8], F32)
    negfb = b.sb([128, 1], F32)
    negA = b.sb([128, 1], F32)
    gmix = b.sb([128, 8], F32)
    gffn = b.sb([128, 8], F32)
    bar_t = b.sb([128, 1], F32)
    CONST_END = b.off

    b.memset('pool', tmpf[:], 1.0, w=['tmpf'])
    b.asel(tmpf[:], tmpf[:], [[-1, 128]], ALU.is_equal, 0.0, 0, 1, r=['tmpf'], w=['tmpf'])
    b.cp('dve', identb[:], tmpf[:], r=['tmpf'], w=['identb'])
    b.memset('pool', ones128[:], 1.0, w=['ones128'])
    b.asel(U128[:], ones128[:], [[1, 128]], ALU.is_ge, 0.0, 0, -1, r=['ones128'], w=['U128'])
    b.cp('dve', trium[:], U128[:], r=['U128'], w=['trium'])
    b.memset('pool', m_upi[:], 1.0, w=['m_upi'])
    b.asel(m_upi[:], m_upi[:], [[0, 8], [1, 64]], ALU.is_ge, 0.0, 0, -1, r=['m_upi'], w=['m_upi'])
    b.memset('pool', m_lows[:], 1.0, w=['m_lows'])
    b.asel(m_lows[:], m_lows[:], [[0, 8], [-1, 64]], ALU.is_gt, 0.0, 0, 1, r=['m_lows'], w=['m_lows'])
    i8f = tmpf[0:64, :].rearrange("p (a c) -> p a c", a=2)
    b.memset('pool', tmpf[:], 1.0, w=['tmpf'])
    b.asel(i8f, i8f, [[0, 2], [-1, 64]], ALU.is_equal, 0.0, 0, 1, r=['tmpf'], w=['tmpf'])
    for a in range(4):
        b.cp('dve', I8[:, 2 * a:2 * a + 2, :], i8f, r=['tmpf'], w=['I8'])
    if part != 2:
        b.dma(ppt[:], pp[:, :], w=['ppt'])
    else:
        b.memset('pool', ppt[:], 0.0, w=['ppt'])
    if part != 1:
        b.dma(gffn[:], g_ffn[:, :], w=['gffn'])
    b.dma(gmix[:], g_mix[:, :], w=['gmix'])
    b.ts('dve', negfb[:], ppt[:, 0:1], -1.0, None, ALU.mult, r=['ppt'], w=['negfb'])
    b.act(negA[:], ppt[:, 1:2], AF.Exp, r=['ppt'], w=['negA'])
    b.ts('dve', negA[:], negA[:], -1.0, None, ALU.mult, r=['negA'], w=['negA'])

    P1_BASE = b.off
    NSTG = 6 if part == 2 else 2
    stg = [b.sb([128, 512], F32) for _ in range(NSTG)] if part != 1 else None
    stb = [b.sb([128, 512], BF16) for _ in range(NSTG)] if part != 1 else None
    cnt = [0]

    pieces = []

    def conv_piece(src, dst, r0, c0, nc_, gain):
        pieces.append((src, dst, r0, c0, nc_, gain))

    def conv_emit(src, dst, r0, c0, nc_, gain):
        T.tag = 'conv'
        s = cnt[0] % NSTG
        cnt[0] += 1
        b.dma(stg[s][:, 0:nc_], src[r0:r0 + 128, c0:c0 + nc_], w=['stg%d' % s])
        ceng = 'dve' if part == 2 else 'pool'
        if gain is None:
            b.cp(ceng, stb[s][:, 0:nc_], stg[s][:, 0:nc_], r=['stg%d' % s], w=['stb%d' % s])
        else:
            b.ts(ceng, stb[s][:, 0:nc_], stg[s][:, 0:nc_], gain, None, ALU.mult,
                 r=['stg%d' % s, 'gmix', 'gffn'], w=['stb%d' % s])
        b.dma(dst[r0:r0 + 128, c0:c0 + nc_], stb[s][:, 0:nc_], r=['stb%d' % s], w=[('W', dst.tensor.name, r0 // 128, c0 // 512)])

    for k in range(8):
        for c0 in range(0, 2048, 512):
            conv_piece(wg, wg_b, k * 128, c0, 512, gmix[:, k:k + 1])
    for k in range(4):
        for c0 in (0, 512):
            conv_piece(wbf, wbf_b, k * 128, c0, 512, None)
            conv_piece(wbg, wbg_b, k * 128, c0, 512, None)
    for k in range(8):
        for c0 in (0, 512):
            conv_piece(wo, wo_b, k * 128, c0, 512, None)
    for c0 in range(0, 5632, 512):
        for k in range(8):
            conv_piece(wup, wup_b, k * 128, c0, 512, gffn[:, k:k + 1])
    for k in range(22):
        for c0 in (0, 512):
            conv_piece(wdn, wdn_b, k * 128, c0, 512, None)

    p1scope = contextlib.ExitStack()
    b.scope = p1scope
    if part != 2:
        b.off = P1_BASE + 2 * 4096 + 2 * 2048
        w1fs = b.sb([128, 8, NF], BF16)
        w1ts = b.sb([128, 8, NT], BF16)
        cw = b.sb([64, 12], F32)
        gnws = b.sb([64, 64], F32)
        KT = b.sb([128, LP], BF16)
        Vst = b.sb([128, 2, NB, 65], BF16)
        cneg = b.sb([128, 2, NB], F32)
        cend = b.sb([128, 2, NB], F32)
        carry = b.sb([128, 2], F32)
        hbuf = [b.sb([128, 4, D], F32) for _ in range(2)]
        hnb = b.sb([128, 4, D], BF16)
        hnT1 = b.sb([128, 8, 512], BF16)
        hnT = [hnT1, hnT1]
        QT = [b.sb([128, 512], BF16) for _ in range(2)]
        ssq = b.sb([128, 8], F32)
        rstd = b.sb([128, 8], F32)
        PT = [b.sb([128, 512], BF16) for _ in range(4)]
        biasq = [b.sb([128, NB], F32) for _ in range(2)]
        spt = b.sb([128, 4], F32)
        pre = b.sb([128, 4], F32)
        osb = b.sb([65, 512], F32)
        rdn = b.sb([64, 512], F32)
        sel65 = b.sb([65, 64], F32)
        fill_rhs = b.sb([128, 512], BF16)
        oTf = [b.sb([64, 512], BF16) for _ in range(2)]
        oTg = [b.sb([64, 512], BF16) for _ in range(2)]
        cin = [[b.sb([64, 3 + 512], F32) for g in range(3)] for bb in range(2)]
        Sst = [b.sb([64, 64], F32) for bb in range(2)]
        Sb = [b.sb([64, 64], BF16) for bb in range(2)]

        def g512(dt):
            return b.sb([64, 8, 64], dt)

        cy = [g512(F32) for _ in range(3)]
        ex = g512(F32)
        sq = g512(F32)
        qh = g512(BF16)
        kh = g512(BF16)
        vT = g512(BF16)
        vb = g512(BF16)
        kbg = g512(BF16)
        ktail = g512(BF16)
        Dm = g512(F32)
        E1 = g512(F32)
        E2 = g512(F32)
        A_ = [g512(BF16) for _ in range(2)]
        B_ = [g512(BF16) for _ in range(2)]
        IA = g512(BF16)
        R_ = [g512(BF16) for _ in range(2)]
        attnT = g512(BF16)
        val = g512(F32)
        kcdT = g512(BF16)
        og = g512(F32)
        ogb = g512(BF16)
        zsb = [g512(F32) for _ in range(2)]
        beta2 = [b.sb([64, 8], F32) for _ in range(2)]
        g2 = [b.sb([64, 8], F32) for _ in range(2)]
        qS = b.sb([64, 64], F32)
        vn = b.sb([64, 64], BF16)
        sc8 = {nm: b.sb([64, 8], F32) for nm in ['beta', 'g', 'gc', 'gtot', 'egc', 'etail', 'cd', 'bege', 'ss', 'rs', 'nb']}

        w1stage = hbuf[1][:].rearrange("p t d -> p (t d)")[:, 0:8 * NF].rearrange("p (k n) -> p k n", k=8)
        b.dma(w1stage, w1f.rearrange("(k p) n -> p k n", p=128), w=['h1'])
        for k in range(8):
            b.ts('dve', w1fs[:, k, :], w1stage[:, k, :], gmix[:, k:k + 1], None, ALU.mult, r=['h1', 'gmix'], w=['w1fs'])
        b.ts('dve', w1fs[:, :, 0:128], w1fs[:, :, 0:128], 0.125, None, ALU.mult, r=['w1fs'], w=['w1fs'])
        b.dma(w1stage[:, :, 0:NT], w1t.rearrange("(k p) n -> p k n", p=128), r=['w1fs'], w=['h1'])
        for k in range(8):
            b.ts('dve', w1ts[:, k, :], w1stage[:, k, 0:NT], gmix[:, k:k + 1], None, ALU.mult, r=['h1', 'gmix'], w=['w1ts'])
        b.dma(cw[:], convw[:, :], w=['cw'])
        b.dma(gnws[:], gnw[:, :], w=['gnws'])
        b.memset('pool', Vst[:], 1.0, w=['Vst_%d_%d' % (bb_, t_) for bb_ in range(2) for t_ in range(0, NB, 4)])
        b.memset('pool', carry[:], 0.0, w=['carry'])
        b.memset('pool', fill_rhs[:], 1.0, w=['fill_rhs'])
        b.memset('pool', sel65[:], 0.0, w=['sel65'])
        b.memset('pool', sel65[64:65, :], 1.0, w=['sel65'])
        for bb in range(2):
            for g in range(3):
                b.memset('pool', cin[bb][g][:, 0:3], 0.0, w=['cin%d%d' % (bb, g)])
            b.memset('pool', Sst[bb][:], 0.0, w=['S%d' % bb])
            b.memset('pool', Sb[bb][:], 0.0, w=['Sb%d' % bb])

        nc._dbg = dict(KT=KT, Vst=Vst, cneg=cneg, cend=cend, hnT=hnT1, QT0=QT[0], QT1=QT[1], w1fs=w1fs, w1ts=w1ts, hnb=hnb, rstd=rstd, oTf0=oTf[0], oTg0=oTg[0], kh=kh, qh=qh, vT=vT, og=og, val=val, A0=A_[0], E1=E1, E2=E2, R1=R_[1], beta=sc8['beta'], g=sc8['g'], gc=sc8['gc'], S0=Sst[0])
        nc._p1_free = nc.sbuf_bytes_remaining
        tiles = []
        for t0 in range(0, NB, 4):
            for bb in range(2):
                tiles.append((bb, t0, min(4, NB - t0)))

        psrot = [0]

        def stage_A(ti):
            T.tag = 'A'
            bb, t0, nblk = tiles[ti]
            s = ti % 2
            ntok = nblk * 128
            p0 = t0 * 128
            hk, hTk, qk = 'h%d' % s, 'hnT', 'QT%d' % s
            b.dma(hbuf[s][:, 0:nblk, :], xf[bb, p0:p0 + ntok, :].rearrange("(t p) d -> p t d", p=128), w=[hk])
            b.memset('pool', ssq[:], 0.0, w=['ssq'])
            for t in range(nblk):
                b.act(hnb[:, t, :], hbuf[s][:, t, :], AF.Square, accum_out=ssq[:, t:t + 1], r=[hk], w=['hnb', 'ssq'])
            b.act(rstd[:, 0:nblk], ssq[:, 0:nblk], AF.Ln, bias=EPS, scale=1.0 / D, r=['ssq'], w=['rstd'])
            b.act(rstd[:, 0:nblk], rstd[:, 0:nblk], AF.Exp, scale=-0.5, r=['rstd'], w=['rstd'])
            for t in range(nblk):
                b.ts('dve', hnb[:, t, :], hbuf[s][:, t, :], rstd[:, t:t + 1], None, ALU.mult, r=[hk, 'rstd'], w=['hnb'])
            pst = PS[0][:].bitcast(BF16).rearrange("p (k c) -> p k c", k=8)
            for t in range(nblk):
                for k in range(8):
                    b.tr(pst[:, k, :], hnb[:, t, k * 128:(k + 1) * 128], identb[:], r=['hnb', 'identb'], w=['ps0'])
                b.cp('dve', hnT[s][:, :, t * 128:(t + 1) * 128], pst, r=['ps0'], w=[hTk])
            groups = [(0, 128), (128, 128), (256, 64), (320, 64), (384, 64)]
            for gi, (c0, m) in enumerate(groups):
                pb = 1
                psrot[0] += 1
                pk = 'ps%d' % pb
                for k in range(8):
                    b.mm(PS[pb][0:m, 0:ntok], w1fs[:, k, c0:c0 + m], hnT[s][:, k, 0:ntok], start=(k == 0), stop=(k == 7),
                         r=['w1fs', hTk], w=[pk])
                lo = bb * 64
                if gi == 0:
                    b.cp('dve', QT[s][lo:lo + 64, 0:ntok], PS[pb][lo:lo + 64, 0:ntok], r=[pk], w=[qk])
                elif gi == 1:
                    b.cp('dve', KT[lo:lo + 64, p0:p0 + ntok], PS[pb][lo:lo + 64, 0:ntok], r=[pk], w=['KT_%d_%d' % (bb, t0)])
                else:
                    g = gi - 2
                    b.cp('dve', cin[bb][g][:, 3:3 + ntok], PS[pb][0:64, 0:ntok], r=[pk], w=['cin%d%d' % (bb, g)])
            psv = PS[0][:, 0:260].rearrange("p (t c) -> p t c", t=4)
            for t in range(nblk):
                for k in range(8):
                    b.mm(psv[:, t, :], hnT[s][:, k, t * 128:(t + 1) * 128], w1ts[:, k, 0:65], start=(k == 0), stop=(k == 7),
                         r=['w1ts', hTk], w=['ps0'])
            b.cp('dve', Vst[:, bb, t0:t0 + nblk, 0:64], psv[:, 0:nblk, 0:64], r=['ps0'], w=['Vst_%d_%d' % (bb, t0)])
            b.act(spt[:, 0:nblk], psv[:, 0:nblk, 64], AF.Exp, bias=negfb[:, 0:1], scale=-1.0, r=['ps0', 'negfb'], w=['spt'])
            b.act(spt[:, 0:nblk], spt[:, 0:nblk], AF.Ln, bias=1.0, r=['spt'], w=['spt'])
            b.mm(PS[0][:, 264:264 + nblk], U128[:], spt[:, 0:nblk], r=['U128', 'spt'], w=['ps0'])
            b.mm(PS[0][:, 272:272 + nblk], ones128[:], spt[:, 0:nblk], r=['ones128', 'spt'], w=['ps0'])
            for t in range(nblk):
                prev = carry[:, bb:bb + 1] if t == 0 else cend[:, bb, t0 + t - 1:t0 + t]
                b.tt('dve', cend[:, bb, t0 + t:t0 + t + 1], PS[0][:, 272 + t:273 + t], prev, ALU.add,
                     r=['ps0', 'carry', 'cend_%d_%d' % (bb, t0), 'cend_%d_%d' % (bb, max(t0 - 4, 0))], w=['cend_%d_%d' % (bb, t0)])
                b.tt('dve', cneg[:, bb, t0 + t:t0 + t + 1], PS[0][:, 264 + t:265 + t], prev, ALU.add,
                     r=['ps0', 'carry', 'cend_%d_%d' % (bb, t0), 'cend_%d_%d' % (bb, max(t0 - 4, 0))], w=['cneg_%d_%d' % (bb, t0)])
            b.cp('dve', carry[:, bb:bb + 1], cend[:, bb, t0 + nblk - 1:t0 + nblk], r=['cend_%d_%d' % (bb, t0)], w=['carry'])
            nch = nblk * 2
            for c4 in range(0, nch, 4):
                n4 = min(4, nch - c4)
                pz = PS[1][0:64, 0:264].rearrange("p (n c) -> p n c", n=4)
                for n in range(n4):
                    cn = c4 + n
                    for k in range(8):
                        b.mm(pz[:, n, :], hnT[s][:, k, cn * 64:(cn + 1) * 64], w1ts[:, k, 65:131], start=(k == 0), stop=(k == 7),
                             r=['w1ts', hTk], w=['ps1'])
                b.cp('dve', zsb[s][:, c4:c4 + n4, :], pz[:, 0:n4, 0:64], r=['ps1'], w=['zs%d' % s])
                b.act(beta2[s][:, c4:c4 + n4], pz[:, 0:n4, 64], AF.Exp, scale=-1.0, r=['ps1'], w=['beta%d' % s])
                b.act(g2[s][:, c4:c4 + n4], pz[:, 0:n4, 65], AF.Exp, bias=ppt[0:64, 2:3], r=['ps1', 'ppt'], w=['g%d' % s])

        def stage_B(ti):
            T.tag = 'B'
            bb, t0, nblk = tiles[ti]
            s = ti % 2
            qk = 'QT%d' % s
            lo = bb * 64
            nkb = t0 + nblk
            NQ = nblk * 128
            halves = [(h0, min(2, nblk - h0)) for h0 in range(0, nblk, 2)]
            for hi, (h0, hn_) in enumerate(halves):
                b.ts('dve', biasq[hi][:, 0:nkb], cneg[:, bb, 0:nkb], cend[:, bb, t0 + h0:t0 + h0 + 1], None, ALU.subtract,
                     r=['cneg_%d_%d' % (bb, t_) for t_ in range(0, nkb, 4)] + ['cend_%d_%d' % (bb, t0)], w=['biasq%d' % hi])
            psoT = PS[6][0:65, 0:NQ]
            for kb in range(nkb):
                qlo = max(0, kb - t0)
                sl = 4 + kb % 2
                ps_s = PS[sl][:, 0:512]
                pts = kb % 4
                c0 = qlo * 128
                b.mm(ps_s[:, c0:NQ], KT[lo:lo + 64, kb * 128:(kb + 1) * 128], QT[s][lo:lo + 64, c0:NQ],
                     r=['KT_%d_%d' % (bb, kb // 4 * 4), qk], w=['ps%d' % sl, 'fillpos'])
                if kb % FILL_EVERY == 0:
                    b.mm(PS[3][:, 0:FILL_N], identb[:], fill_rhs[:, 0:FILL_N], r=['identb', 'fill_rhs', 'fillpos'], w=['ps3'])
                for hi, (h0, hn_) in enumerate(halves):
                    a0 = max(c0, h0 * 128)
                    a1 = (h0 + hn_) * 128
                    if a0 >= a1:
                        continue
                    b.act(PT[pts][:, a0:a1], ps_s[:, a0:a1], AF.Exp, bias=biasq[hi][:, kb:kb + 1],
                          r=['ps%d' % sl, 'biasq%d' % hi], w=['PT%d' % pts])
                if qlo > 0:
                    b.memset('pool', PT[pts][:, 0:c0], 0.0, w=['PT%d' % pts])
                if kb >= t0:
                    j = kb - t0
                    b.tt('pool', PT[pts][:, j * 128:(j + 1) * 128], PT[pts][:, j * 128:(j + 1) * 128], trium[:], ALU.mult,
                         r=['PT%d' % pts, 'trium'], w=['PT%d' % pts])
                b.mm(psoT, Vst[:, bb, kb, :], PT[pts][:, 0:NQ], start=(kb == 0), stop=(kb == nkb - 1),
                     r=['PT%d' % pts, 'Vst_%d_%d' % (bb, kb // 4 * 4)], w=['ps6'])
            b.cp('dve', osb[:, 0:NQ], psoT, r=['ps6'], w=['osb'])
            b.mm(PS[6][0:64, 0:NQ], sel65[:, :], osb[:, 0:NQ], r=['osb', 'sel65'], w=['ps6'])
            b.recip(rdn[:, 0:NQ], PS[6][0:64, 0:NQ], r=['ps6'], w=['rdn'])
            b.tt('dve', oTf[s][:, 0:NQ], osb[0:64, 0:NQ], rdn[:, 0:NQ], ALU.mult, r=['osb', 'rdn'], w=['oTf%d' % s])

        def stage_C(ti):
            T.tag = 'C'
            bb, t0, nblk = tiles[ti]
            s = ti % 2
            ntok = nblk * 128
            nch = nblk * 2
            W = nch * 64

            def v3(t):
                return t[:, 0:nch, :]

            def v2(t):
                return t[:].rearrange("p n c -> p (n c)")[:, 0:W]

            for g in range(3):
                ck = 'cin%d%d' % (bb, g)
                b.ts('dve', v2(cy[g]), cin[bb][g][:, 0:W], cw[:, g * 4:g * 4 + 1], None, ALU.mult, r=[ck, 'cw'], w=['cy%d' % g])
                for j in range(1, 4):
                    b.stt('dve', v2(cy[g]), cin[bb][g][:, j:j + W], cw[:, g * 4 + j:g * 4 + j + 1], v2(cy[g]), ALU.mult, ALU.add,
                          r=[ck, 'cw', 'cy%d' % g], w=['cy%d' % g])
                b.cp('pool', cin[bb][g][:, 0:3], cin[bb][g][:, W:W + 3], r=[ck], w=[ck])
                b.act(v2(ex), v2(cy[g]), AF.Exp, scale=-1.0, r=['cy%d' % g], w=['ex'])
                b.ts('dve', v2(ex), v2(ex), 1.0, None, ALU.add, r=['ex'], w=['ex'])
                b.recip(v2(ex), v2(ex), r=['ex'], w=['ex'])
                if g == 2:
                    b.tt('dve', v2(vT), v2(cy[g]), v2(ex), ALU.mult, r=['ex', 'cy2'], w=['vT'])
                else:
                    b.tt('dve', v2(cy[g]), v2(cy[g]), v2(ex), ALU.mult, r=['ex', 'cy%d' % g], w=['cy%d' % g])
                    b.tt('pool', v2(sq), v2(cy[g]), v2(cy[g]), ALU.mult, r=['cy%d' % g], w=['sq'])
                    b.mm(PS[2][0:64, 0:W], ones128[0:64, 0:64], v2(sq), r=['sq', 'ones128'], w=['ps2'])
                    b.act(v2(ex), PS[2][0:64, 0:W], AF.Ln, bias=EPS, r=['ps2'], w=['ex'])
                    b.act(v2(ex), v2(ex), AF.Exp, scale=-0.5, r=['ex'], w=['ex'])
                    if g == 0:
                        b.stt('dve', v2(qh), v2(cy[g]), 0.125, v2(ex), ALU.mult, ALU.mult, r=['cy0', 'ex'], w=['qh'])
                    else:
                        b.tt('dve', v2(kh), v2(cy[g]), v2(ex), ALU.mult, r=['cy1', 'ex'], w=['kh'])
            be, gg = beta2[s], g2[s]
            b.ts('dve', be[:, 0:nch], be[:, 0:nch], 1.0, None, ALU.add, r=['beta%d' % s], w=['beta%d' % s])
            b.recip(be[:, 0:nch], be[:, 0:nch], r=['beta%d' % s], w=['beta%d' % s])
            b.act(gg[:, 0:nch], gg[:, 0:nch], AF.Ln, bias=1.0, r=['g%d' % s], w=['g%d' % s])
            b.ts('dve', gg[:, 0:nch], gg[:, 0:nch], negA[0:64, 0:1], None, ALU.mult, r=['g%d' % s, 'negA'], w=['g%d' % s])
            b.mm(PS[2][0:64, 0:nch], U128[0:64, 0:64], gg[:, 0:nch], r=['g%d' % s, 'U128'], w=['ps2'])
            b.mm(PS[2][0:64, 8:8 + nch], ones128[0:64, 0:64], gg[:, 0:nch], r=['g%d' % s, 'ones128'], w=['ps2'])
            b.cp('dve', sc8['gc'][:, 0:nch], PS[2][0:64, 0:nch], r=['ps2'], w=['gc'])
            b.cp('dve', sc8['gtot'][:, 0:nch], PS[2][0:64, 8:8 + nch], r=['ps2'], w=['gtot'])
            b.act(sc8['egc'][:, 0:nch], sc8['gc'][:, 0:nch], AF.Exp, r=['gc'], w=['egc'])
            b.act(sc8['cd'][:, 0:nch], sc8['gtot'][:, 0:nch], AF.Exp, r=['gtot'], w=['cd'])
            b.tt('dve', sc8['etail'][:, 0:nch], sc8['gtot'][:, 0:nch], sc8['gc'][:, 0:nch], ALU.subtract, r=['gtot', 'gc'], w=['etail'])
            b.act(sc8['etail'][:, 0:nch], sc8['etail'][:, 0:nch], AF.Exp, r=['etail'], w=['etail'])
            b.tt('dve', sc8['bege'][:, 0:nch], be[:, 0:nch], sc8['egc'][:, 0:nch], ALU.mult, r=['beta%d' % s, 'egc'], w=['bege'])
            b.ts('dve', sc8['nb'][:, 0:nch], be[:, 0:nch], -1.0, None, ALU.mult, r=['beta%d' % s], w=['nb'])
            b.tt('dve', v3(sq), bcm(U128[0:64, 0:64], nch), bc3(gg[:, 0:nch], 64), ALU.mult, r=['U128', 'g%d' % s], w=['sq'])
            b.mm(PS[2][0:64, 0:W], ones128[0:64, 0:64], v2(sq), r=['sq', 'ones128'], w=['ps2'])
            ps7v = PS[2][0:64, :].rearrange("p (n c) -> p n c", n=8)[:, 0:nch, :]
            b.tt('dve', v3(Dm), bc3(sc8['gc'][:, 0:nch], 64), ps7v, ALU.subtract, r=['gc', 'ps2'], w=['Dm'])
            b.ts('dve', v3(E1), v3(Dm), 0.0, None, ALU.min, r=['Dm'], w=['E1'])
            b.ts('dve', v3(E2), v3(Dm), 0.0, -1.0, ALU.max, ALU.mult, r=['Dm'], w=['E2'])
            b.act(v2(E1), v2(E1), AF.Exp, r=['E1'], w=['E1'])
            b.act(v2(E2), v2(E2), AF.Exp, r=['E2'], w=['E2'])
            b.tt('pool', v3(E1), v3(E1), m_lows[:, 0:nch, :], ALU.mult, r=['E1', 'm_lows'], w=['E1'])
            b.tt('pool', v3(E1), v3(E1), bc3(sc8['nb'][:, 0:nch], 64), ALU.mult, r=['E1', 'nb'], w=['E1'])
            b.tt('pool', v3(E2), v3(E2), m_upi[:, 0:nch, :], ALU.mult, r=['E2', 'm_upi'], w=['E2'])
            pT = PS[7][0:64, :].bitcast(BF16).rearrange("p (n c) -> p n c", n=8)
            for n in range(nch):
                b.tr(pT[:, n, 0:64], kh[:, n, :], identb[0:64, 0:64], r=['kh', 'identb', 'zs%d' % s], w=['ps7'])
                b.tr(pT[:, n, 64:128], vT[:, n, :], identb[0:64, 0:64], r=['vT', 'identb'], w=['ps7'])
            b.tt('dve', v3(kbg), pT[:, 0:nch, 0:64], bc3(sc8['bege'][:, 0:nch], 64), ALU.mult, r=['ps7', 'bege'], w=['kbg'])
            b.tt('dve', v3(ktail), pT[:, 0:nch, 0:64], bc3(sc8['etail'][:, 0:nch], 64), ALU.mult, r=['ps7', 'etail'], w=['ktail'])
            b.tt('dve', v3(vb), pT[:, 0:nch, 64:128], bc3(be[:, 0:nch], 64), ALU.mult, r=['ps7', 'beta%d' % s], w=['vb'])
            psG = PS[2][0:64, :].rearrange("p (n c) -> p n c", n=8)
            for n in range(nch):
                b.mm(psG[:, n, :], kh[:, n, :], kh[:, n, :], r=['kh'], w=['ps2'])
            b.tt('dve', v3(A_[0]), psG[:, 0:nch, :], v3(E1), ALU.mult, r=['ps2', 'E1'], w=['A0'])
            for n in range(nch):
                b.mm(psG[:, n, :], kh[:, n, :], qh[:, n, :], r=['kh', 'qh'], w=['ps2'])
            b.tt('dve', v3(attnT), psG[:, 0:nch, :], v3(E2), ALU.mult, r=['ps2', 'E2'], w=['attnT'])
            pB = PS[7][0:64, 0:256].bitcast(BF16).rearrange("p (n c) -> p n c", n=8)
            for n in range(nch):
                b.tr(pB[:, n, :], A_[0][:, n, :], identb[0:64, 0:64], r=['A0', 'identb', 'kbg', 'ktail', 'vb'], w=['ps7'])
            b.cp('dve', v3(B_[0]), pB[:, 0:nch, :], r=['ps7'], w=['B0'])
            b.tt('pool', v3(R_[0]), v3(B_[0]), I8[:, 0:nch, :], ALU.add, r=['B0', 'I8'], w=['R0'])
            for lv in range(1, 6):
                ca, pa = lv % 2, (lv - 1) % 2
                for n in range(nch):
                    b.mm(psG[:, n, :], B_[pa][:, n, :], A_[pa][:, n, :], r=['A%d' % pa, 'B%d' % pa], w=['ps2'])
                b.cp('dve', v3(A_[ca]), psG[:, 0:nch, :], r=['ps2'], w=['A%d' % ca])
                b.tt('dve', v3(IA), psG[:, 0:nch, :], I8[:, 0:nch, :], ALU.add, r=['ps2', 'I8'], w=['IA'])
                if lv < 5:
                    psB2 = PS[7][0:64, :].rearrange("p (n c) -> p n c", n=8)
                    for n in range(nch):
                        b.mm(psB2[:, n, :], A_[pa][:, n, :], B_[pa][:, n, :], r=['A%d' % pa, 'B%d' % pa], w=['ps7'])
                    b.cp('dve', v3(B_[ca]), psB2[:, 0:nch, :], r=['ps7'], w=['B%d' % ca])
                for n in range(nch):
                    b.mm(psG[:, n, :], IA[:, n, :], R_[pa][:, n, :], r=['IA', 'R%d' % pa], w=['ps2'])
                b.cp('dve', v3(R_[ca]), psG[:, 0:nch, :], r=['ps2'], w=['R%d' % ca])
            TT = R_[1]
            for n in range(nch):
                b.mm(psG[:, n, :], TT[:, n, :], vb[:, n, :], r=['R1', 'vb'], w=['ps2'])
            b.cp('dve', v3(val), psG[:, 0:nch, :], r=['ps2'], w=['val'])
            psK = PS[7][0:64, :].rearrange("p (n c) -> p n c", n=8)
            for n in range(nch):
                b.mm(psK[:, n, :], kbg[:, n, :], TT[:, n, :], r=['R1', 'kbg'], w=['ps7'])
            b.cp('dve', v3(kcdT), psK[:, 0:nch, :], r=['ps7'], w=['kcdT'])
            sk, sbk = 'S%d' % bb, 'Sb%d' % bb
            for n in range(nch):
                p1 = PS[7][0:64, 0:128]
                p2 = PS[2][0:64, 0:128]
                b.mm(p1[:, 0:64], kcdT[:, n, :], Sb[bb][:], r=['kcdT', sbk], w=['ps7'])
                b.mm(p1[:, 64:128], qh[:, n, :], Sb[bb][:], r=['qh', sbk], w=['ps7'])
                b.tt('dve', vn[:], val[:, n, :], p1[:, 0:64], ALU.subtract, r=['val', 'ps7'], w=['vn'])
                b.ts('dve', qS[:], p1[:, 64:128], sc8['egc'][:, n:n + 1], None, ALU.mult, r=['ps7', 'egc'], w=['qS'])
                b.mm(p2[:, 0:64], attnT[:, n, :], vn[:], r=['attnT', 'vn'], w=['ps2'])
                b.mm(p2[:, 64:128], ktail[:, n, :], vn[:], r=['ktail', 'vn'], w=['ps2'])
                b.tt('dve', og[:, n, :], qS[:], p2[:, 0:64], ALU.add, r=['qS', 'ps2'], w=['og'])
                b.stt('dve', Sst[bb][:], Sst[bb][:], sc8['cd'][:, n:n + 1], p2[:, 64:128], ALU.mult, ALU.add,
                      r=[sk, 'cd', 'ps2'], w=[sk])
                b.cp('dve', Sb[bb][:], Sst[bb][:], r=[sk], w=[sbk])
            b.tt('pool', v3(Dm), v3(og), v3(og), ALU.mult, r=['og'], w=['Dm'])
            b.rsum('dve', sc8['ss'][:, 0:nch], v3(Dm), r=['Dm'], w=['ss'])
            b.act(sc8['rs'][:, 0:nch], sc8['ss'][:, 0:nch], AF.Ln, bias=EPS, scale=1.0 / 64, r=['ss'], w=['rs'])
            b.act(sc8['rs'][:, 0:nch], sc8['rs'][:, 0:nch], AF.Exp, scale=-0.5, r=['rs'], w=['rs'])
            b.tt('dve', v3(og), v3(og), bc3(sc8['rs'][:, 0:nch], 64), ALU.mult, r=['og', 'rs'], w=['og'])
            b.tt('pool', v3(og), v3(og), bcm(gnws[:], nch), ALU.mult, r=['og', 'gnws'], w=['og'])
            b.act(v2(ex), v2(zsb[s]), AF.Exp, scale=-1.0, r=['zs%d' % s], w=['ex'])
            b.ts('dve', v2(ex), v2(ex), 1.0, None, ALU.add, r=['ex'], w=['ex'])
            b.recip(v2(ex), v2(ex), r=['ex'], w=['ex'])
            b.tt('pool', v2(ex), v2(ex), v2(zsb[s]), ALU.mult, r=['ex', 'zs%d' % s], w=['ex'])
            b.tt('dve', v3(ogb), v3(og), v3(ex), ALU.mult, r=['og', 'ex'], w=['ogb'])
            pO = PS[7][0:64, 0:256].bitcast(BF16)
            for n in range(nch):
                b.tr(pO[:, n * 64:(n + 1) * 64], ogb[:, n, :], identb[0:64, 0:64], r=['ogb', 'identb', 'kcdT'], w=['ps7'])
            b.cp('dve', oTg[s][:, 0:W], pO[:, 0:W], r=['ps7'], w=['oTg%d' % s])

        def stage_store(ti):
            T.tag = 'store'
            bb, t0, nblk = tiles[ti]
            s = ti % 2
            for t in range(nblk):
                p = (t0 + t) * 128
                for src, rows, key in ((oTf[s], 0, 'oTf%d' % s), (oTg[s], 64, 'oTg%d' % s)):
                    blk = src[:, t * 128:(t + 1) * 128]
                    if p < 4 * Q:
                        j = p // Q
                        it, col = divmod(p - j * Q, TW)
                        d = ((bb * 4 + j) * NTT + it) * 128 + rows
                        b.dma(a2a_in[d:d + 64, col:col + 128], blk, r=[key], w=['a2a_in'])
                        if j >= 1 and p % Q == 0:
                            d = ((bb * 4 + j - 1) * NTT + NT2) * 128 + rows
                            b.dma(a2a_in[d:d + 64, 0:16], blk[:, 0:16], r=[key], w=['a2a_in'])
                    else:
                        d = ((bb * 4 + 3) * NTT + NT2) * 128 + rows
                        b.dma(a2a_in[d:d + 64, 0:16], blk[:, 0:16], r=[key], w=['a2a_in'])

        ztile = b.sb([128, TW], BF16)
        b.memset('pool', ztile[:], 0.0, w=['ztile'])
        for d_ in range(8):
            r_ = (d_ * NTT + NT2) * 128
            b.dma(a2a_in[r_:r_ + 128, :], ztile[:], r=['ztile'], w=['a2a_in'])
        per_tile = (len(pieces) + len(tiles) - 1) // len(tiles)
        for ti in range(len(tiles)):
            stage_A(ti)
            for pc in (pieces[ti * per_tile:(ti + 1) * per_tile] if part == 0 else []):
                conv_emit(*pc)
            stage_B(ti)
            stage_C(ti)
            stage_store(ti)

    if part == 0:
        NBLK = 8 * NTT
        CH = ag_chunk_blocks(TW)
        for k in range((NBLK + CH - 1) // CH):
            nbk = min(CH, NBLK - k * CH)
            src_ap = a2a_in[k * CH * 128:(k * CH + nbk) * 128, :]
            dst_ap = gath[8 * 128 * CH * k:8 * 128 * CH * k + 8 * nbk * 128, :]
            T.op('pool', lambda e, src_ap=src_ap, dst_ap=dst_ap: e.collective_compute(
                "AllGather", ALU.bypass, replica_groups=[list(range(8))], ins=[src_ap], outs=[dst_ap]),
                r=['a2a_in'], w=['gath'], cc=True, cost=300.0, lat=60000.0)
    T.barrier(lambda e: e.memset(bar_t[:], 0.0))
    p1scope.close()
    p2scope = contextlib.ExitStack()
    b.scope = p2scope
    if part == 2:
        for pc in pieces:
            conv_emit(*pc)
    if part != 1:
        b.off = P1_BASE
        h2 = b.sb([128, 4, D], F32)
        hb2 = b.sb([128, 4, D], BF16)
        hT2 = b.sb([128, 8, 512], BF16)
        oT2 = b.sb([128, 8, 512], BF16)
        gT = b.sb([128, 16, 512], BF16)
        mT = b.sb([128, 8, 512], BF16)
        aT = b.sb([128, 22, 512], BF16)
        uprev = b.sb([128, 44, 2], F32)
        ugL = [b.sb([128, 514], F32) for _ in range(2)]
        uuL = [b.sb([128, 514], F32) for _ in range(2)]
        cgL = [b.sb([128, 512], F32) for _ in range(2)]
        cuL = [b.sb([128, 512], F32) for _ in range(2)]
        yTL = [b.sb([128, 512], F32) for _ in range(2)]
        ug, uu, cg, cu, yT = ugL[0], uuL[0], cgL[0], cuL[0], yTL[0]
        gfin = b.sb([128, D], F32)
        gbs = b.sb([128, 16], F32)
        fcws = b.sb([128, 44, 3], F32)
        fcbs = b.sb([128, 44], F32)
        ss2 = b.sb([128, 8], F32)
        rs2 = b.sb([128, 8], F32)
        NWR = 6
        WR = [b.sb([128, 5632], BF16) for _ in range(NWR)]
        wcnt = [0]

        idxs = b.sb([128, 8 * NTT], mybir.dt.int32)
        b.dma(idxs[:], idx[:, :], w=['idxs'])
        b.dma(gfin[:], g_fin[:, :], w=['gfin'])
        b.dma(gbs[:], gbias[:, :], w=['gbs'])
        b.dma(fcws[:], fcw[:, :, :], w=['fcws'])
        b.dma(fcbs[:], fcb[:, :], w=['fcbs'])
        b.memset('pool', uprev[:], 0.0, w=['uprev'])
        b.memset('pool', h2[:], 0.0, w=['h2'])

        def wload(src, kch, c0, ncols):
            s_ = wcnt[0] % NWR
            wcnt[0] += 1
            v = WR[s_][:, 0:kch * ncols].rearrange("p (k n) -> p k n", k=kch)
            rk = [('W', src.tensor.name, k_, cb_) for k_ in range(kch) for cb_ in range(c0 // 512, (c0 + ncols + 511) // 512)]
            b.dma(v, src[:, c0:c0 + ncols].rearrange("(k p) n -> p k n", p=128), r=rk, w=['WR%d' % s_])
            return v, 'WR%d' % s_

        def norm_T(nblk, ntok, rows_ap_fn, key_h):
            T.tag = 'norm'
            b.memset('pool', ss2[:], 0.0, w=['ss2'])
            for t in range(nblk):
                b.act(hb2[:, t, :], h2[:, t, :], AF.Square, accum_out=ss2[:, t:t + 1], r=[key_h], w=['hb2', 'ss2'])
            b.act(rs2[:, 0:nblk], ss2[:, 0:nblk], AF.Ln, bias=EPS, scale=1.0 / D, r=['ss2'], w=['rs2'])
            b.act(rs2[:, 0:nblk], rs2[:, 0:nblk], AF.Exp, scale=-0.5, r=['rs2'], w=['rs2'])
            pst = PS[0][:].bitcast(BF16).rearrange("p (k c) -> p k c", k=8)
            for t in range(nblk):
                b.ts('dve', hb2[:, t, :], h2[:, t, :], rs2[:, t:t + 1], None, ALU.mult, r=[key_h, 'rs2'], w=['hb2'])
                for k in range(8):
                    b.tr(pst[:, k, :], hb2[:, t, k * 128:(k + 1) * 128], identb[:], r=['hb2', 'identb'], w=['ps0'])
                b.cp('dve', hT2[:, :, t * 128:(t + 1) * 128], pst, r=['ps0'], w=['hT2'])

        def ffn_up(ntok, first_halo):
            T.tag = 'ffn'
            for pi in range(11):
                wv_g, kg = wload(wup_b, 8, pi * 256, 256)
                wv_u, ku = wload(wup_b, 8, DFF + pi * 256, 256)
                for cc in range(2):
                    j = pi * 2 + cc
                    q_ = j % 2
                    ug, uu, cg, cu, yT = ugL[q_], uuL[q_], cgL[q_], cuL[q_], yTL[q_]
                    for half, (wv, kk, dst, pb) in enumerate(((wv_g, kg, ug, 1), (wv_u, ku, uu, 2))):
                        jj = j + 22 * half
                        for k in range(8):
                            b.mm(PS[pb][:, 0:ntok], wv[:, k, cc * 128:(cc + 1) * 128], hT2[:, k, 0:ntok], start=(k == 0), stop=(k == 7),
                                 r=[kk, 'hT2'], w=['ps%d' % pb])
                        dk = ('ug%d' if half == 0 else 'uu%d') % q_
                        b.cp('pool', dst[:, 0:2], uprev[:, jj, :], r=['uprev'], w=[dk])
                        b.cp('dve', dst[:, 2:2 + ntok], PS[pb][:, 0:ntok], r=['ps%d' % pb], w=[dk])
                        b.cp('pool', uprev[:, jj, :], dst[:, ntok:ntok + 2], r=[dk], w=['uprev'])
                        if first_halo:
                            continue
                        co = cg if half == 0 else cu
                        ck = ('cg%d' if half == 0 else 'cu%d') % q_
                        b.ts('dve', co[:, 0:ntok], dst[:, 0:ntok], fcws[:, jj, 0:1], fcbs[:, jj:jj + 1], ALU.mult, ALU.add,
                             r=[dk, 'fcws', 'fcbs'], w=[ck])
                        b.stt('dve', co[:, 0:ntok], dst[:, 1:1 + ntok], fcws[:, jj, 1:2], co[:, 0:ntok], ALU.mult, ALU.add,
                              r=[dk, 'fcws', ck], w=[ck])
                        b.stt('dve', co[:, 0:ntok], dst[:, 2:2 + ntok], fcws[:, jj, 2:3], co[:, 0:ntok], ALU.mult, ALU.add,
                              r=[dk, 'fcws', ck], w=[ck])
                    if first_halo:
                        continue
                    b.act(yT[:, 0:ntok], cg[:, 0:ntok], AF.Sigmoid, r=['cg%d' % q_], w=['yT%d' % q_])
                    b.tt('pool', cg[:, 0:ntok], cg[:, 0:ntok], cu[:, 0:ntok], ALU.mult, r=['cg%d' % q_, 'cu%d' % q_], w=['cg%d' % q_])
                    b.tt('dve', aT[:, j, 0:ntok], cg[:, 0:ntok], yT[:, 0:ntok], ALU.mult, r=['cg%d' % q_, 'yT%d' % q_], w=['aT'])

        def mixer_tail(r0, ntok, nblk):
            T.tag = 'mix'
            np_ = ntok if ntok < 128 else 128
            b.dma(h2[0:np_, 0:nblk, :], x2[r0:r0 + ntok, :].rearrange("(t p) d -> p t d", p=np_), w=['h2'])
            norm_T(nblk, ntok, None, 'h2')
            tix = r0 // TW
            for c in range(8):
                col = tix * 8 + c
                T.op('pool', lambda e, c=c, col=col: e.indirect_dma_start(
                    out=oT2[:, c, 0:TW], out_offset=None, in_=gath[:, :],
                    in_offset=bass.IndirectOffsetOnAxis(ap=idxs[:, col:col + 1], axis=0)),
                    r=['gath', 'idxs'], w=['oT2'], dma=True, cost=1500.0, lat=3000.0)
            for pi in range(4):
                wv, kk = wload(wg_b, 8, pi * 512, 512)
                for cc in range(4):
                    j = pi * 4 + cc
                    pb = 1 + j % 2
                    for k in range(8):
                        b.mm(PS[pb][:, 0:ntok], wv[:, k, cc * 128:(cc + 1) * 128], hT2[:, k, 0:ntok], start=(k == 0), stop=(k == 7),
                             r=[kk, 'hT2'], w=['ps%d' % pb])
                    b.act(gT[:, j, 0:ntok], PS[pb][:, 0:ntok], AF.Sigmoid, bias=gbs[:, j:j + 1], scale=1.0,
                          r=['ps%d' % pb, 'gbs'], w=['gT'])
            wf_, kf = wload(wbf_b, 4, 0, 1024)
            wg_, kg_ = wload(wbg_b, 4, 0, 1024)
            for j in range(8):
                for k in range(4):
                    b.mm(PS[1][:, 0:ntok], wf_[:, k, j * 128:(j + 1) * 128], oT2[:, k, 0:ntok], start=(k == 0), stop=(k == 3),
                         r=[kf, 'oT2'], w=['ps1'])
                for k in range(4):
                    b.mm(PS[2][:, 0:ntok], wg_[:, k, j * 128:(j + 1) * 128], oT2[:, 4 + k, 0:ntok], start=(k == 0), stop=(k == 3),
                         r=[kg_, 'oT2'], w=['ps2'])
                b.tt('dve', yT[:, 0:ntok], PS[1][:, 0:ntok], gT[:, j, 0:ntok], ALU.mult, r=['ps1', 'gT'], w=['yT0'])
                b.tt('dve', cg[:, 0:ntok], PS[2][:, 0:ntok], gT[:, 8 + j, 0:ntok], ALU.mult, r=['ps2', 'gT'], w=['cg0'])
                b.tt('pool', mT[:, j, 0:ntok], yT[:, 0:ntok], cg[:, 0:ntok], ALU.add, r=['yT0', 'cg0'], w=['mT'])
            for hc in range(2):
                wv, kk = wload(wo_b, 8, hc * 512, 512)
                for t in range(nblk):
                    nt_ = min(128, ntok - t * 128)
                    pb = 1 + t % 2
                    for k in range(8):
                        b.mm(PS[pb][0:nt_, :], mT[:, k, t * 128:t * 128 + nt_], wv[:, k, :], start=(k == 0), stop=(k == 7),
                             r=[kk, 'mT'], w=['ps%d' % pb])
                    b.tt('dve', h2[0:nt_, t, hc * 512:(hc + 1) * 512], h2[0:nt_, t, hc * 512:(hc + 1) * 512], PS[pb][0:nt_, :], ALU.add,
                         r=['ps%d' % pb, 'h2'], w=['h2'])


        nc._p2_free = nc.sbuf_bytes_remaining
        p2tiles = [(it * TW, TW) for it in range(NT2)] + [(Q, 16)]
        for r0, ntok in p2tiles:
            nblk = (ntok + 127) // 128
            mixer_tail(r0, ntok, nblk)
            norm_T(nblk, ntok, None, 'h2')
            ffn_up(ntok, False)
            T.tag = 'down'
            for hc in range(4):
                wv, kk = wload(wdn_b, 22, hc * 256, 256)
                for t in range(nblk):
                    nt_ = min(128, ntok - t * 128)
                    pb = 1 + t % 2
                    for k in range(22):
                        b.mm(PS[pb][0:nt_, 0:256], aT[:, k, t * 128:t * 128 + nt_], wv[:, k, :], start=(k == 0), stop=(k == 21),
                             r=[kk, 'aT'], w=['ps%d' % pb])
                    b.tt('dve', h2[0:nt_, t, hc * 256:(hc + 1) * 256], h2[0:nt_, t, hc * 256:(hc + 1) * 256], PS[pb][0:nt_, 0:256], ALU.add,
                         r=['ps%d' % pb, 'h2'], w=['h2'])
            b.memset('pool', ss2[:], 0.0, w=['ss2'])
            for t in range(nblk):
                b.act(hb2[:, t, :], h2[:, t, :], AF.Square, accum_out=ss2[:, t:t + 1], r=['h2'], w=['hb2', 'ss2'])
            b.act(rs2[:, 0:nblk], ss2[:, 0:nblk], AF.Ln, bias=EPS, scale=1.0 / D, r=['ss2'], w=['rs2'])
            b.act(rs2[:, 0:nblk], rs2[:, 0:nblk], AF.Exp, scale=-0.5, r=['rs2'], w=['rs2'])
            for t in range(nblk):
                b.stt('dve', h2[:, t, :], h2[:, t, :], rs2[:, t:t + 1], gfin[:], ALU.mult, ALU.mult, r=['h2', 'rs2', 'gfin'], w=['h2'])
            if r0 == 0:
                b.dma(out[0:112, :], h2[16:128, 0, :], r=['h2'], w=['out'])
                if nblk > 1:
                    b.dma(out[112:ntok - 16, :].rearrange("(t p) d -> p t d", p=128), h2[:, 1:nblk, :], r=['h2'], w=['out'])
            elif ntok == TW:
                b.dma(out[r0 - 16:r0 - 16 + TW, :].rearrange("(t p) d -> p t d", p=128), h2[:, 0:nblk, :], r=['h2'], w=['out'])
            else:
                b.dma(out[Q - 16:Q, :], h2[0:16, 0, :], r=['h2'], w=['out'])
        p2scope.close()

    fkeys = ['out'] if part != 1 else ['a2a_in']
    T.finalize(final_wait_keys=fkeys)
    with contextlib.ExitStack() as st:
        esem = {e: st.enter_context(nc.semaphore("s_" + e)) for e in ENGS}
        dsems = {rg: [st.enter_context(nc.semaphore("d%s%d" % (rg, i))) for i in range(NDMA_SEMS)] for rg in ('s', 'p')}
        ccsem = [st.enter_context(nc.semaphore("ccsem%d" % i)) for i in range(max(1, T.ncc))]
        st.enter_context(nc.allow_low_precision("bf16 matmul operands by design, fp32 accumulation"))
        block = st.enter_context(nc.Block())
        T.emit(esem, dsems, ccsem, block)
    outer.close()
    nc._stats = (T.makespan, T.busy, T.nops)
    return nc


def make_in_maps(inp, SEQ, fused=True):
    L, NB, LP, Q, QH = cfg(SEQ)
    f = np.float32
    x = np.asarray(inp["x"], f)
    meta = np.asarray(inp["meta_tokens"], f)
    xf = np.zeros((2, LP, D), f)
    xf[:, 0:NMETA] = meta[None]
    xf[:, NMETA:L] = x
    w_in = np.asarray(inp["w_in"], f)[0]
    conv_w = np.asarray(inp["gdn_conv_w"], f)[0]
    gm = np.asarray(inp["norm_mix_w"], f)[0]
    gf = np.asarray(inp["norm_ffn_w"], f)[0]
    shared = dict(
        g_mix=np.ascontiguousarray(gm.reshape(8, 128).T),
        g_ffn=np.ascontiguousarray(gf.reshape(8, 128).T),
        g_fin=np.ascontiguousarray(np.broadcast_to(np.asarray(inp["norm_final_w"], f)[None, :], (128, D))),
        wg=np.ascontiguousarray(w_in[:, 3608:3608 + 2048]),
        gbias=np.ascontiguousarray(np.asarray(inp["gate_bias"], f)[0].reshape(16, 128).T),
        wbf=np.ascontiguousarray(np.asarray(inp["w_branch_fox"], f)[0]),
        wbg=np.ascontiguousarray(np.asarray(inp["w_branch_gdn"], f)[0]),
        wo=np.ascontiguousarray(np.asarray(inp["w_out"], f)[0]),
        wup=np.ascontiguousarray(np.asarray(inp["ffn_w_up"], f)[0]),
        fcw=np.ascontiguousarray(np.asarray(inp["ffn_conv_w"], f)[0].T.reshape(44, 128, 3).transpose(1, 0, 2)),
        fcb=np.ascontiguousarray(np.asarray(inp["ffn_conv_b"], f)[0].reshape(44, 128).T),
        wdn=np.ascontiguousarray(np.asarray(inp["ffn_w_down"], f)[0]),
        gnw=np.ascontiguousarray(np.broadcast_to(np.asarray(inp["gdn_norm_w"], f)[0][None, :], (64, 64))),
        xf=xf,
    )
    maps = []
    for c in range(8):
        h = c
        fq = w_in[:, h * 64:(h + 1) * 64]
        fk = w_in[:, 512 + h * 64:512 + (h + 1) * 64]
        fv = w_in[:, 1024 + h * 64:1024 + (h + 1) * 64]
        ff = w_in[:, 1536 + h:1537 + h]
        g0 = 1544
        gq = w_in[:, g0 + h * 64:g0 + (h + 1) * 64]
        gk = w_in[:, g0 + 512 + h * 64:g0 + 512 + (h + 1) * 64]
        gv = w_in[:, g0 + 1024 + h * 64:g0 + 1024 + (h + 1) * 64]
        z = w_in[:, 3080 + h * 64:3080 + (h + 1) * 64]
        bcol = w_in[:, 3592 + h:3593 + h]
        acol = w_in[:, 3600 + h:3601 + h]
        w1f = np.ascontiguousarray(np.concatenate([fq, fq, fk, fk, gq, gk, gv], axis=1))
        w1t = np.ascontiguousarray(np.concatenate([fv, ff, z, bcol, acol], axis=1))
        pp = np.zeros((128, 8), f)
        pp[:, 0] = np.asarray(inp["fgt_bias"], f)[0, h]
        pp[:, 1] = np.asarray(inp["gdn_a_log"], f)[0, h]
        pp[:, 2] = np.asarray(inp["gdn_dt_bias"], f)[0, h]
        cwm = np.zeros((64, 12), f)
        for g in range(3):
            cwm[:, g * 4:(g + 1) * 4] = conv_w[:, g * 512 + h * 64:g * 512 + (h + 1) * 64].T
        bb, j = divmod(c, 4)
        x2 = np.ascontiguousarray(xf[bb, j * Q:j * Q + QH])
        TW = min(512, Q)
        NTT = Q // TW + 1
        idx = np.zeros((128, 8 * NTT), np.int32)
        pa = np.arange(128)
        for it in range(NTT):
            for ch in range(8):
                half, cc = divmod(ch, 4)
                src = 2 * cc + pa // 64
                blk = c * NTT + it
                if fused:
                    base = np.array([gath_row(int(s_), blk, 8 * NTT, ag_chunk_blocks(TW)) for s_ in src])
                else:
                    base = (src * NTT + it) * 128
                idx[:, it * 8 + ch] = base + half * 64 + pa % 64
        m = dict(shared)
        m.update(w1f=w1f, w1t=w1t, pp=pp, convw=cwm, x2=x2, idx=idx)
        maps.append(m)
    return maps


_CACHE = {}


FUSED = False
P1KEYS = ["xf", "w1f", "w1t", "pp", "convw", "gnw", "g_mix"]
P2KEYS = ["x2", "g_mix", "g_ffn", "g_fin", "wg", "gbias", "wbf", "wbg", "wo", "wup", "fcw", "fcb", "wdn", "idx"]


def run(inp, SEQ, fused=None):
    fused = FUSED if fused is None else fused
    L, NB, LP, Q, QH = cfg(SEQ)
    TW = min(512, Q)
    NTT = Q // TW + 1
    maps = make_in_maps(inp, SEQ, fused)
    if fused:
        if (SEQ, 0) not in _CACHE:
            _CACHE[(SEQ, 0)] = build(SEQ, 0)
        res = run_bass_kernel_spmd(_CACHE[(SEQ, 0)], maps, core_ids=list(range(8)))
    else:
        for part in (1, 2):
            if (SEQ, part) not in _CACHE:
                _CACHE[(SEQ, part)] = build(SEQ, part)
        res1 = run_bass_kernel_spmd(_CACHE[(SEQ, 1)], [{k: m[k] for k in P1KEYS} for m in maps], core_ids=list(range(8)))
        sh = [np.asarray(res1.results[c]["a2a_in"]) for c in range(8)]
        maps2 = []
        for c in range(8):
            m = {k: maps[c][k] for k in P2KEYS}
            m["gath"] = np.ascontiguousarray(np.concatenate(
                [sh[s_][c * NTT * 128:(c + 1) * NTT * 128] for s_ in range(8)], axis=0))
            maps2.append(m)
        res = run_bass_kernel_spmd(_CACHE[(SEQ, 2)], maps2, core_ids=list(range(8)))
    out = np.zeros((2, SEQ, D), np.float32)
    for c in range(8):
        bb, j = divmod(c, 4)
        out[bb, j * Q:(j + 1) * Q] = res.results[c]["out"]
    return out, res


def kernel(**inputs):
    SEQ = inputs["x"].shape[1]
    out, _ = run(inputs, SEQ)
    return out
```

```python
import contextlib
import numpy as np
import concourse.bass as bass
import concourse.mybir as mybir
from concourse.bass_utils import run_bass_kernel_spmd

F32 = mybir.dt.float32
BF16 = mybir.dt.bfloat16
AF = mybir.ActivationFunctionType
ALU = mybir.AluOpType
AX = mybir.AxisListType

D = 1024
NMETA = 16
DFF = 2816
EPS = 1e-6
SELF_SYNC = True
NDMA_SEMS = 24
FILL_EVERY = 10 ** 9
FILL_N = 512
HOP_NS = 380.0
ENGS = ['pe', 'act', 'dve', 'pool', 'sp']


class Tracker:
    def __init__(self):
        self.ops = []
        self.lastw = {}
        self.readers = {}
        self.ndma = {'s': 0, 'p': 0}
        self.bar = set()
        self.ncc = 0
        self.tag = ''

    def op(self, eng, fn, r=(), w=(), dma=False, cc=False, cost=100.0, lat=0.0):
        i = len(self.ops)
        w = list(w) + [k for k in r if isinstance(k, str) and k.startswith('ps') and k[2:].isdigit()]
        deps = set(self.bar)
        for k in r:
            j = self.lastw.get(k)
            if j is not None:
                deps.add(j)
        for k in w:
            j = self.lastw.get(k)
            if j is not None:
                deps.add(j)
            deps.update(self.readers.get(k, ()))
        d = None
        if dma:
            ring = 'p' if eng == 'pool' else 's'
            d = (ring, -1)
        ccid = None
        if cc:
            ccid = self.ncc
            self.ncc += 1
        self.ops.append(dict(eng=eng, fn=fn, deps=deps, dma=d, users=False, cc=cc, ccid=ccid, cost=cost, lat=lat, bar=False, tag=self.tag))
        for k in r:
            self.readers.setdefault(k, []).append(i)
        for k in w:
            self.lastw[k] = i
            self.readers[k] = []
        return i

    def barrier(self, fn):
        self.bar = set()
        i = self.op('pool', fn, (), ())
        self.ops[i]['bar'] = True
        self.bar = {i}
        return i

    def finalize(self, final_wait_keys=()):
        import heapq
        ops = self.ops
        n = len(ops)
        fin = set()
        for k in final_wait_keys:
            j = self.lastw.get(k)
            if j is not None:
                fin.add(j)
        succ = [[] for _ in range(n)]
        npend = [0] * n
        for i, o in enumerate(ops):
            dd = range(i) if o['bar'] else o['deps']
            npend[i] = len(dd)
            for j in dd:
                succ[j].append(i)
        ready = [0.0] * n
        start = [0.0] * n
        t_e = {e: 0.0 for e in ENGS}
        fut = {e: [] for e in ENGS}
        avail = {e: [] for e in ENGS}
        for i, o in enumerate(ops):
            if npend[i] == 0:
                heapq.heappush(fut[o['eng']], (0.0, i))
        done = 0
        last_on = {}
        t_prev_end = {}
        rby = [None] * n
        while done < n:
            best = None
            for e in ENGS:
                f, a_ = fut[e], avail[e]
                while f and f[0][0] <= t_e[e]:
                    heapq.heappush(a_, heapq.heappop(f)[1])
                if a_:
                    cand = (t_e[e], a_[0], e, True)
                elif f:
                    cand = (f[0][0], f[0][1], e, False)
                else:
                    continue
                if best is None or cand[:2] < best[:2]:
                    best = cand
            st, i, e, from_avail = best
            if from_avail:
                heapq.heappop(avail[e])
            else:
                heapq.heappop(fut[e])
            o = ops[i]
            start[i] = st
            o['by_eng'] = (st <= t_prev_end.get(e, 0.0) + 1e-9 and e in last_on) and last_on.get(e)
            last_on[e] = i
            t_prev_end[e] = st + o['cost']
            t_e[e] = st + o['cost']
            fi = st + o['cost'] + o['lat']
            for s_ in succ[i]:
                os_ = ops[s_]
                rt = fi + (60.0 if (os_['eng'] == e and o['dma'] is None and not o['cc']) else HOP_NS)
                if rt > ready[s_]:
                    ready[s_] = rt
                    rby[s_] = i
                npend[s_] -= 1
                if npend[s_] == 0:
                    heapq.heappush(fut[os_['eng']], (ready[s_], s_))
            done += 1
        self.makespan = max(t_e.values())
        self.start = start
        self.rby = rby
        self.busy = {e: sum(o['cost'] for o in ops if o['eng'] == e) for e in ENGS}
        self.nops = {e: sum(1 for o in ops if o['eng'] == e) for e in ENGS}
        self.per_eng = {e: [] for e in ENGS}
        for i in sorted(range(n), key=lambda i: (start[i], i)):
            self.per_eng[ops[i]['eng']].append(i)
        pos = {}
        for e in ENGS:
            for p_, i in enumerate(self.per_eng[e]):
                pos[i] = p_
        for i, o in enumerate(ops):
            if not o['bar']:
                continue
            red = set()
            for e in ENGS:
                pre = [j for j in self.per_eng[e] if j < i and ops[j]['dma'] is None and not ops[j]['cc']]
                if pre:
                    red.add(pre[-1])
                for ring in ('s', 'p'):
                    pd = [j for j in self.per_eng[e] if j < i and ops[j]['dma'] is not None and ops[j]['dma'][0] == ring]
                    red.update(pd[-NDMA_SEMS:])
            red.update(j for j in range(i) if ops[j]['cc'])
            o['deps'] = red
        for ring, e in (('s', 'sp'), ('p', 'pool')):
            dma_ops = [i for i in self.per_eng[e] if ops[i]['dma'] is not None]
            assert all(ops[i]['dma'][0] == ring for i in dma_ops)
            for n_, i in enumerate(dma_ops):
                ops[i]['dma'] = (ring, n_)
                if n_ >= NDMA_SEMS:
                    ops[i]['deps'].add(dma_ops[n_ - NDMA_SEMS])
        for e in ENGS:
            assert e in ('sp', 'pool') or all(ops[i]['dma'] is None for i in self.per_eng[e])
        for i, o in enumerate(ops):
            for j in o['deps']:
                dj = ops[j]
                if dj['dma'] is not None or dj['cc']:
                    continue
                if dj['eng'] == o['eng'] and o['dma'] is None and not o['cc']:
                    if o['eng'] == 'pe' or not SELF_SYNC:
                        continue
                dj['users'] = True
        for j in fin:
            if ops[j]['dma'] is None and not ops[j]['cc']:
                ops[j]['users'] = True
        for e in ENGS:
            c_ = 0
            for i in self.per_eng[e]:
                o = ops[i]
                if o['dma'] is None and not o['cc'] and o['users']:
                    c_ += 1
                    o['val'] = c_
        self.fin = fin

    def emit(self, esem, dsems, ccsem, block):
        ops = self.ops

        def target(j):
            dj = ops[j]
            if dj['cc']:
                return ('c', dj['ccid']), ccsem[dj['ccid']], 1
            if dj['dma'] is not None:
                ring, d = dj['dma']
                return ('d', ring, d % NDMA_SEMS), dsems[ring][d % NDMA_SEMS], 16 * (d // NDMA_SEMS + 1)
            return ('e', dj['eng']), esem[dj['eng']], dj['val']

        def run(engname, eng):
            known = {}
            for i in self.per_eng[engname]:
                o = ops[i]
                waits = {}
                for j in o['deps']:
                    dj = ops[j]
                    if (dj['dma'] is None and not dj['cc'] and dj['eng'] == engname
                            and o['dma'] is None and not o['cc']):
                        if engname == 'pe' or not SELF_SYNC:
                            continue
                    key, sem, val = target(j)
                    if known.get(key, 0) >= val:
                        continue
                    if key not in waits or waits[key][1] < val:
                        waits[key] = (sem, val)
                for key, (sem, val) in waits.items():
                    eng.wait_ge(sem, val)
                    known[key] = val
                ins = o['fn'](eng)
                if o['cc']:
                    ins.then_inc(ccsem[o['ccid']])
                elif o['dma'] is not None:
                    ins.then_inc(dsems[o['dma'][0]][o['dma'][1] % NDMA_SEMS], 16)
                elif o['users']:
                    ins.then_inc(esem[engname], 1)
            if engname == 'sp':
                for j in self.fin:
                    key, sem, val = target(j)
                    if known.get(key, 0) >= val:
                        continue
                    eng.wait_ge(sem, val)
                    known[key] = val

        block.tensor(lambda e: run('pe', e))
        block.scalar(lambda e: run('act', e))
        block.vector(lambda e: run('dve', e))
        block.gpsimd(lambda e: run('pool', e))
        block.sync(lambda e: run('sp', e))


class B:
    def __init__(self, nc):
        self.nc = nc
        self.T = Tracker()
        self.off = 0
        self.n = 0
        self.scope = None

    def sb(self, shape, dt):
        t = self.scope.enter_context(self.nc.sbuf_tensor("sb%d" % self.n, list(shape), dt))
        self.n += 1
        return t

    @staticmethod
    def _n(ap):
        sh = ap.shape
        n = 1
        for x in sh[1:]:
            n *= int(x)
        return n

    def _c(self, eng, out, in_=None):
        n = self._n(out)
        if eng == 'act':
            return 155.0 + n * 0.835
        if eng == 'dve':
            f = 0.96
            return 60.0 + n / f
        if eng == 'pool':
            return 100.0 + n / 0.6
        return 100.0

    def mm(self, out, lhsT, rhs, start=True, stop=True, r=(), w=()):
        n = self._n(rhs)
        m = self._n(lhsT)
        f4 = 4.0 if rhs.dtype == F32 else 1.0
        c = 10.0 + m / 1.2 * (2.0 if f4 > 1 else 1.0) + (n / 2.4) * f4
        return self.T.op('pe', lambda e: e.matmul(out, lhsT=lhsT, rhs=rhs, start=start, stop=stop), r, w, cost=c, lat=0.0)

    def tr(self, out, in_, ident, r=(), w=()):
        return self.T.op('pe', lambda e: e.transpose(out=out, in_=in_, identity=ident), r, w, cost=10.0 + 128 / 1.2 + self._n(in_) / 2.4, lat=0.0)

    def act(self, out, in_, func, bias=0.0, scale=1.0, accum_out=None, r=(), w=()):
        c = self._c('act', out)
        if accum_out is None:
            return self.T.op('act', lambda e: e.activation(out=out, in_=in_, func=func, bias=bias, scale=scale), r, w, cost=c)
        return self.T.op('act', lambda e: e.activation(out=out, in_=in_, func=func, bias=bias, scale=scale,
                                                       accum_out=accum_out), r, w, cost=c)

    def ts(self, eng, out, in0, s1, s2, op0, op1=None, r=(), w=()):
        c = self._c(eng, out)
        if op1 is None:
            return self.T.op(eng, lambda e: e.tensor_scalar(out=out, in0=in0, scalar1=s1, scalar2=None, op0=op0), r, w, cost=c)
        return self.T.op(eng, lambda e: e.tensor_scalar(out=out, in0=in0, scalar1=s1, scalar2=s2, op0=op0, op1=op1), r, w, cost=c)

    def tt(self, eng, out, in0, in1, op, r=(), w=()):
        return self.T.op(eng, lambda e: e.tensor_tensor(out=out, in0=in0, in1=in1, op=op), r, w, cost=self._c(eng, out))

    def stt(self, eng, out, in0, scalar, in1, op0, op1, r=(), w=()):
        return self.T.op(eng, lambda e: e.scalar_tensor_tensor(out=out, in0=in0, scalar=scalar, in1=in1,
                                                               op0=op0, op1=op1), r, w, cost=self._c(eng, out))

    def cp(self, eng, out, in_, r=(), w=()):
        return self.T.op(eng, lambda e: e.tensor_copy(out=out, in_=in_), r, w, cost=self._c(eng, out))

    def recip(self, out, in_, r=(), w=()):
        return self.T.op('dve', lambda e: e.reciprocal(out=out, in_=in_), r, w, cost=self._c('dve', out))

    def memset(self, eng, ap, val, w=()):
        return self.T.op(eng, lambda e: e.memset(ap, val), (), w, cost=self._c(eng, ap))

    def asel(self, out, in_, pattern, cmp, fill, base, cm, r=(), w=()):
        return self.T.op('pool', lambda e: e.affine_select(out=out, in_=in_, pattern=pattern, compare_op=cmp,
                                                           fill=fill, base=base, channel_multiplier=cm), r, w, cost=self._c('pool', out))

    def rsum(self, eng, out, in_, r=(), w=()):
        return self.T.op(eng, lambda e: e.reduce_sum(out=out, in_=in_, axis=AX.X), r, w, cost=self._c(eng, in_))

    def dma(self, out, in_, r=(), w=(), eng='sp'):
        sh = out.shape
        nbytes = 1
        for x in sh:
            nbytes *= int(x)
        nbytes *= 4 if out.dtype == F32 else 2
        return self.T.op(eng, lambda e: e.dma_start(out=out, in_=in_), r, w, dma=True, cost=60.0, lat=2000.0 + nbytes / 60.0)


def bc3(ap2, n):
    return ap2.unsqueeze(2).to_broadcast([ap2.shape[0], ap2.shape[1], n])


def bcm(ap2, a):
    return ap2.unsqueeze(1).to_broadcast([ap2.shape[0], a, ap2.shape[1]])


def ag_chunk_blocks(TW):
    import os
    return max(1, (int(os.environ.get("AGKB", "768")) * 1024) // (128 * TW * 2))


def gath_row(src, blk, NBLK, CH):
    k, w = divmod(blk, CH)
    nbk = min(CH, NBLK - k * CH)
    return 8 * 128 * CH * k + src * (nbk * 128) + w * 128


def cfg(SEQ):
    L = SEQ + NMETA
    NB = (L + 127) // 128
    LP = NB * 128
    Q = SEQ // 4
    QH = Q + 16
    return L, NB, LP, Q, QH


NF = 128 + 128 + 64 * 3
NT = 65 + 64 + 2
UP_PIECES = [(i * 512, 512) for i in range(11)]


def build(SEQ, part=0, debug=False):
    L, NB, LP, Q, QH = cfg(SEQ)
    TW = min(512, Q)
    NT2 = Q // TW
    NTT = NT2 + 1
    assert Q % 128 == 0 and Q % TW == 0
    nc = bass.Bass("TRN2", target_bir_lowering=False)

    P1IN = {"xf", "w1f", "w1t", "pp", "convw", "gnw", "g_mix"}
    P2IN = {"x2", "g_mix", "g_ffn", "g_fin", "wg", "gbias", "wbf", "wbg", "wo", "wup", "fcw", "fcb", "wdn"}

    def din(name, shape):
        if part == 1 and name not in P1IN:
            return None
        if part == 2 and name not in P2IN:
            return None
        return nc.dram_tensor(name, list(shape), F32, kind="ExternalInput").ap()

    xf = din("xf", [2, LP, D])
    x2 = din("x2", [QH, D])
    w1f = din("w1f", [D, NF])
    w1t = din("w1t", [D, NT])
    pp = din("pp", [128, 8])
    convw = din("convw", [64, 12])
    gnw = din("gnw", [64, 64])
    g_mix = din("g_mix", [128, 8])
    g_ffn = din("g_ffn", [128, 8])
    g_fin = din("g_fin", [128, D])
    wg = din("wg", [D, 2 * D])
    gbias = din("gbias", [128, 16])
    wbf = din("wbf", [512, D])
    wbg = din("wbg", [512, D])
    wo = din("wo", [D, D])
    wup = din("wup", [D, 2 * DFF])
    fcw = din("fcw", [128, 44, 3])
    fcb = din("fcb", [128, 44])
    wdn = din("wdn", [DFF, D])
    if part != 1:
        out = nc.dram_tensor("out", [Q, D], F32, kind="ExternalOutput").ap()
        idx = nc.dram_tensor("idx", [128, 8 * NTT], mybir.dt.int32, kind="ExternalInput").ap()

    if part == 1:
        a2a_in = nc.dram_tensor("a2a_in", [8 * NTT * 128, TW], BF16, kind="ExternalOutput").ap()
    elif part == 2:
        gath = nc.dram_tensor("gath", [8 * NTT * 128, TW], BF16, kind="ExternalInput").ap()
    else:
        a2a_in = nc.dram_tensor("a2a_in", [8 * NTT * 128, TW], BF16).ap()
        gath = nc.dram_tensor("gath", [8 * 8 * NTT * 128, TW], BF16).ap()
    wg_b = nc.dram_tensor("wg_b", [D, 2 * D], BF16).ap()
    wbf_b = nc.dram_tensor("wbf_b", [512, D], BF16).ap()
    wbg_b = nc.dram_tensor("wbg_b", [512, D], BF16).ap()
    wo_b = nc.dram_tensor("wo_b", [D, D], BF16).ap()
    wup_b = nc.dram_tensor("wup_b", [D, 2 * DFF], BF16).ap()
    wdn_b = nc.dram_tensor("wdn_b", [DFF, D], BF16).ap()

    b = B(nc)
    T = b.T
    outer = contextlib.ExitStack()
    b.scope = outer
    PS = [nc.alloc_psum_tensor("ps%d" % i, [128, 512], F32) for i in range(8)]
    PSB = [nc.alloc_psum_tensor("psb%d" % i, [128, 1024], BF16) for i in range(0)]

    identb = b.sb([128, 128], BF16)
    trium = b.sb([128, 128], BF16)
    U128 = b.sb([128, 128], F32)
    ones128 = b.sb([128, 128], F32)
    m_upi = b.sb([64, 8, 64], F32)
    m_lows = b.sb([64, 8, 64], F32)
    I8 = b.sb([64, 8, 64], BF16)
    tmpf = b.sb([128, 128], F32)
    ppt = b.sb([128, 8], F32)
    negfb = b.sb([128, 1], F32)
    negA = b.sb([128, 1], F32)
    gmix = b.sb([128, 8], F32)
    gffn = b.sb([128, 8], F32)
    bar_t = b.sb([128, 1], F32)
    CONST_END = b.off

    b.memset('pool', tmpf[:], 1.0, w=['tmpf'])
    b.asel(tmpf[:], tmpf[:], [[-1, 128]], ALU.is_equal, 0.0, 0, 1, r=['tmpf'], w=['tmpf'])
    b.cp('dve', identb[:], tmpf[:], r=['tmpf'], w=['identb'])
    b.memset('pool', ones128[:], 1.0, w=['ones128'])
    b.asel(U128[:], ones128[:], [[1, 128]], ALU.is_ge, 0.0, 0, -1, r=['ones128'], w=['U128'])
    b.cp('dve', trium[:], U128[:], r=['U128'], w=['trium'])
    b.memset('pool', m_upi[:], 1.0, w=['m_upi'])
    b.asel(m_upi[:], m_upi[:], [[0, 8], [1, 64]], ALU.is_ge, 0.0, 0, -1, r=['m_upi'], w=['m_upi'])
    b.memset('pool', m_lows[:], 1.0, w=['m_lows'])
    b.asel(m_lows[:], m_lows[:], [[0, 8], [-1, 64]], ALU.is_gt, 0.0, 0, 1, r=['m_lows'], w=['m_lows'])
    i8f = tmpf[0:64, :].rearrange("p (a c) -> p a c", a=2)
    b.memset('pool', tmpf[:], 1.0, w=['tmpf'])
    b.asel(i8f, i8f, [[0, 2], [-1, 64]], ALU.is_equal, 0.0, 0, 1, r=['tmpf'], w=['tmpf'])
    for a in range(4):
        b.cp('dve', I8[:, 2 * a:2 * a + 2, :], i8f, r=['tmpf'], w=['I8'])
    if part != 2:
        b.dma(ppt[:], pp[:, :], w=['ppt'])
    else:
        b.memset('pool', ppt[:], 0.0, w=['ppt'])
    if part != 1:
        b.dma(gffn[:], g_ffn[:, :], w=['gffn'])
    b.dma(gmix[:], g_mix[:, :], w=['gmix'])
    b.ts('dve', negfb[:], ppt[:, 0:1], -1.0, None, ALU.mult, r=['ppt'], w=['negfb'])
    b.act(negA[:], ppt[:, 1:2], AF.Exp, r=['ppt'], w=['negA'])
    b.ts('dve', negA[:], negA[:], -1.0, None, ALU.mult, r=['negA'], w=['negA'])

    P1_BASE = b.off
    NSTG = 6 if part == 2 else 2
    stg = [b.sb([128, 512], F32) for _ in range(NSTG)] if part != 1 else None
    stb = [b.sb([128, 512], BF16) for _ in range(NSTG)] if part != 1 else None
    cnt = [0]

    pieces = []

    def conv_piece(src, dst, r0, c0, nc_, gain):
        pieces.append((src, dst, r0, c0, nc_, gain))

    def conv_emit(src, dst, r0, c0, nc_, gain):
        T.tag = 'conv'
        s = cnt[0] % NSTG
        cnt[0] += 1
        b.dma(stg[s][:, 0:nc_], src[r0:r0 + 128, c0:c0 + nc_], w=['stg%d' % s])
        ceng = 'dve' if part == 2 else 'pool'
        if gain is None:
            b.cp(ceng, stb[s][:, 0:nc_], stg[s][:, 0:nc_], r=['stg%d' % s], w=['stb%d' % s])
        else:
            b.ts(ceng, stb[s][:, 0:nc_], stg[s][:, 0:nc_], gain, None, ALU.mult,
                 r=['stg%d' % s, 'gmix', 'gffn'], w=['stb%d' % s])
        b.dma(dst[r0:r0 + 128, c0:c0 + nc_], stb[s][:, 0:nc_], r=['stb%d' % s], w=[('W', dst.tensor.name, r0 // 128, c0 // 512)])

    for k in range(8):
        for c0 in range(0, 2048, 512):
            conv_piece(wg, wg_b, k * 128, c0, 512, gmix[:, k:k + 1])
    for k in range(4):
        for c0 in (0, 512):
            conv_piece(wbf, wbf_b, k * 128, c0, 512, None)
            conv_piece(wbg, wbg_b, k * 128, c0, 512, None)
    for k in range(8):
        for c0 in (0, 512):
            conv_piece(wo, wo_b, k * 128, c0, 512, None)
    for c0 in range(0, 5632, 512):
        for k in range(8):
            conv_piece(wup, wup_b, k * 128, c0, 512, gffn[:, k:k + 1])
    for k in range(22):
        for c0 in (0, 512):
            conv_piece(wdn, wdn_b, k * 128, c0, 512, None)

    p1scope = contextlib.ExitStack()
    b.scope = p1scope
    if part != 2:
        b.off = P1_BASE + 2 * 4096 + 2 * 2048
        w1fs = b.sb([128, 8, NF], BF16)
        w1ts = b.sb([128, 8, NT], BF16)
        cw = b.sb([64, 12], F32)
        gnws = b.sb([64, 64], F32)
        KT = b.sb([128, LP], BF16)
        Vst = b.sb([128, 2, NB, 65], BF16)
        cneg = b.sb([128, 2, NB], F32)
        cend = b.sb([128, 2, NB], F32)
        carry = b.sb([128, 2], F32)
        hbuf = [b.sb([128, 4, D], F32) for _ in range(2)]
        hnb = b.sb([128, 4, D], BF16)
        hnT1 = b.sb([128, 8, 512], BF16)
        hnT = [hnT1, hnT1]
        QT = [b.sb([128, 512], BF16) for _ in range(2)]
        ssq = b.sb([128, 8], F32)
        rstd = b.sb([128, 8], F32)
        PT = [b.sb([128, 512], BF16) for _ in range(4)]
        biasq = [b.sb([128, NB], F32) for _ in range(2)]
        spt = b.sb([128, 4], F32)
        pre = b.sb([128, 4], F32)
        osb = b.sb([65, 512], F32)
        rdn = b.sb([64, 512], F32)
        sel65 = b.sb([65, 64], F32)
        fill_rhs = b.sb([128, 512], BF16)
        oTf = [b.sb([64, 512], BF16) for _ in range(2)]
        oTg = [b.sb([64, 512], BF16) for _ in range(2)]
        cin = [[b.sb([64, 3 + 512], F32) for g in range(3)] for bb in range(2)]
        Sst = [b.sb([64, 64], F32) for bb in range(2)]
        Sb = [b.sb([64, 64], BF16) for bb in range(2)]

        def g512(dt):
            return b.sb([64, 8, 64], dt)

        cy = [g512(F32) for _ in range(3)]
        ex = g512(F32)
        sq = g512(F32)
        qh = g512(BF16)
        kh = g512(BF16)
        vT = g512(BF16)
        vb = g512(BF16)
        kbg = g512(BF16)
        ktail = g512(BF16)
        Dm = g512(F32)
        E1 = g512(F32)
        E2 = g512(F32)
        A_ = [g512(BF16) for _ in range(2)]
        B_ = [g512(BF16) for _ in range(2)]
        IA = g512(BF16)
        R_ = [g512(BF16) for _ in range(2)]
        attnT = g512(BF16)
        val = g512(F32)
        kcdT = g512(BF16)
        og = g512(F32)
        ogb = g512(BF16)
        zsb = [g512(F32) for _ in range(2)]
        beta2 = [b.sb([64, 8], F32) for _ in range(2)]
        g2 = [b.sb([64, 8], F32) for _ in range(2)]
        qS = b.sb([64, 64], F32)
        vn = b.sb([64, 64], BF16)
        sc8 = {nm: b.sb([64, 8], F32) for nm in ['beta', 'g', 'gc', 'gtot', 'egc', 'etail', 'cd', 'bege', 'ss', 'rs', 'nb']}

        w1stage = hbuf[1][:].rearrange("p t d -> p (t d)")[:, 0:8 * NF].rearrange("p (k n) -> p k n", k=8)
        b.dma(w1stage, w1f.rearrange("(k p) n -> p k n", p=128), w=['h1'])
        for k in range(8):
            b.ts('dve', w1fs[:, k, :], w1stage[:, k, :], gmix[:, k:k + 1], None, ALU.mult, r=['h1', 'gmix'], w=['w1fs'])
        b.ts('dve', w1fs[:, :, 0:128], w1fs[:, :, 0:128], 0.125, None, ALU.mult, r=['w1fs'], w=['w1fs'])
        b.dma(w1stage[:, :, 0:NT], w1t.rearrange("(k p) n -> p k n", p=128), r=['w1fs'], w=['h1'])
        for k in range(8):
            b.ts('dve', w1ts[:, k, :], w1stage[:, k, 0:NT], gmix[:, k:k + 1], None, ALU.mult, r=['h1', 'gmix'], w=['w1ts'])
        b.dma(cw[:], convw[:, :], w=['cw'])
        b.dma(gnws[:], gnw[:, :], w=['gnws'])
        b.memset('pool', Vst[:], 1.0, w=['Vst_%d_%d' % (bb_, t_) for bb_ in range(2) for t_ in range(0, NB, 4)])
        b.memset('pool', carry[:], 0.0, w=['carry'])
        b.memset('pool', fill_rhs[:], 1.0, w=['fill_rhs'])
        b.memset('pool', sel65[:], 0.0, w=['sel65'])
        b.memset('pool', sel65[64:65, :], 1.0, w=['sel65'])
        for bb in range(2):
            for g in range(3):
                b.memset('pool', cin[bb][g][:, 0:3], 0.0, w=['cin%d%d' % (bb, g)])
            b.memset('pool', Sst[bb][:], 0.0, w=['S%d' % bb])
            b.memset('pool', Sb[bb][:], 0.0, w=['Sb%d' % bb])

        nc._dbg = dict(KT=KT, Vst=Vst, cneg=cneg, cend=cend, hnT=hnT1, QT0=QT[0], QT1=QT[1], w1fs=w1fs, w1ts=w1ts, hnb=hnb, rstd=rstd, oTf0=oTf[0], oTg0=oTg[0], kh=kh, qh=qh, vT=vT, og=og, val=val, A0=A_[0], E1=E1, E2=E2, R1=R_[1], beta=sc8['beta'], g=sc8['g'], gc=sc8['gc'], S0=Sst[0])
        nc._p1_free = nc.sbuf_bytes_remaining
        tiles = []
        for t0 in range(0, NB, 4):
            for bb in range(2):
                tiles.append((bb, t0, min(4, NB - t0)))

        psrot = [0]

        def stage_A(ti):
            T.tag = 'A'
            bb, t0, nblk = tiles[ti]
            s = ti % 2
            ntok = nblk * 128
            p0 = t0 * 128
            hk, hTk, qk = 'h%d' % s, 'hnT', 'QT%d' % s
            b.dma(hbuf[s][:, 0:nblk, :], xf[bb, p0:p0 + ntok, :].rearrange("(t p) d -> p t d", p=128), w=[hk])
            b.memset('pool', ssq[:], 0.0, w=['ssq'])
            for t in range(nblk):
                b.act(hnb[:, t, :], hbuf[s][:, t, :], AF.Square, accum_out=ssq[:, t:t + 1], r=[hk], w=['hnb', 'ssq'])
            b.act(rstd[:, 0:nblk], ssq[:, 0:nblk], AF.Ln, bias=EPS, scale=1.0 / D, r=['ssq'], w=['rstd'])
            b.act(rstd[:, 0:nblk], rstd[:, 0:nblk], AF.Exp, scale=-0.5, r=['rstd'], w=['rstd'])
            for t in range(nblk):
                b.ts('dve', hnb[:, t, :], hbuf[s][:, t, :], rstd[:, t:t + 1], None, ALU.mult, r=[hk, 'rstd'], w=['hnb'])
            pst = PS[0][:].bitcast(BF16).rearrange("p (k c) -> p k c", k=8)
            for t in range(nblk):
                for k in range(8):
                    b.tr(pst[:, k, :], hnb[:, t, k * 128:(k + 1) * 128], identb[:], r=['hnb', 'identb'], w=['ps0'])
                b.cp('dve', hnT[s][:, :, t * 128:(t + 1) * 128], pst, r=['ps0'], w=[hTk])
            groups = [(0, 128), (128, 128), (256, 64), (320, 64), (384, 64)]
            for gi, (c0, m) in enumerate(groups):
                pb = 1
                psrot[0] += 1
                pk = 'ps%d' % pb
                for k in range(8):
                    b.mm(PS[pb][0:m, 0:ntok], w1fs[:, k, c0:c0 + m], hnT[s][:, k, 0:ntok], start=(k == 0), stop=(k == 7),
                         r=['w1fs', hTk], w=[pk])
                lo = bb * 64
                if gi == 0:
                    b.cp('dve', QT[s][lo:lo + 64, 0:ntok], PS[pb][lo:lo + 64, 0:ntok], r=[pk], w=[qk])
                elif gi == 1:
                    b.cp('dve', KT[lo:lo + 64, p0:p0 + ntok], PS[pb][lo:lo + 64, 0:ntok], r=[pk], w=['KT_%d_%d' % (bb, t0)])
                else:
                    g = gi - 2
                    b.cp('dve', cin[bb][g][:, 3:3 + ntok], PS[pb][0:64, 0:ntok], r=[pk], w=['cin%d%d' % (bb, g)])
            psv = PS[0][:, 0:260].rearrange("p (t c) -> p t c", t=4)
            for t in range(nblk):
                for k in range(8):
                    b.mm(psv[:, t, :], hnT[s][:, k, t * 128:(t + 1) * 128], w1ts[:, k, 0:65], start=(k == 0), stop=(k == 7),
                         r=['w1ts', hTk], w=['ps0'])
            b.cp('dve', Vst[:, bb, t0:t0 + nblk, 0:64], psv[:, 0:nblk, 0:64], r=['ps0'], w=['Vst_%d_%d' % (bb, t0)])
            b.act(spt[:, 0:nblk], psv[:, 0:nblk, 64], AF.Exp, bias=negfb[:, 0:1], scale=-1.0, r=['ps0', 'negfb'], w=['spt'])
            b.act(spt[:, 0:nblk], spt[:, 0:nblk], AF.Ln, bias=1.0, r=['spt'], w=['spt'])
            b.mm(PS[0][:, 264:264 + nblk], U128[:], spt[:, 0:nblk], r=['U128', 'spt'], w=['ps0'])
            b.mm(PS[0][:, 272:272 + nblk], ones128[:], spt[:, 0:nblk], r=['ones128', 'spt'], w=['ps0'])
            for t in range(nblk):
                prev = carry[:, bb:bb + 1] if t == 0 else cend[:, bb, t0 + t - 1:t0 + t]
                b.tt('dve', cend[:, bb, t0 + t:t0 + t + 1], PS[0][:, 272 + t:273 + t], prev, ALU.add,
                     r=['ps0', 'carry', 'cend_%d_%d' % (bb, t0), 'cend_%d_%d' % (bb, max(t0 - 4, 0))], w=['cend_%d_%d' % (bb, t0)])
                b.tt('dve', cneg[:, bb, t0 + t:t0 + t + 1], PS[0][:, 264 + t:265 + t], prev, ALU.add,
                     r=['ps0', 'carry', 'cend_%d_%d' % (bb, t0), 'cend_%d_%d' % (bb, max(t0 - 4, 0))], w=['cneg_%d_%d' % (bb, t0)])
            b.cp('dve', carry[:, bb:bb + 1], cend[:, bb, t0 + nblk - 1:t0 + nblk], r=['cend_%d_%d' % (bb, t0)], w=['carry'])
            nch = nblk * 2
            for c4 in range(0, nch, 4):
                n4 = min(4, nch - c4)
                pz = PS[1][0:64, 0:264].rearrange("p (n c) -> p n c", n=4)
                for n in range(n4):
                    cn = c4 + n
                    for k in range(8):
                        b.mm(pz[:, n, :], hnT[s][:, k, cn * 64:(cn + 1) * 64], w1ts[:, k, 65:131], start=(k == 0), stop=(k == 7),
                             r=['w1ts', hTk], w=['ps1'])
                b.cp('dve', zsb[s][:, c4:c4 + n4, :], pz[:, 0:n4, 0:64], r=['ps1'], w=['zs%d' % s])
                b.act(beta2[s][:, c4:c4 + n4], pz[:, 0:n4, 64], AF.Exp, scale=-1.0, r=['ps1'], w=['beta%d' % s])
                b.act(g2[s][:, c4:c4 + n4], pz[:, 0:n4, 65], AF.Exp, bias=ppt[0:64, 2:3], r=['ps1', 'ppt'], w=['g%d' % s])

        def stage_B(ti):
            T.tag = 'B'
            bb, t0, nblk = tiles[ti]
            s = ti % 2
            qk = 'QT%d' % s
            lo = bb * 64
            nkb = t0 + nblk
            NQ = nblk * 128
            halves = [(h0, min(2, nblk - h0)) for h0 in range(0, nblk, 2)]
            for hi, (h0, hn_) in enumerate(halves):
                b.ts('dve', biasq[hi][:, 0:nkb], cneg[:, bb, 0:nkb], cend[:, bb, t0 + h0:t0 + h0 + 1], None, ALU.subtract,
                     r=['cneg_%d_%d' % (bb, t_) for t_ in range(0, nkb, 4)] + ['cend_%d_%d' % (bb, t0)], w=['biasq%d' % hi])
            psoT = PS[6][0:65, 0:NQ]
            for kb in range(nkb):
                qlo = max(0, kb - t0)
                sl = 4 + kb % 2
                ps_s = PS[sl][:, 0:512]
                pts = kb % 4
                c0 = qlo * 128
                b.mm(ps_s[:, c0:NQ], KT[lo:lo + 64, kb * 128:(kb + 1) * 128], QT[s][lo:lo + 64, c0:NQ],
                     r=['KT_%d_%d' % (bb, kb // 4 * 4), qk], w=['ps%d' % sl, 'fillpos'])
                if kb % FILL_EVERY == 0:
                    b.mm(PS[3][:, 0:FILL_N], identb[:], fill_rhs[:, 0:FILL_N], r=['identb', 'fill_rhs', 'fillpos'], w=['ps3'])
                for hi, (h0, hn_) in enumerate(halves):
                    a0 = max(c0, h0 * 128)
                    a1 = (h0 + hn_) * 128
                    if a0 >= a1:
                        continue
                    b.act(PT[pts][:, a0:a1], ps_s[:, a0:a1], AF.Exp, bias=biasq[hi][:, kb:kb + 1],
                          r=['ps%d' % sl, 'biasq%d' % hi], w=['PT%d' % pts])
                if qlo > 0:
                    b.memset('pool', PT[pts][:, 0:c0], 0.0, w=['PT%d' % pts])
                if kb >= t0:
                    j = kb - t0
                    b.tt('pool', PT[pts][:, j * 128:(j + 1) * 128], PT[pts][:, j * 128:(j + 1) * 128], trium[:], ALU.mult,
                         r=['PT%d' % pts, 'trium'], w=['PT%d' % pts])
                b.mm(psoT, Vst[:, bb, kb, :], PT[pts][:, 0:NQ], start=(kb == 0), stop=(kb == nkb - 1),
                     r=['PT%d' % pts, 'Vst_%d_%d' % (bb, kb // 4 * 4)], w=['ps6'])
            b.cp('dve', osb[:, 0:NQ], psoT, r=['ps6'], w=['osb'])
            b.mm(PS[6][0:64, 0:NQ], sel65[:, :], osb[:, 0:NQ], r=['osb', 'sel65'], w=['ps6'])
            b.recip(rdn[:, 0:NQ], PS[6][0:64, 0:NQ], r=['ps6'], w=['rdn'])
            b.tt('dve', oTf[s][:, 0:NQ], osb[0:64, 0:NQ], rdn[:, 0:NQ], ALU.mult, r=['osb', 'rdn'], w=['oTf%d' % s])

        def stage_C(ti):
            T.tag = 'C'
            bb, t0, nblk = tiles[ti]
            s = ti % 2
            ntok = nblk * 128
            nch = nblk * 2
            W = nch * 64

            def v3(t):
                return t[:, 0:nch, :]

            def v2(t):
                return t[:].rearrange("p n c -> p (n c)")[:, 0:W]

            for g in range(3):
                ck = 'cin%d%d' % (bb, g)
                b.ts('dve', v2(cy[g]), cin[bb][g][:, 0:W], cw[:, g * 4:g * 4 + 1], None, ALU.mult, r=[ck, 'cw'], w=['cy%d' % g])
                for j in range(1, 4):
                    b.stt('dve', v2(cy[g]), cin[bb][g][:, j:j + W], cw[:, g * 4 + j:g * 4 + j + 1], v2(cy[g]), ALU.mult, ALU.add,
                          r=[ck, 'cw', 'cy%d' % g], w=['cy%d' % g])
                b.cp('pool', cin[bb][g][:, 0:3], cin[bb][g][:, W:W + 3], r=[ck], w=[ck])
                b.act(v2(ex), v2(cy[g]), AF.Exp, scale=-1.0, r=['cy%d' % g], w=['ex'])
                b.ts('dve', v2(ex), v2(ex), 1.0, None, ALU.add, r=['ex'], w=['ex'])
                b.recip(v2(ex), v2(ex), r=['ex'], w=['ex'])
                if g == 2:
                    b.tt('dve', v2(vT), v2(cy[g]), v2(ex), ALU.mult, r=['ex', 'cy2'], w=['vT'])
                else:
                    b.tt('dve', v2(cy[g]), v2(cy[g]), v2(ex), ALU.mult, r=['ex', 'cy%d' % g], w=['cy%d' % g])
                    b.tt('dve', v2(sq), v2(cy[g]), v2(cy[g]), ALU.mult, r=['cy%d' % g], w=['sq'])
                    b.mm(PS[2][0:64, 0:W], ones128[0:64, 0:64], v2(sq), r=['sq', 'ones128'], w=['ps2'])
                    b.act(v2(ex), PS[2][0:64, 0:W], AF.Ln, bias=EPS, r=['ps2'], w=['ex'])
                    b.act(v2(ex), v2(ex), AF.Exp, scale=-0.5, r=['ex'], w=['ex'])
                    if g == 0:
                        b.stt('dve', v2(qh), v2(cy[g]), 0.125, v2(ex), ALU.mult, ALU.mult, r=['cy0', 'ex'], w=['qh'])
                    else:
                        b.tt('dve', v2(kh), v2(cy[g]), v2(ex), ALU.mult, r=['cy1', 'ex'], w=['kh'])
            be, gg = beta2[s], g2[s]
            b.ts('dve', be[:, 0:nch], be[:, 0:nch], 1.0, None, ALU.add, r=['beta%d' % s], w=['beta%d' % s])
            b.recip(be[:, 0:nch], be[:, 0:nch], r=['beta%d' % s], w=['beta%d' % s])
            b.act(gg[:, 0:nch], gg[:, 0:nch], AF.Ln, bias=1.0, r=['g%d' % s], w=['g%d' % s])
            b.ts('dve', gg[:, 0:nch], gg[:, 0:nch], negA[0:64, 0:1], None, ALU.mult, r=['g%d' % s, 'negA'], w=['g%d' % s])
            b.mm(PS[2][0:64, 0:nch], U128[0:64, 0:64], gg[:, 0:nch], r=['g%d' % s, 'U128'], w=['ps2'])
            b.mm(PS[2][0:64, 8:8 + nch], ones128[0:64, 0:64], gg[:, 0:nch], r=['g%d' % s, 'ones128'], w=['ps2'])
            b.cp('dve', sc8['gc'][:, 0:nch], PS[2][0:64, 0:nch], r=['ps2'], w=['gc'])
            b.cp('dve', sc8['gtot'][:, 0:nch], PS[2][0:64, 8:8 + nch], r=['ps2'], w=['gtot'])
            b.act(sc8['egc'][:, 0:nch], sc8['gc'][:, 0:nch], AF.Exp, r=['gc'], w=['egc'])
            b.act(sc8['cd'][:, 0:nch], sc8['gtot'][:, 0:nch], AF.Exp, r=['gtot'], w=['cd'])
            b.tt('dve', sc8['etail'][:, 0:nch], sc8['gtot'][:, 0:nch], sc8['gc'][:, 0:nch], ALU.subtract, r=['gtot', 'gc'], w=['etail'])
            b.act(sc8['etail'][:, 0:nch], sc8['etail'][:, 0:nch], AF.Exp, r=['etail'], w=['etail'])
            b.tt('dve', sc8['bege'][:, 0:nch], be[:, 0:nch], sc8['egc'][:, 0:nch], ALU.mult, r=['beta%d' % s, 'egc'], w=['bege'])
            b.ts('dve', sc8['nb'][:, 0:nch], be[:, 0:nch], -1.0, None, ALU.mult, r=['beta%d' % s], w=['nb'])
            b.tt('dve', v3(sq), bcm(U128[0:64, 0:64], nch), bc3(gg[:, 0:nch], 64), ALU.mult, r=['U128', 'g%d' % s], w=['sq'])
            b.mm(PS[2][0:64, 0:W], ones128[0:64, 0:64], v2(sq), r=['sq', 'ones128'], w=['ps2'])
            ps7v = PS[2][0:64, :].rearrange("p (n c) -> p n c", n=8)[:, 0:nch, :]
            b.tt('dve', v3(Dm), bc3(sc8['gc'][:, 0:nch], 64), ps7v, ALU.subtract, r=['gc', 'ps2'], w=['Dm'])
            b.ts('dve', v3(E1), v3(Dm), 0.0, None, ALU.min, r=['Dm'], w=['E1'])
            b.ts('dve', v3(E2), v3(Dm), 0.0, -1.0, ALU.max, ALU.mult, r=['Dm'], w=['E2'])
            b.act(v2(E1), v2(E1), AF.Exp, r=['E1'], w=['E1'])
            b.act(v2(E2), v2(E2), AF.Exp, r=['E2'], w=['E2'])
            b.tt('dve', v3(E1), v3(E1), m_lows[:, 0:nch, :], ALU.mult, r=['E1', 'm_lows'], w=['E1'])
            b.tt('dve', v3(E1), v3(E1), bc3(sc8['nb'][:, 0:nch], 64), ALU.mult, r=['E1', 'nb'], w=['E1'])
            b.tt('dve', v3(E2), v3(E2), m_upi[:, 0:nch, :], ALU.mult, r=['E2', 'm_upi'], w=['E2'])
            pT = PS[7][0:64, :].bitcast(BF16).rearrange("p (n c) -> p n c", n=8)
            for n in range(nch):
                b.tr(pT[:, n, 0:64], kh[:, n, :], identb[0:64, 0:64], r=['kh', 'identb', 'zs%d' % s], w=['ps7'])
                b.tr(pT[:, n, 64:128], vT[:, n, :], identb[0:64, 0:64], r=['vT', 'identb'], w=['ps7'])
            b.tt('dve', v3(kbg), pT[:, 0:nch, 0:64], bc3(sc8['bege'][:, 0:nch], 64), ALU.mult, r=['ps7', 'bege'], w=['kbg'])
            b.tt('dve', v3(ktail), pT[:, 0:nch, 0:64], bc3(sc8['etail'][:, 0:nch], 64), ALU.mult, r=['ps7', 'etail'], w=['ktail'])
            b.tt('dve', v3(vb), pT[:, 0:nch, 64:128], bc3(be[:, 0:nch], 64), ALU.mult, r=['ps7', 'beta%d' % s], w=['vb'])
            psG = PS[2][0:64, :].rearrange("p (n c) -> p n c", n=8)
            for n in range(nch):
                b.mm(psG[:, n, :], kh[:, n, :], kh[:, n, :], r=['kh'], w=['ps2'])
            b.tt('dve', v3(A_[0]), psG[:, 0:nch, :], v3(E1), ALU.mult, r=['ps2', 'E1'], w=['A0'])
            for n in range(nch):
                b.mm(psG[:, n, :], kh[:, n, :], qh[:, n, :], r=['kh', 'qh'], w=['ps2'])
            b.tt('dve', v3(attnT), psG[:, 0:nch, :], v3(E2), ALU.mult, r=['ps2', 'E2'], w=['attnT'])
            pB = PS[7][0:64, 0:256].bitcast(BF16).rearrange("p (n c) -> p n c", n=8)
            for n in range(nch):
                b.tr(pB[:, n, :], A_[0][:, n, :], identb[0:64, 0:64], r=['A0', 'identb', 'kbg', 'ktail', 'vb'], w=['ps7'])
            b.cp('dve', v3(B_[0]), pB[:, 0:nch, :], r=['ps7'], w=['B0'])
            b.tt('dve', v3(R_[0]), v3(B_[0]), I8[:, 0:nch, :], ALU.add, r=['B0', 'I8'], w=['R0'])
            for lv in range(1, 6):
                ca, pa = lv % 2, (lv - 1) % 2
                for n in range(nch):
                    b.mm(psG[:, n, :], B_[pa][:, n, :], A_[pa][:, n, :], r=['A%d' % pa, 'B%d' % pa], w=['ps2'])
                b.cp('dve', v3(A_[ca]), psG[:, 0:nch, :], r=['ps2'], w=['A%d' % ca])
                b.tt('dve', v3(IA), psG[:, 0:nch, :], I8[:, 0:nch, :], ALU.add, r=['ps2', 'I8'], w=['IA'])
                if lv < 5:
                    psB2 = PS[7][0:64, :].rearrange("p (n c) -> p n c", n=8)
                    for n in range(nch):
                        b.mm(psB2[:, n, :], A_[pa][:, n, :], B_[pa][:, n, :], r=['A%d' % pa, 'B%d' % pa], w=['ps7'])
                    b.cp('dve', v3(B_[ca]), psB2[:, 0:nch, :], r=['ps7'], w=['B%d' % ca])
                for n in range(nch):
                    b.mm(psG[:, n, :], IA[:, n, :], R_[pa][:, n, :], r=['IA', 'R%d' % pa], w=['ps2'])
                b.cp('dve', v3(R_[ca]), psG[:, 0:nch, :], r=['ps2'], w=['R%d' % ca])
            TT = R_[1]
            for n in range(nch):
                b.mm(psG[:, n, :], TT[:, n, :], vb[:, n, :], r=['R1', 'vb'], w=['ps2'])
            b.cp('dve', v3(val), psG[:, 0:nch, :], r=['ps2'], w=['val'])
            psK = PS[7][0:64, :].rearrange("p (n c) -> p n c", n=8)
            for n in range(nch):
                b.mm(psK[:, n, :], kbg[:, n, :], TT[:, n, :], r=['R1', 'kbg'], w=['ps7'])
            b.cp('dve', v3(kcdT), psK[:, 0:nch, :], r=['ps7'], w=['kcdT'])
            sk, sbk = 'S%d' % bb, 'Sb%d' % bb
            for n in range(nch):
                p1 = PS[7][0:64, 0:128]
                p2 = PS[2][0:64, 0:128]
                b.mm(p1[:, 0:64], kcdT[:, n, :], Sb[bb][:], r=['kcdT', sbk], w=['ps7'])
                b.mm(p1[:, 64:128], qh[:, n, :], Sb[bb][:], r=['qh', sbk], w=['ps7'])
                b.tt('dve', vn[:], val[:, n, :], p1[:, 0:64], ALU.subtract, r=['val', 'ps7'], w=['vn'])
                b.ts('dve', qS[:], p1[:, 64:128], sc8['egc'][:, n:n + 1], None, ALU.mult, r=['ps7', 'egc'], w=['qS'])
                b.mm(p2[:, 0:64], attnT[:, n, :], vn[:], r=['attnT', 'vn'], w=['ps2'])
                b.mm(p2[:, 64:128], ktail[:, n, :], vn[:], r=['ktail', 'vn'], w=['ps2'])
                b.tt('dve', og[:, n, :], qS[:], p2[:, 0:64], ALU.add, r=['qS', 'ps2'], w=['og'])
                b.stt('dve', Sst[bb][:], Sst[bb][:], sc8['cd'][:, n:n + 1], p2[:, 64:128], ALU.mult, ALU.add,
                      r=[sk, 'cd', 'ps2'], w=[sk])
                b.cp('dve', Sb[bb][:], Sst[bb][:], r=[sk], w=[sbk])
            b.tt('dve', v3(Dm), v3(og), v3(og), ALU.mult, r=['og'], w=['Dm'])
            b.rsum('dve', sc8['ss'][:, 0:nch], v3(Dm), r=['Dm'], w=['ss'])
            b.act(sc8['rs'][:, 0:nch], sc8['ss'][:, 0:nch], AF.Ln, bias=EPS, scale=1.0 / 64, r=['ss'], w=['rs'])
            b.act(sc8['rs'][:, 0:nch], sc8['rs'][:, 0:nch], AF.Exp, scale=-0.5, r=['rs'], w=['rs'])
            b.tt('dve', v3(og), v3(og), bc3(sc8['rs'][:, 0:nch], 64), ALU.mult, r=['og', 'rs'], w=['og'])
            b.tt('dve', v3(og), v3(og), bcm(gnws[:], nch), ALU.mult, r=['og', 'gnws'], w=['og'])
            b.act(v2(ex), v2(zsb[s]), AF.Exp, scale=-1.0, r=['zs%d' % s], w=['ex'])
            b.ts('dve', v2(ex), v2(ex), 1.0, None, ALU.add, r=['ex'], w=['ex'])
            b.recip(v2(ex), v2(ex), r=['ex'], w=['ex'])
            b.tt('pool', v2(ex), v2(ex), v2(zsb[s]), ALU.mult, r=['ex', 'zs%d' % s], w=['ex'])
            b.tt('dve', v3(ogb), v3(og), v3(ex), ALU.mult, r=['og', 'ex'], w=['ogb'])
            pO = PS[7][0:64, 0:256].bitcast(BF16)
            for n in range(nch):
                b.tr(pO[:, n * 64:(n + 1) * 64], ogb[:, n, :], identb[0:64, 0:64], r=['ogb', 'identb', 'kcdT'], w=['ps7'])
            b.cp('dve', oTg[s][:, 0:W], pO[:, 0:W], r=['ps7'], w=['oTg%d' % s])

        def stage_store(ti):
            T.tag = 'store'
            bb, t0, nblk = tiles[ti]
            s = ti % 2
            for t in range(nblk):
                p = (t0 + t) * 128
                for src, rows, key in ((oTf[s], 0, 'oTf%d' % s), (oTg[s], 64, 'oTg%d' % s)):
                    blk = src[:, t * 128:(t + 1) * 128]
                    if p < 4 * Q:
                        j = p // Q
                        it, col = divmod(p - j * Q, TW)
                        d = ((bb * 4 + j) * NTT + it) * 128 + rows
                        b.dma(a2a_in[d:d + 64, col:col + 128], blk, r=[key], w=['a2a_in'])
                        if j >= 1 and p % Q == 0:
                            d = ((bb * 4 + j - 1) * NTT + NT2) * 128 + rows
                            b.dma(a2a_in[d:d + 64, 0:16], blk[:, 0:16], r=[key], w=['a2a_in'])
                    else:
                        d = ((bb * 4 + 3) * NTT + NT2) * 128 + rows
                        b.dma(a2a_in[d:d + 64, 0:16], blk[:, 0:16], r=[key], w=['a2a_in'])

        ztile = b.sb([128, TW], BF16)
        b.memset('pool', ztile[:], 0.0, w=['ztile'])
        for d_ in range(8):
            r_ = (d_ * NTT + NT2) * 128
            b.dma(a2a_in[r_:r_ + 128, :], ztile[:], r=['ztile'], w=['a2a_in'])
        per_tile = (len(pieces) + len(tiles) - 1) // len(tiles)
        for ti in range(len(tiles)):
            stage_A(ti)
            for pc in (pieces[ti * per_tile:(ti + 1) * per_tile] if part == 0 else []):
                conv_emit(*pc)
            stage_B(ti)
            stage_C(ti)
            stage_store(ti)

    if part == 0:
        NBLK = 8 * NTT
        CH = ag_chunk_blocks(TW)
        for k in range((NBLK + CH - 1) // CH):
            nbk = min(CH, NBLK - k * CH)
            src_ap = a2a_in[k * CH * 128:(k * CH + nbk) * 128, :]
            dst_ap = gath[8 * 128 * CH * k:8 * 128 * CH * k + 8 * nbk * 128, :]
            T.op('pool', lambda e, src_ap=src_ap, dst_ap=dst_ap: e.collective_compute(
                "AllGather", ALU.bypass, replica_groups=[list(range(8))], ins=[src_ap], outs=[dst_ap]),
                r=['a2a_in'], w=['gath'], cc=True, cost=300.0, lat=60000.0)
    T.barrier(lambda e: e.memset(bar_t[:], 0.0))
    p1scope.close()
    p2scope = contextlib.ExitStack()
    b.scope = p2scope
    if part == 2:
        for pc in pieces:
            conv_emit(*pc)
    if part != 1:
        b.off = P1_BASE
        h2 = b.sb([128, 4, D], F32)
        hb2 = b.sb([128, 4, D], BF16)
        hT2 = b.sb([128, 8, 512], BF16)
        oT2 = b.sb([128, 8, 512], BF16)
        gT = b.sb([128, 16, 512], BF16)
        mT = b.sb([128, 8, 512], BF16)
        aT = b.sb([128, 22, 512], BF16)
        uprev = b.sb([128, 44, 2], F32)
        ugL = [b.sb([128, 514], F32) for _ in range(2)]
        uuL = [b.sb([128, 514], F32) for _ in range(2)]
        cgL = [b.sb([128, 512], F32) for _ in range(2)]
        cuL = [b.sb([128, 512], F32) for _ in range(2)]
        yTL = [b.sb([128, 512], F32) for _ in range(2)]
        ug, uu, cg, cu, yT = ugL[0], uuL[0], cgL[0], cuL[0], yTL[0]
        gfin = b.sb([128, D], F32)
        gbs = b.sb([128, 16], F32)
        fcws = b.sb([128, 44, 3], F32)
        fcbs = b.sb([128, 44], F32)
        ss2 = b.sb([128, 8], F32)
        rs2 = b.sb([128, 8], F32)
        NWR = 6
        WR = [b.sb([128, 5632], BF16) for _ in range(NWR)]
        wcnt = [0]

        idxs = b.sb([128, 8 * NTT], mybir.dt.int32)
        b.dma(idxs[:], idx[:, :], w=['idxs'])
        b.dma(gfin[:], g_fin[:, :], w=['gfin'])
        b.dma(gbs[:], gbias[:, :], w=['gbs'])
        b.dma(fcws[:], fcw[:, :, :], w=['fcws'])
        b.dma(fcbs[:], fcb[:, :], w=['fcbs'])
        b.memset('pool', uprev[:], 0.0, w=['uprev'])
        b.memset('pool', h2[:], 0.0, w=['h2'])

        def wload(src, kch, c0, ncols):
            s_ = wcnt[0] % NWR
            wcnt[0] += 1
            v = WR[s_][:, 0:kch * ncols].rearrange("p (k n) -> p k n", k=kch)
            rk = [('W', src.tensor.name, k_, cb_) for k_ in range(kch) for cb_ in range(c0 // 512, (c0 + ncols + 511) // 512)]
            b.dma(v, src[:, c0:c0 + ncols].rearrange("(k p) n -> p k n", p=128), r=rk, w=['WR%d' % s_])
            return v, 'WR%d' % s_

        def norm_T(nblk, ntok, rows_ap_fn, key_h):
            T.tag = 'norm'
            b.memset('pool', ss2[:], 0.0, w=['ss2'])
            for t in range(nblk):
                b.act(hb2[:, t, :], h2[:, t, :], AF.Square, accum_out=ss2[:, t:t + 1], r=[key_h], w=['hb2', 'ss2'])
            b.act(rs2[:, 0:nblk], ss2[:, 0:nblk], AF.Ln, bias=EPS, scale=1.0 / D, r=['ss2'], w=['rs2'])
            b.act(rs2[:, 0:nblk], rs2[:, 0:nblk], AF.Exp, scale=-0.5, r=['rs2'], w=['rs2'])
            pst = PS[0][:].bitcast(BF16).rearrange("p (k c) -> p k c", k=8)
            for t in range(nblk):
                b.ts('dve', hb2[:, t, :], h2[:, t, :], rs2[:, t:t + 1], None, ALU.mult, r=[key_h, 'rs2'], w=['hb2'])
                for k in range(8):
                    b.tr(pst[:, k, :], hb2[:, t, k * 128:(k + 1) * 128], identb[:], r=['hb2', 'identb'], w=['ps0'])
                b.cp('dve', hT2[:, :, t * 128:(t + 1) * 128], pst, r=['ps0'], w=['hT2'])

        def ffn_up(ntok, first_halo):
            T.tag = 'ffn'
            for pi in range(11):
                wv_g, kg = wload(wup_b, 8, pi * 256, 256)
                wv_u, ku = wload(wup_b, 8, DFF + pi * 256, 256)
                for cc in range(2):
                    j = pi * 2 + cc
                    q_ = j % 2
                    ug, uu, cg, cu, yT = ugL[q_], uuL[q_], cgL[q_], cuL[q_], yTL[q_]
                    for half, (wv, kk, dst, pb) in enumerate(((wv_g, kg, ug, 1), (wv_u, ku, uu, 2))):
                        jj = j + 22 * half
                        for k in range(8):
                            b.mm(PS[pb][:, 0:ntok], wv[:, k, cc * 128:(cc + 1) * 128], hT2[:, k, 0:ntok], start=(k == 0), stop=(k == 7),
                                 r=[kk, 'hT2'], w=['ps%d' % pb])
                        dk = ('ug%d' if half == 0 else 'uu%d') % q_
                        b.cp('pool', dst[:, 0:2], uprev[:, jj, :], r=['uprev'], w=[dk])
                        b.cp('dve', dst[:, 2:2 + ntok], PS[pb][:, 0:ntok], r=['ps%d' % pb], w=[dk])
                        b.cp('pool', uprev[:, jj, :], dst[:, ntok:ntok + 2], r=[dk], w=['uprev'])
                        if first_halo:
                            continue
                        co = cg if half == 0 else cu
                        ck = ('cg%d' if half == 0 else 'cu%d') % q_
                        b.ts('dve', co[:, 0:ntok], dst[:, 0:ntok], fcws[:, jj, 0:1], fcbs[:, jj:jj + 1], ALU.mult, ALU.add,
                             r=[dk, 'fcws', 'fcbs'], w=[ck])
                        b.stt('dve', co[:, 0:ntok], dst[:, 1:1 + ntok], fcws[:, jj, 1:2], co[:, 0:ntok], ALU.mult, ALU.add,
                              r=[dk, 'fcws', ck], w=[ck])
                        b.stt('dve', co[:, 0:ntok], dst[:, 2:2 + ntok], fcws[:, jj, 2:3], co[:, 0:ntok], ALU.mult, ALU.add,
                              r=[dk, 'fcws', ck], w=[ck])
                    if first_halo:
                        continue
                    b.act(yT[:, 0:ntok], cg[:, 0:ntok], AF.Sigmoid, r=['cg%d' % q_], w=['yT%d' % q_])
                    b.tt('dve', cg[:, 0:ntok], cg[:, 0:ntok], cu[:, 0:ntok], ALU.mult, r=['cg%d' % q_, 'cu%d' % q_], w=['cg%d' % q_])
                    b.tt('dve', aT[:, j, 0:ntok], cg[:, 0:ntok], yT[:, 0:ntok], ALU.mult, r=['cg%d' % q_, 'yT%d' % q_], w=['aT'])

        def mixer_tail(r0, ntok, nblk):
            T.tag = 'mix'
            np_ = ntok if ntok < 128 else 128
            b.dma(h2[0:np_, 0:nblk, :], x2[r0:r0 + ntok, :].rearrange("(t p) d -> p t d", p=np_), w=['h2'])
            norm_T(nblk, ntok, None, 'h2')
            tix = r0 // TW
            for c in range(8):
                col = tix * 8 + c
                T.op('pool', lambda e, c=c, col=col: e.indirect_dma_start(
                    out=oT2[:, c, 0:TW], out_offset=None, in_=gath[:, :],
                    in_offset=bass.IndirectOffsetOnAxis(ap=idxs[:, col:col + 1], axis=0)),
                    r=['gath', 'idxs'], w=['oT2'], dma=True, cost=1500.0, lat=3000.0)
            for pi in range(4):
                wv, kk = wload(wg_b, 8, pi * 512, 512)
                for cc in range(4):
                    j = pi * 4 + cc
                    pb = 1 + j % 2
                    for k in range(8):
                        b.mm(PS[pb][:, 0:ntok], wv[:, k, cc * 128:(cc + 1) * 128], hT2[:, k, 0:ntok], start=(k == 0), stop=(k == 7),
                             r=[kk, 'hT2'], w=['ps%d' % pb])
                    b.act(gT[:, j, 0:ntok], PS[pb][:, 0:ntok], AF.Sigmoid, bias=gbs[:, j:j + 1], scale=1.0,
                          r=['ps%d' % pb, 'gbs'], w=['gT'])
            wf_, kf = wload(wbf_b, 4, 0, 1024)
            wg_, kg_ = wload(wbg_b, 4, 0, 1024)
            for j in range(8):
                for k in range(4):
                    b.mm(PS[1][:, 0:ntok], wf_[:, k, j * 128:(j + 1) * 128], oT2[:, k, 0:ntok], start=(k == 0), stop=(k == 3),
                         r=[kf, 'oT2'], w=['ps1'])
                for k in range(4):
                    b.mm(PS[2][:, 0:ntok], wg_[:, k, j * 128:(j + 1) * 128], oT2[:, 4 + k, 0:ntok], start=(k == 0), stop=(k == 3),
                         r=[kg_, 'oT2'], w=['ps2'])
                b.tt('dve', yT[:, 0:ntok], PS[1][:, 0:ntok], gT[:, j, 0:ntok], ALU.mult, r=['ps1', 'gT'], w=['yT0'])
                b.tt('dve', cg[:, 0:ntok], PS[2][:, 0:ntok], gT[:, 8 + j, 0:ntok], ALU.mult, r=['ps2', 'gT'], w=['cg0'])
                b.tt('dve', mT[:, j, 0:ntok], yT[:, 0:ntok], cg[:, 0:ntok], ALU.add, r=['yT0', 'cg0'], w=['mT'])
            for hc in range(2):
                wv, kk = wload(wo_b, 8, hc * 512, 512)
                for t in range(nblk):
                    nt_ = min(128, ntok - t * 128)
                    pb = 1 + t % 2
                    for k in range(8):
                        b.mm(PS[pb][0:nt_, :], mT[:, k, t * 128:t * 128 + nt_], wv[:, k, :], start=(k == 0), stop=(k == 7),
                             r=[kk, 'mT'], w=['ps%d' % pb])
                    b.tt('dve', h2[0:nt_, t, hc * 512:(hc + 1) * 512], h2[0:nt_, t, hc * 512:(hc + 1) * 512], PS[pb][0:nt_, :], ALU.add,
                         r=['ps%d' % pb, 'h2'], w=['h2'])


        nc._p2_free = nc.sbuf_bytes_remaining
        p2tiles = [(it * TW, TW) for it in range(NT2)] + [(Q, 16)]
        for r0, ntok in p2tiles:
            nblk = (ntok + 127) // 128
            mixer_tail(r0, ntok, nblk)
            norm_T(nblk, ntok, None, 'h2')
            ffn_up(ntok, False)
            T.tag = 'down'
            for hc in range(4):
                wv, kk = wload(wdn_b, 22, hc * 256, 256)
                for t in range(nblk):
                    nt_ = min(128, ntok - t * 128)
                    pb = 1 + t % 2
                    for k in range(22):
                        b.mm(PS[pb][0:nt_, 0:256], aT[:, k, t * 128:t * 128 + nt_], wv[:, k, :], start=(k == 0), stop=(k == 21),
                             r=[kk, 'aT'], w=['ps%d' % pb])
                    b.tt('dve', h2[0:nt_, t, hc * 256:(hc + 1) * 256], h2[0:nt_, t, hc * 256:(hc + 1) * 256], PS[pb][0:nt_, 0:256], ALU.add,
                         r=['ps%d' % pb, 'h2'], w=['h2'])
            b.memset('pool', ss2[:], 0.0, w=['ss2'])
            for t in range(nblk):
                b.act(hb2[:, t, :], h2[:, t, :], AF.Square, accum_out=ss2[:, t:t + 1], r=['h2'], w=['hb2', 'ss2'])
            b.act(rs2[:, 0:nblk], ss2[:, 0:nblk], AF.Ln, bias=EPS, scale=1.0 / D, r=['ss2'], w=['rs2'])
            b.act(rs2[:, 0:nblk], rs2[:, 0:nblk], AF.Exp, scale=-0.5, r=['rs2'], w=['rs2'])
            for t in range(nblk):
                b.stt('dve', h2[:, t, :], h2[:, t, :], rs2[:, t:t + 1], gfin[:], ALU.mult, ALU.mult, r=['h2', 'rs2', 'gfin'], w=['h2'])
            if r0 == 0:
                b.dma(out[0:112, :], h2[16:128, 0, :], r=['h2'], w=['out'])
                if nblk > 1:
                    b.dma(out[112:ntok - 16, :].rearrange("(t p) d -> p t d", p=128), h2[:, 1:nblk, :], r=['h2'], w=['out'])
            elif ntok == TW:
                b.dma(out[r0 - 16:r0 - 16 + TW, :].rearrange("(t p) d -> p t d", p=128), h2[:, 0:nblk, :], r=['h2'], w=['out'])
            else:
                b.dma(out[Q - 16:Q, :], h2[0:16, 0, :], r=['h2'], w=['out'])
        p2scope.close()

    fkeys = ['out'] if part != 1 else ['a2a_in']
    T.finalize(final_wait_keys=fkeys)
    with contextlib.ExitStack() as st:
        esem = {e: st.enter_context(nc.semaphore("s_" + e)) for e in ENGS}
        dsems = {rg: [st.enter_context(nc.semaphore("d%s%d" % (rg, i))) for i in range(NDMA_SEMS)] for rg in ('s', 'p')}
        ccsem = [st.enter_context(nc.semaphore("ccsem%d" % i)) for i in range(max(1, T.ncc))]
        st.enter_context(nc.allow_low_precision("bf16 matmul operands by design, fp32 accumulation"))
        block = st.enter_context(nc.Block())
        T.emit(esem, dsems, ccsem, block)
    outer.close()
    nc._stats = (T.makespan, T.busy, T.nops)
    return nc


def make_in_maps(inp, SEQ, fused=True):
    L, NB, LP, Q, QH = cfg(SEQ)
    f = np.float32
    x = np.asarray(inp["x"], f)
    meta = np.asarray(inp["meta_tokens"], f)
    xf = np.zeros((2, LP, D), f)
    xf[:, 0:NMETA] = meta[None]
    xf[:, NMETA:L] = x
    w_in = np.asarray(inp["w_in"], f)[0]
    conv_w = np.asarray(inp["gdn_conv_w"], f)[0]
    gm = np.asarray(inp["norm_mix_w"], f)[0]
    gf = np.asarray(inp["norm_ffn_w"], f)[0]
    shared = dict(
        g_mix=np.ascontiguousarray(gm.reshape(8, 128).T),
        g_ffn=np.ascontiguousarray(gf.reshape(8, 128).T),
        g_fin=np.ascontiguousarray(np.broadcast_to(np.asarray(inp["norm_final_w"], f)[None, :], (128, D))),
        wg=np.ascontiguousarray(w_in[:, 3608:3608 + 2048]),
        gbias=np.ascontiguousarray(np.asarray(inp["gate_bias"], f)[0].reshape(16, 128).T),
        wbf=np.ascontiguousarray(np.asarray(inp["w_branch_fox"], f)[0]),
        wbg=np.ascontiguousarray(np.asarray(inp["w_branch_gdn"], f)[0]),
        wo=np.ascontiguousarray(np.asarray(inp["w_out"], f)[0]),
        wup=np.ascontiguousarray(np.asarray(inp["ffn_w_up"], f)[0]),
        fcw=np.ascontiguousarray(np.asarray(inp["ffn_conv_w"], f)[0].T.reshape(44, 128, 3).transpose(1, 0, 2)),
        fcb=np.ascontiguousarray(np.asarray(inp["ffn_conv_b"], f)[0].reshape(44, 128).T),
        wdn=np.ascontiguousarray(np.asarray(inp["ffn_w_down"], f)[0]),
        gnw=np.ascontiguousarray(np.broadcast_to(np.asarray(inp["gdn_norm_w"], f)[0][None, :], (64, 64))),
        xf=xf,
    )
    maps = []
    for c in range(8):
        h = c
        fq = w_in[:, h * 64:(h + 1) * 64]
        fk = w_in[:, 512 + h * 64:512 + (h + 1) * 64]
        fv = w_in[:, 1024 + h * 64:1024 + (h + 1) * 64]
        ff = w_in[:, 1536 + h:1537 + h]
        g0 = 1544
        gq = w_in[:, g0 + h * 64:g0 + (h + 1) * 64]
        gk = w_in[:, g0 + 512 + h * 64:g0 + 512 + (h + 1) * 64]
        gv = w_in[:, g0 + 1024 + h * 64:g0 + 1024 + (h + 1) * 64]
        z = w_in[:, 3080 + h * 64:3080 + (h + 1) * 64]
        bcol = w_in[:, 3592 + h:3593 + h]
        acol = w_in[:, 3600 + h:3601 + h]
        w1f = np.ascontiguousarray(np.concatenate([fq, fq, fk, fk, gq, gk, gv], axis=1))
        w1t = np.ascontiguousarray(np.concatenate([fv, ff, z, bcol, acol], axis=1))
        pp = np.zeros((128, 8), f)
        pp[:, 0] = np.asarray(inp["fgt_bias"], f)[0, h]
        pp[:, 1] = np.asarray(inp["gdn_a_log"], f)[0, h]
        pp[:, 2] = np.asarray(inp["gdn_dt_bias"], f)[0, h]
        cwm = np.zeros((64, 12), f)
        for g in range(3):
            cwm[:, g * 4:(g + 1) * 4] = conv_w[:, g * 512 + h * 64:g * 512 + (h + 1) * 64].T
        bb, j = divmod(c, 4)
        x2 = np.ascontiguousarray(xf[bb, j * Q:j * Q + QH])
        TW = min(512, Q)
        NTT = Q // TW + 1
        idx = np.zeros((128, 8 * NTT), np.int32)
        pa = np.arange(128)
        for it in range(NTT):
            for ch in range(8):
                half, cc = divmod(ch, 4)
                src = 2 * cc + pa // 64
                blk = c * NTT + it
                if fused:
                    base = np.array([gath_row(int(s_), blk, 8 * NTT, ag_chunk_blocks(TW)) for s_ in src])
                else:
                    base = (src * NTT + it) * 128
                idx[:, it * 8 + ch] = base + half * 64 + pa % 64
        m = dict(shared)
        m.update(w1f=w1f, w1t=w1t, pp=pp, convw=cwm, x2=x2, idx=idx)
        maps.append(m)
    return maps


_CACHE = {}


FUSED = False
P1KEYS = ["xf", "w1f", "w1t", "pp", "convw", "gnw", "g_mix"]
P2KEYS = ["x2", "g_mix", "g_ffn", "g_fin", "wg", "gbias", "wbf", "wbg", "wo", "wup", "fcw", "fcb", "wdn", "idx"]


def run(inp, SEQ, fused=None):
    fused = FUSED if fused is None else fused
    L, NB, LP, Q, QH = cfg(SEQ)
    TW = min(512, Q)
    NTT = Q // TW + 1
    maps = make_in_maps(inp, SEQ, fused)
    if fused:
        if (SEQ, 0) not in _CACHE:
            _CACHE[(SEQ, 0)] = build(SEQ, 0)
        res = run_bass_kernel_spmd(_CACHE[(SEQ, 0)], maps, core_ids=list(range(8)))
    else:
        for part in (1, 2):
            if (SEQ, part) not in _CACHE:
                _CACHE[(SEQ, part)] = build(SEQ, part)
        res1 = run_bass_kernel_spmd(_CACHE[(SEQ, 1)], [{k: m[k] for k in P1KEYS} for m in maps], core_ids=list(range(8)))
        sh = [np.asarray(res1.results[c]["a2a_in"]) for c in range(8)]
        maps2 = []
        for c in range(8):
            m = {k: maps[c][k] for k in P2KEYS}
            m["gath"] = np.ascontiguousarray(np.concatenate(
                [sh[s_][c * NTT * 128:(c + 1) * NTT * 128] for s_ in range(8)], axis=0))
            maps2.append(m)
        res = run_bass_kernel_spmd(_CACHE[(SEQ, 2)], maps2, core_ids=list(range(8)))
    out = np.zeros((2, SEQ, D), np.float32)
    for c in range(8):
        bb, j = divmod(c, 4)
        out[bb, j * Q:(j + 1) * Q] = res.results[c]["out"]
    return out, res


def kernel(**inputs):
    SEQ = inputs["x"].shape[1]
    out, _ = run(inputs, SEQ)
    return out
```
